# Optimizing a Trainium2 kernel written in Bass

```python
import math
import jax, jax.numpy as jnp
from jax import lax
import numpy as np

D_MODEL = 1024
BATCH = 16
SEQ = 4096
DEPTH = 4

GRID_W = 64
CTX_LEN = 256
N_MIXERS = 3
CHUNK = 128
EPS = 1e-6
CM_WIDTH = 2 * D_MODEL
CM_GROUPS = 8
LRU_WIDTH = D_MODEL
LRU_HEADS = 4
LRU_BLOCK = LRU_WIDTH // LRU_HEADS
LRU_CONV = 4
LRU_PAD = (2, 1)
LRU_C = 8.0
MLSTM_HEADS = 4
MLSTM_QK = D_MODEL // 2
MLSTM_V = D_MODEL
MLSTM_DK = MLSTM_QK // MLSTM_HEADS
MLSTM_DV = MLSTM_V // MLSTM_HEADS
FFN_HIDDEN = (8 * D_MODEL // 3) // 128 * 128
FFN_CONV = 3
N_CM = (DEPTH + 2) // 3
N_LRU = (DEPTH + 1) // 3
N_ML = DEPTH // 3

kernel_name = "hybrid_chunkmlp_rglru_mlstm_prefix_dit"


def rms_norm(x, g):
    xf = x.astype(jnp.float32)
    y = xf * lax.rsqrt(jnp.mean(xf * xf, axis=-1, keepdims=True) + EPS)
    return (y * g.astype(jnp.float32)).astype(x.dtype)


def layer_norm(x, g, b):
    xf = x.astype(jnp.float32)
    mu = jnp.mean(xf, axis=-1, keepdims=True)
    var = jnp.mean(jnp.square(xf - mu), axis=-1, keepdims=True)
    y = (xf - mu) * lax.rsqrt(var + EPS)
    return (y * g.astype(jnp.float32) + b.astype(jnp.float32)).astype(x.dtype)


def modulate(h, shift, scale):
    return h * (1.0 + scale) + shift


def dwconv1d(x, w, pad):
    return lax.conv_general_dilated(x, w[:, None, :].astype(x.dtype), (1,), [pad],
                                    dimension_numbers=('NWC', 'WIO', 'NWC'),
                                    feature_group_count=x.shape[-1])


def dwconv2d(x, w):
    return lax.conv_general_dilated(x, w[:, :, None, :].astype(x.dtype), (1, 1), [(1, 1), (1, 1)],
                                    dimension_numbers=('NHWC', 'HWIO', 'NHWC'),
                                    feature_group_count=x.shape[-1])


def chunk_mlp(h, w_in, b_in, v_g, v_b, w_s, b_s, w_out):
    bsz, t, _ = h.shape
    z = jax.nn.gelu(h @ w_in + b_in)
    u, v = jnp.split(z, 2, axis=-1)
    v = layer_norm(v, v_g, v_b)
    v = v.reshape(bsz, t // CHUNK, CHUNK, CM_GROUPS, CM_WIDTH // CM_GROUPS)
    s = jnp.einsum('gpq,bnqgc->bnpgc', w_s, v) + jnp.transpose(b_s)[:, :, None]
    return (u * s.reshape(bsz, t, CM_WIDTH)) @ w_out


def _linear_combine(e1, e2):
    a1, b1 = e1
    a2, b2 = e2
    return a1 * a2, a2 * b1 + b2


def rglru_scan(xr, w_rg, b_rg, w_ig, b_ig, lam, h0):
    bsz, t, w = xr.shape
    xh = xr.reshape(bsz, t, LRU_HEADS, LRU_BLOCK)
    r = jax.nn.sigmoid(jnp.einsum('bthi,hij->bthj', xh, w_rg).reshape(bsz, t, w) + b_rg)
    i = jax.nn.sigmoid(jnp.einsum('bthi,hij->bthj', xh, w_ig).reshape(bsz, t, w) + b_ig)
    log_a = -LRU_C * r * jax.nn.softplus(-lam)
    a = jnp.exp(log_a)
    b = jnp.sqrt(-jnp.expm1(2.0 * log_a)) * (i * xr)
    b = b.at[:, 0].add(a[:, 0] * h0)
    _, hs = lax.associative_scan(_linear_combine, (a, b), axis=1)
    return hs, hs[:, -1]


def lru_mixer(hc, hl, w_in, conv_w, conv_b, w_rg, b_rg, w_ig, b_ig, lam, w_out, ctx_out):
    def branches(h):
        y, xr = jnp.split(h @ w_in, 2, axis=-1)
        xr = dwconv1d(xr, conv_w, LRU_PAD) + conv_b
        return jax.nn.gelu(y), xr.astype(jnp.float32)

    yc, xc = branches(hc)
    yl, xl = branches(hl)
    zero = jnp.zeros((hl.shape[0], LRU_WIDTH), jnp.float32)

    def direction(d, xc_d, xl_d):
        p = (w_rg[d], b_rg[d], w_ig[d], b_ig[d], lam[d])
        hc_d, s_ctx = rglru_scan(xc_d, *p, zero)
        hl_d, _ = rglru_scan(xl_d, *p, s_ctx)
        return hc_d, hl_d

    hcf, hlf = direction(0, xc, xl)
    hcb, hlb = direction(1, jnp.flip(xc, 1), jnp.flip(xl, 1))
    ol = ((hlf + jnp.flip(hlb, 1)).astype(hl.dtype) * yl) @ w_out
    oc = ((hcf + jnp.flip(hcb, 1)).astype(hc.dtype) * yc) @ w_out if ctx_out else None
    return oc, ol


def mlstm_chunked(q, k, v, ig, lf, state):
    bsz, nh, t, _ = q.shape
    dv = v.shape[-1]
    nc = t // CHUNK

    def to_chunks(a):
        return jnp.moveaxis(a.reshape(bsz, nh, nc, CHUNK, *a.shape[3:]), 2, 0)

    causal = jnp.tril(jnp.ones((CHUNK, CHUNK), bool))

    def step(carry, xs):
        C, n, m = carry
        qc, kc, vc, ic, fc = xs
        b = jnp.cumsum(fc, axis=-1)
        dlog = jnp.where(causal, b[..., :, None] - b[..., None, :] + ic[..., None, :], -jnp.inf)
        inter = b + m[..., None]
        m_t = jnp.maximum(inter, jnp.max(dlog, axis=-1))
        dw = jnp.exp(dlog - m_t[..., None])
        iw = jnp.exp(inter - m_t)
        s = jnp.einsum('bhtd,bhsd->bhts', qc, kc) * dw
        num = iw[..., None] * jnp.einsum('bhtd,bhde->bhte', qc, C) + jnp.einsum('bhts,bhse->bhte', s, vc)
        den = iw * jnp.einsum('bhtd,bhd->bht', qc, n) + jnp.sum(s, axis=-1)
        h = num / jnp.maximum(jnp.abs(den), jnp.exp(-m_t))[..., None]
        b_end = b[..., -1]
        wlog = b_end[..., None] - b + ic
        m_new = jnp.maximum(b_end + m, jnp.max(wlog, axis=-1))
        decay = jnp.exp(b_end + m - m_new)
        w = jnp.exp(wlog - m_new[..., None])
        C = decay[..., None, None] * C + jnp.einsum('bhs,bhsd,bhse->bhde', w, kc, vc)
        n = decay[..., None] * n + jnp.einsum('bhs,bhsd->bhd', w, kc)
        return (C, n, m_new), h

    state, hs = lax.scan(step, state, (to_chunks(q), to_chunks(k), to_chunks(v), to_chunks(ig), to_chunks(lf)))
    return jnp.moveaxis(hs, 0, 2).reshape(bsz, nh, t, dv), state


def mlstm_mixer(hc, hl, w_in, b_gate, norm_g, w_out, ctx_out):
    splits = [MLSTM_QK, 2 * MLSTM_QK, 2 * MLSTM_QK + MLSTM_V, 2 * MLSTM_QK + 2 * MLSTM_V]

    def project(h):
        bsz, t, _ = h.shape
        q, k, v, o, g = jnp.split(h @ w_in, splits, axis=-1)

        def heads(a, d):
            return jnp.transpose(a.reshape(bsz, t, MLSTM_HEADS, d), (0, 2, 1, 3)).astype(jnp.float32)

        g = g.reshape(bsz, t, 2, 2, MLSTM_HEADS).astype(jnp.float32) + b_gate.astype(jnp.float32)
        g = jnp.transpose(g, (2, 3, 0, 4, 1))
        return heads(q, MLSTM_DK) * MLSTM_DK ** -0.5, heads(k, MLSTM_DK), heads(v, MLSTM_DV), o, g

    qc, kc, vc, oc_g, gc = project(hc)
    ql, kl, vl, ol_g, gl = project(hl)
    bsz = hl.shape[0]
    zero = (jnp.zeros((bsz, MLSTM_HEADS, MLSTM_DK, MLSTM_DV), jnp.float32),
            jnp.zeros((bsz, MLSTM_HEADS, MLSTM_DK), jnp.float32),
            jnp.zeros((bsz, MLSTM_HEADS), jnp.float32))

    def run(d, rev):
        f = (lambda a: jnp.flip(a, axis=2)) if rev else (lambda a: a)
        hc_d, st = mlstm_chunked(f(qc), f(kc), f(vc), f(gc[d, 0]), f(jax.nn.log_sigmoid(gc[d, 1])), zero)
        hl_d, _ = mlstm_chunked(f(ql), f(kl), f(vl), f(gl[d, 0]), f(jax.nn.log_sigmoid(gl[d, 1])), st)
        return f(hc_d), f(hl_d)

    hcf, hlf = run(0, False)
    hcb, hlb = run(1, True)

    def readout(hf, hb, o, dtype):
        h = hf + hb
        h = h * lax.rsqrt(jnp.mean(h * h, axis=-1, keepdims=True) + EPS)
        b_, _, t, _ = h.shape
        h = (jnp.transpose(h, (0, 2, 1, 3)).reshape(b_, t, MLSTM_V) * norm_g).astype(dtype)
        return (jax.nn.sigmoid(o) * h) @ w_out

    ol = readout(hlf, hlb, ol_g, hl.dtype)
    oc = readout(hcf, hcb, oc_g, hc.dtype) if ctx_out else None
    return oc, ol


def conv_ffn(h, w_up, conv_w, conv_b, w_down, on_grid):
    bsz, t, _ = h.shape
    z = h @ w_up
    if on_grid:
        rows = t // GRID_W
        z = dwconv2d(z.reshape(bsz, rows, GRID_W, z.shape[-1]), conv_w).reshape(bsz, t, z.shape[-1])
    else:
        z = dwconv1d(z, conv_w[FFN_CONV // 2], (1, 1))
    g, u = jnp.split(z + conv_b, 2, axis=-1)
    return (jax.nn.silu(g) * u) @ w_down


def setup_inputs(seed: int = 0) -> dict:
    key = jax.random.key(seed)
    ks = iter(jax.random.split(key, 48))
    D = D_MODEL
    F = FFN_HIDDEN

    def nrm(shape, scale):
        return jax.random.normal(next(ks), shape, jnp.float32) * scale

    def gain(shape):
        return 1.0 + nrm(shape, 0.02)

    x = nrm((BATCH, SEQ, D), 1.0)
    c = nrm((BATCH, D), 1.0)
    ctx = nrm((BATCH, CTX_LEN, D), 1.0)
    c_ctx = nrm((D,), 1.0)
    norm1_g = gain((DEPTH, D))
    norm2_g = gain((DEPTH, D))
    mod_w = nrm((DEPTH, D, 6 * D), 0.5 * D ** -0.5)
    mod_b = nrm((DEPTH, 6 * D), 0.02)
    ffn_w_up = nrm((DEPTH, D, 2 * F), D ** -0.5)
    ffn_conv_w = nrm((DEPTH, FFN_CONV, FFN_CONV, 2 * F), 1.0 / FFN_CONV)
    ffn_conv_b = nrm((DEPTH, 2 * F), 0.02)
    ffn_w_down = nrm((DEPTH, F, D), F ** -0.5)
    cm_w_in = nrm((N_CM, D, 2 * CM_WIDTH), D ** -0.5)
    cm_b_in = nrm((N_CM, 2 * CM_WIDTH), 0.02)
    cm_v_g = gain((N_CM, CM_WIDTH))
    cm_v_b = nrm((N_CM, CM_WIDTH), 0.02)
    cm_w_s = nrm((N_CM, CM_GROUPS, CHUNK, CHUNK), CHUNK ** -0.5)
    cm_b_s = 1.0 + nrm((N_CM, CM_GROUPS, CHUNK), 0.02)
    cm_w_out = nrm((N_CM, CM_WIDTH, D), CM_WIDTH ** -0.5)
    lru_w_in = nrm((N_LRU, D, 2 * LRU_WIDTH), D ** -0.5)
    lru_conv_w = nrm((N_LRU, LRU_CONV, LRU_WIDTH), LRU_CONV ** -0.5)
    lru_conv_b = nrm((N_LRU, LRU_WIDTH), 0.02)
    lru_w_rg = nrm((N_LRU, 2, LRU_HEADS, LRU_BLOCK, LRU_BLOCK), LRU_BLOCK ** -0.5)
    lru_b_rg = nrm((N_LRU, 2, LRU_WIDTH), 0.02)
    lru_w_ig = nrm((N_LRU, 2, LRU_HEADS, LRU_BLOCK, LRU_BLOCK), LRU_BLOCK ** -0.5)
    lru_b_ig = nrm((N_LRU, 2, LRU_WIDTH), 0.02)
    p = jax.random.uniform(next(ks), (N_LRU, 2, LRU_WIDTH), jnp.float32, 0.9, 0.999)
    lru_lambda = jnp.log(p) - jnp.log1p(-p)
    lru_w_out = nrm((N_LRU, LRU_WIDTH, D), LRU_WIDTH ** -0.5)
    ml_w_in = nrm((N_ML, D, 2 * MLSTM_QK + 2 * MLSTM_V + 4 * MLSTM_HEADS), D ** -0.5)
    ig_b = nrm((N_ML, 2, 1, MLSTM_HEADS), 0.1)
    fg_b = jax.random.uniform(next(ks), (N_ML, 2, 1, MLSTM_HEADS), jnp.float32, 3.0, 6.0)
    ml_b_gate = jnp.concatenate([ig_b, fg_b], axis=2)
    ml_norm_g = gain((N_ML, MLSTM_V))
    ml_w_out = nrm((N_ML, MLSTM_V, D), MLSTM_V ** -0.5)
    final_norm_g = gain((D,))
    return {"x": x, "c": c, "ctx": ctx, "c_ctx": c_ctx,
            "norm1_g": norm1_g, "norm2_g": norm2_g, "mod_w": mod_w, "mod_b": mod_b,
            "ffn_w_up": ffn_w_up, "ffn_conv_w": ffn_conv_w, "ffn_conv_b": ffn_conv_b, "ffn_w_down": ffn_w_down,
            "cm_w_in": cm_w_in, "cm_b_in": cm_b_in, "cm_v_g": cm_v_g, "cm_v_b": cm_v_b,
            "cm_w_s": cm_w_s, "cm_b_s": cm_b_s, "cm_w_out": cm_w_out,
            "lru_w_in": lru_w_in, "lru_conv_w": lru_conv_w, "lru_conv_b": lru_conv_b,
            "lru_w_rg": lru_w_rg, "lru_b_rg": lru_b_rg, "lru_w_ig": lru_w_ig, "lru_b_ig": lru_b_ig,
            "lru_lambda": lru_lambda, "lru_w_out": lru_w_out,
            "ml_w_in": ml_w_in, "ml_b_gate": ml_b_gate, "ml_norm_g": ml_norm_g, "ml_w_out": ml_w_out,
            "final_norm_g": final_norm_g}


def reference(x, c, ctx, c_ctx, norm1_g, norm2_g, mod_w, mod_b,
              ffn_w_up, ffn_conv_w, ffn_conv_b, ffn_w_down,
              cm_w_in, cm_b_in, cm_v_g, cm_v_b, cm_w_s, cm_b_s, cm_w_out,
              lru_w_in, lru_conv_w, lru_conv_b, lru_w_rg, lru_b_rg, lru_w_ig, lru_b_ig,
              lru_lambda, lru_w_out,
              ml_w_in, ml_b_gate, ml_norm_g, ml_w_out, final_norm_g):
    xl = x
    xc = ctx
    cond_lat = jax.nn.silu(c)
    cond_ctx = jax.nn.silu(c_ctx)
    for l in range(DEPTH):
        kind = l % N_MIXERS
        j = l // N_MIXERS
        last = l == DEPTH - 1
        sh1, sc1, g1, sh2, sc2, g2 = jnp.split((cond_lat @ mod_w[l] + mod_b[l])[:, None, :], 6, axis=-1)
        csh1, csc1, cg1, csh2, csc2, cg2 = jnp.split(cond_ctx @ mod_w[l] + mod_b[l], 6, axis=-1)
        hl = modulate(rms_norm(xl, norm1_g[l]), sh1, sc1)
        if kind == 0:
            cm_args = (cm_w_in[j], cm_b_in[j], cm_v_g[j], cm_v_b[j], cm_w_s[j], cm_b_s[j], cm_w_out[j])
            ol = chunk_mlp(hl, *cm_args)
            oc = None if last else chunk_mlp(modulate(rms_norm(xc, norm1_g[l]), csh1, csc1), *cm_args)
        elif kind == 1:
            hc = modulate(rms_norm(xc, norm1_g[l]), csh1, csc1)
            oc, ol = lru_mixer(hc, hl, lru_w_in[j], lru_conv_w[j], lru_conv_b[j], lru_w_rg[j], lru_b_rg[j],
                               lru_w_ig[j], lru_b_ig[j], lru_lambda[j], lru_w_out[j], not last)
        else:
            hc = modulate(rms_norm(xc, norm1_g[l]), csh1, csc1)
            oc, ol = mlstm_mixer(hc, hl, ml_w_in[j], ml_b_gate[j], ml_norm_g[j], ml_w_out[j], not last)
        ffn_args = (ffn_w_up[l], ffn_conv_w[l], ffn_conv_b[l], ffn_w_down[l])
        xl = xl + g1 * ol
        xl = xl + g2 * conv_ffn(modulate(rms_norm(xl, norm2_g[l]), sh2, sc2), *ffn_args, True)
        if not last:
            xc = xc + cg1 * oc
            xc = xc + cg2 * conv_ffn(modulate(rms_norm(xc, norm2_g[l]), csh2, csc2), *ffn_args, False)
    return rms_norm(xl, final_norm_g)
```

```python
import contextlib
import os
import numpy as np
import concourse.bass as bass
import concourse.mybir as mybir
from concourse.bass_utils import run_bass_kernel_spmd

F32 = mybir.dt.float32
BF16 = mybir.dt.bfloat16
AF = mybir.ActivationFunctionType
ALU = mybir.AluOpType

D = 1024
CT = 256
LT = 4096
T = CT + LT
NB = 2
FH = 2688
NF = 21
EPS = 1e-6
ENGS = ["pe", "act", "dve", "pool", "sp"]
HANDLE = {"pe": "tensor", "act": "scalar", "dve": "vector", "pool": "gpsimd", "sp": "sync"}


class Sched:
    def __init__(self, nc, gstack, n_dma_sems=16):
        self.nc = nc
        self.stack = gstack
        self.ops = {e: [] for e in ENGS}
        self.sem = {e: gstack.enter_context(nc.semaphore("s_" + e)) for e in ENGS}
        self.count = {e: 0 for e in ENGS}
        self.waited = {e: {} for e in ENGS}
        self.same_sync = {"act", "dve", "pool"}
        self.last_writer = {}
        self.readers = {}
        self.dma_sems = {}
        self.dma_rr = {}
        for q in ("sp", "pool"):
            self.dma_sems[q] = [[gstack.enter_context(nc.semaphore("d_%s%d" % (q, i))), 0] for i in range(n_dma_sems)]
            self.dma_rr[q] = 0
        self.n_ops = 0

    def sb(self, name, shape, dt=F32):
        self.n_sb = getattr(self, "n_sb", 0) + 1
        return self.stack.enter_context(self.nc.sbuf_tensor("sb%d_%s" % (self.n_sb, name), shape, dt))

    def _deps(self, reads, writes):
        deps = []
        for k in reads:
            w = self.last_writer.get(k)
            if w is not None:
                deps.append(w)
        for k in writes:
            w = self.last_writer.get(k)
            if w is not None:
                deps.append(w)
            deps.extend(self.readers.get(k, {}).values())
        return deps

    def _record(self, tok, reads, writes):
        for k in reads:
            self.readers.setdefault(k, {})[id(tok[0])] = tok
        for k in writes:
            self.last_writer[k] = tok
            self.readers[k] = {}

    def _waits(self, eng, deps):
        waits = []
        wd = self.waited[eng]
        own = self.sem[eng]
        for (sem, val) in deps:
            if sem is own and eng not in self.same_sync:
                continue
            if wd.get(id(sem), 0) < val:
                wd[id(sem)] = val
                waits.append((sem, val))
        return waits

    def op(self, eng, fn, reads=(), writes=()):
        deps = self._deps(reads, writes)
        waits = self._waits(eng, deps)
        self.count[eng] += 1
        tok = (self.sem[eng], self.count[eng])
        self.ops[eng].append((fn, waits, (self.sem[eng], 1)))
        self._record(tok, reads, writes)
        self.n_ops += 1
        return tok

    def dma(self, q, fn, reads=(), writes=()):
        deps = self._deps(reads, writes)
        slot = self.dma_sems[q][self.dma_rr[q]]
        self.dma_rr[q] = (self.dma_rr[q] + 1) % len(self.dma_sems[q])
        if slot[1] > 0:
            deps.append((slot[0], slot[1]))
        waits = self._waits(q, deps)
        slot[1] += 16
        tok = (slot[0], slot[1])
        self.ops[q].append((fn, waits, (slot[0], 16)))
        self._record(tok, reads, writes)
        self.n_ops += 1
        return tok

    def end_phase(self):
        toks = []
        for q in self.dma_sems:
            for sem, val in self.dma_sems[q]:
                if val > 0:
                    toks.append((sem, val))
        for e in ENGS:
            if self.count[e] > 0:
                toks.append((self.sem[e], self.count[e]))
        for e in ENGS:
            waits = self._waits(e, list(toks))
            self.ops[e].append((None, waits, None))
        nc = self.nc
        with nc.Block() as block:
            for e in ENGS:
                ops = self.ops[e]

                def body(eng, ops=ops):
                    for fn, waits, inc in ops:
                        for (sem, val) in waits:
                            eng.wait_ge(sem, val)
                        if fn is not None:
                            inst = fn(eng)
                            inst.then_inc(inc[0], inc[1])

                getattr(block, HANDLE[e])(body)
        self.ops = {e: [] for e in ENGS}
        self.last_writer = {}
        self.readers = {}

    def mm(self, out, lhsT, rhs, start, stop, reads, writes):
        return self.op("pe", lambda e: e.matmul(out, lhsT=lhsT, rhs=rhs, start=start, stop=stop), reads, writes)

    def act(self, out, in_, func, reads, writes, bias=0.0, scale=1.0, accum_out=None):
        if accum_out is None:
            return self.op("act", lambda e: e.activation(out=out, in_=in_, func=func, bias=bias, scale=scale), reads, writes)
        return self.op("act", lambda e: e.activation(out=out, in_=in_, func=func, bias=bias, scale=scale, accum_out=accum_out), reads, writes)

    def tt(self, out, in0, in1, op, reads, writes, eng="dve"):
        return self.op(eng, lambda e: e.tensor_tensor(out=out, in0=in0, in1=in1, op=op), reads, writes)

    def ts(self, out, in0, s1, s2, op0, op1, reads, writes, eng="dve"):
        if s2 is None:
            return self.op(eng, lambda e: e.tensor_scalar(out=out, in0=in0, scalar1=s1, scalar2=None, op0=op0), reads, writes)
        return self.op(eng, lambda e: e.tensor_scalar(out=out, in0=in0, scalar1=s1, scalar2=s2, op0=op0, op1=op1), reads, writes)

    def stt(self, out, in0, scalar, in1, op0, op1, reads, writes, accum_out=None):
        if accum_out is None:
            return self.op("dve", lambda e: e.scalar_tensor_tensor(out=out, in0=in0, scalar=scalar, in1=in1, op0=op0, op1=op1), reads, writes)
        return self.op("dve", lambda e: e.scalar_tensor_tensor(out=out, in0=in0, scalar=scalar, in1=in1, op0=op0, op1=op1, accum_out=accum_out), reads, writes)

    def load(self, out, in_, reads, writes, q="sp"):
        return self.dma(q, lambda e: e.dma_start(out=out, in_=in_), reads, writes)


def xkeys(name, b, t0, t1):
    return [(name, b, i) for i in range(t0 // 64, (t1 + 63) // 64)]


class Prog:
    def __init__(self, n_layers=4, dbg=None):
        self.n_layers = n_layers
        self.dbg = dbg

    def dram_in(self, name, shape, dt=F32):
        return self.nc.dram_tensor(name, list(shape), dt, kind="ExternalInput").ap()

    def dram_tmp(self, name, shape, dt=F32):
        return self.nc.dram_tensor(name, list(shape), dt, kind="Internal").ap()

    def build(self):
        nc = bass.Bass("TRN2", target_bir_lowering=False)
        self.nc = nc
        I = {}
        for name, shape in INPUT_SHAPES.items():
            I[name] = self.dram_in(name, shape)
        self.I = I
        self.outT = nc.dram_tensor("outT", [NB, 128, 8, LT], F32, kind="ExternalOutput").ap()
        self.XA = self.dram_tmp("XA", [NB, 128, 8, T])
        self.XB = self.dram_tmp("XB", [NB, 128, 8, T])
        self.up_bf = self.dram_tmp("up_bf", [4, NF, 128, 2, 8, 128], BF16)
        self.DG = self.dram_tmp("DGs", [NF, 128, 2, 9, 128], BF16)
        self.XR = self.dram_tmp("XRs", [NB, 128, 8, T])
        self.YG = self.dram_tmp("YGs", [NB, 128, 8, T])
        self.MT = self.dram_tmp("MTs", [NB, 128, 8, T], BF16)
        NCH = T // 128
        self.QT = self.dram_tmp("QTs", [NB, 128, 4, T])
        self.KT = self.dram_tmp("KTs", [NB, 128, 4, T])
        self.KTOK = self.dram_tmp("KTOKs", [NB, NCH, 128, 512])
        self.VTOK = self.dram_tmp("VTOKs", [NB, NCH, 128, 1024])
        self.OTOK = self.dram_tmp("OTOKs", [NB, NCH, 128, 1024])
        self.GT = self.dram_tmp("GTs", [NB, NCH, 128, 16])
        self.HF = self.dram_tmp("HFs", [NB, NCH, 128, 1024])
        if self.dbg is not None:
            self.dbgX = nc.dram_tensor("dbgX", [NB, 128, 8, T], F32, kind="ExternalOutput").ap()
        with contextlib.ExitStack() as gst:
            S = Sched(nc, gst)
            self.S = S
            self.ones_bf = S.sb("ones_bf", [128, 128], BF16)
            self.MOD = S.sb("MOD", [128, 4, 6, 8, 4])
            self.psum = [gst.enter_context(nc.psum_tensor("ps%d" % i, [128, 512], F32)) for i in range(8)]
            S.op("dve", lambda e: e.memset(self.ones_bf[:], 1.0), writes=["ones_bf"])
            self.phase_setup()
            cur = ("xin", I["xin"])
            bufs = [("XA", self.XA), ("XB", self.XB)]
            bi = 0
            for l in range(self.n_layers):
                last = l == 3
                kind = l % 3
                dst = bufs[bi]; bi ^= 1
                if kind == 0:
                    self.phase_cm(l, cur, dst, last)
                elif kind == 1:
                    self.phase_lru(l, cur, dst)
                else:
                    self.phase_mlstm(l, cur, dst)
                cur = dst
                if self.dbg == ("mix", l):
                    self.phase_dump(cur)
                dst = bufs[bi]; bi ^= 1
                self.phase_ffn(l, cur, dst, last)
                cur = dst
                if self.dbg == ("ffn", l):
                    self.phase_dump(cur)
            self.phase_final(cur)
        return nc

    def phase_dump(self, cur):
        S = self.S
        for b in range(NB):
            S.dma("sp", lambda e, b=b: e.dma_start(out=self.dbgX[b], in_=cur[1][b]), reads=xkeys(cur[0], b, 0, T), writes=[("dbg", b)])
        S.end_phase()

    def cast_up(self, l):
        S = self.S
        src = self.I["ffn_up"][l].rearrange("f p two k m -> (f p) (two k m)")
        dst = self.up_bf[l].rearrange("f p two k m -> (f p) (two k m)")
        rows = NF * 128
        RSTEP = 384
        for r0 in range(0, rows, RSTEP):
            S.dma("pool", lambda e, r0=r0: e.dma_start(out=dst[r0:r0 + RSTEP, :], in_=src[r0:r0 + RSTEP, :]), writes=[("up_bf", l)])

    def phase_setup(self):
        S = self.S
        I = self.I
        with contextlib.ExitStack() as pst:
            S.stack = pst
            cond = S.sb("cond", [128, 8, 4])
            condr = S.sb("condr", [128, 8, 4])
            modb = S.sb("modb", [128, 4, 48])
            ng = S.sb("ng", [128, 4, 2, 8])
            raw = S.sb("raw", [128, 48, 4])
            wbuf = [S.sb("modw%d" % i, [128, 8, 512]) for i in range(2)]
            S.load(condr[:], I["cond"], [], ["condr"])
            S.load(modb[:], I["mod_b"], [], ["modb"])
            S.load(ng[:], I["norm_g"], [], ["ng"])
            S.act(cond[:], condr[:], AF.Silu, ["condr"], ["cond"])
            mps = self.psum[0]
            mpsv = mps[:, 0:192].rearrange("p (c n) -> p c n", n=4)
            it = 0
            for l in range(4):
                for g in range(12):
                    wb = wbuf[it % 2]; wk = "modw%d" % (it % 2); it += 1
                    S.load(wb[:], I["mod_w"][l, :, :, g * 512:(g + 1) * 512], [], [wk])
                    for m in range(4):
                        ch = g * 4 + m
                        for k in range(8):
                            S.mm(mpsv[:, ch, :], wb[:, k, m * 128:(m + 1) * 128], cond[:, k, :], k == 0, k == 7,
                                 [wk, "cond"], ["mps"])
                S.tt(raw[:], mpsv, modb[:, l, :].rearrange("p (c n) -> p c n", n=1).to_broadcast([128, 48, 4]), ALU.add,
                     ["mps", "modb"], ["raw"])
                rv = raw[:].rearrange("p (s c) n -> p s c n", s=6)
                M = self.MOD
                for half in range(2):
                    sh, sc, gg = rv[:, 3 * half + 0], rv[:, 3 * half + 1], rv[:, 3 * half + 2]
                    gb = ng[:, l, half, :].rearrange("p (c n) -> p c n", n=1).to_broadcast([128, 8, 4])
                    S.stt(M[:, l, 3 * half + 0], sc, 1.0, gb, ALU.add, ALU.mult, ["raw", "ng"], ["MOD"])
                    S.op("dve", lambda e, o=M[:, l, 3 * half + 1], i=sh: e.tensor_copy(out=o, in_=i), ["raw"], ["MOD"])
                    S.op("dve", lambda e, o=M[:, l, 3 * half + 2], i=gg: e.tensor_copy(out=o, in_=i), ["raw"], ["MOD"])
            S.end_phase()

    def norm_mod(self, xt, n, A, B, hT, sq, xn, rstd, ssps, kx, kh, tag, kss=None):
        S = self.S
        kss = kss or ("ssps" + tag)
        S.act(sq[:, :, :n], xt[:, :, :n], AF.Square, [kx], ["sq" + tag])
        for c in range(8):
            S.mm(ssps[:, :n], self.ones_bf[:], sq[:, c, :n], c == 0, c == 7, ["sq" + tag, "ones_bf"], [kss])
        S.act(rstd[:, :n], ssps[:, :n], AF.Sqrt, [kss], ["rstd" + tag], bias=EPS, scale=1.0 / D)
        S.op("dve", lambda e: e.reciprocal(out=rstd[:, :n], in_=rstd[:, :n]), ["rstd" + tag], ["rstd" + tag])
        for c in range(8):
            tmp = xn[c % 2]; kt = "xn%d%s" % (c % 2, tag)
            S.tt(tmp[:, :n], xt[:, c, :n], rstd[:, :n], ALU.mult, [kx, "rstd" + tag], [kt])
            S.act(hT[:, c, :n], tmp[:, :n], AF.Identity, [kt, "MOD"], [kh], bias=B[:, c:c + 1], scale=A[:, c:c + 1])

    def seq_tiles(self, step=512):
        tiles = [(0, CT)]
        for i in range(LT // step):
            tiles.append((CT + i * step, CT + (i + 1) * step))
        return tiles

    def phase_cm(self, l, src, dst, last):
        S = self.S
        I = self.I
        j = l // 3
        M = self.MOD
        P = self.psum
        with contextlib.ExitStack() as pst:
            S.stack = pst
            NT = 256
            w_in = S.sb("cm_w_in", [128, 8, 4096], BF16)
            w_out = S.sb("cm_w_out", [128, 16, 1024], BF16)
            wsT = S.sb("cm_wsT", [128, 8, 128], BF16)
            bin_u = S.sb("cm_bin_u", [128, 2, 16])
            vgb = S.sb("cm_vgb", [128, 2, 2, 16])
            rep = S.sb("cm_rep", [128, 2048])
            CB = S.sb("cm_CB", [128, 16, 128])
            vt = S.sb("vt", [128, 2048])
            bs = vt[:, 0:1024].rearrange("p (g m) -> p g m", g=8)
            for k in range(8):
                for hh in range(2):
                    S.load(w_in[:, k, hh * 2048:(hh + 1) * 2048], I["cm_in"][j, :, k, hh * 2048:(hh + 1) * 2048], [], ["w_in"], q="pool")
            for c in range(16):
                S.load(w_out[:, c, :], I["cm_out"][j, :, c, :], [], ["w_out"], q="pool")
            S.load(wsT[:], I["cm_wsT"][j], [], ["wsT"], q="pool")
            self.cast_up(l)
            S.load(bin_u[:], I["cm_bin_u"], [], ["bin_u"])
            S.load(vgb[:], I["cm_vgb"], [], ["vgb"])
            S.load(rep[:], I["cm_rep"][j, 0], [], ["rep"])
            S.load(bs, I["cm_bs"][j], [], [("vt", 0), ("vt", 1)])
            for hh in range(2):
                S.mm(P[hh][:, :], self.ones_bf[:], wsT[:, 4 * hh:4 * hh + 4, :], True, True, ["ones_bf", "wsT"], ["pv%d" % hh])
            for c in range(16):
                g = c // 2
                S.stt(CB[:, c, :], P[g // 4][:, (g % 4) * 128:(g % 4 + 1) * 128], vgb[:, j, 1, c:c + 1], bs[:, g, :], ALU.mult, ALU.add,
                      ["pv%d" % (g // 4), "vgb", ("vt", 0), ("vt", 1)], ["CB"])
            xts = [S.sb("xt%d" % i, [128, 8, NT]) for i in range(1)]
            xos = [S.sb("xo%d" % i, [128, 8, NT]) for i in range(2)]
            hTs = [S.sb("hT%d" % i, [128, 8, NT], BF16) for i in range(2)]
            vns = [[S.sb("vn%d_%d" % (i, q), [128, 2048], BF16) for q in range(2)] for i in range(2)]
            mTs = [S.sb("mT%d" % i, [128, 16, NT], BF16) for i in range(2)]
            sq = S.sb("sq", [128, 8, NT], BF16)
            xn = [S.sb("xn%d" % i, [128, NT]) for i in range(2)]
            rstd = S.sb("rstd", [128, NT])
            vxs = [S.sb("vx%d" % i, [128, 512]) for i in range(2)]
            junk = S.sb("junk", [128, 1024], BF16)
            st = S.sb("st", [128, 16])
            ux = [S.sb("ux%d" % i, [128, NT]) for i in range(2)]
            sx = [S.sb("sx%d" % i, [128, NT]) for i in range(2)]
            tiles = [t for t in self.seq_tiles(NT) if not (t[0] < CT and last)]
            cnt = {"c": 0, "o": 0}

            def stN(s_, b, t0, t1):
                mi = 2 if t0 < CT else b
                n = t1 - t0
                xt = xts[0]; kx = "xt0"
                S.load(xt[:, :, :n], src[1][b, :, :, t0:t1], xkeys(src[0], b, t0, t1), [kx])
                self.norm_mod(xt, n, M[:, l, 0, :, mi], M[:, l, 1, :, mi], hTs[s_], sq, xn, rstd, P[0], kx, "hT%d" % s_, "", kss="pv0")

            def stV(s_, b, t0, t1):
                n = t1 - t0
                hTi = hTs[s_]; kh = "hT%d" % s_
                S.load(xos[s_][:, :, :n], src[1][b, :, :, t0:t1], xkeys(src[0], b, t0, t1), ["xo%d" % s_])
                for q in range(n // 128):
                    for cg in range(4):
                        pv = P[cg % 2]; kpv = "pv%d" % (cg % 2)
                        for k in range(8):
                            S.mm(pv[:, :], hTi[:, k, q * 128:(q + 1) * 128], w_in[:, k, 2048 + cg * 512:2048 + (cg + 1) * 512],
                                 k == 0, k == 7, [kh, "w_in"], [kpv])
                        S.tt(vxs[cg % 2][:], pv[:, :], rep[:, cg * 512:(cg + 1) * 512], ALU.add, [kpv, "rep"], [("vx", cg % 2)])
                        S.act(vt[:, cg * 512:(cg + 1) * 512], vxs[cg % 2][:], AF.Gelu_apprx_tanh,
                              [("vx", cg % 2)], [("vt", cg)], accum_out=st[:, cg:cg + 1])
                    vtk = [("vt", c_) for c_ in range(4)]
                    for hh in range(2):
                        S.act(junk[:], vt[:, hh * 1024:(hh + 1) * 1024], AF.Square, vtk, ["junk"], accum_out=st[:, 11 + hh:12 + hh])
                    S.op("dve", lambda e: e.reduce_sum(out=st[:, 5:6], in_=st[:, 0:4], axis=mybir.AxisListType.X), vtk, ["st5"])
                    S.tt(st[:, 4:5], st[:, 11:12], st[:, 12:13], ALU.add, ["junk"], ["st4"])
                    S.ts(st[:, 6:7], st[:, 5:6], 1.0 / 2048, None, ALU.mult, None, ["st5"], ["st6"])
                    S.tt(st[:, 7:8], st[:, 6:7], st[:, 6:7], ALU.mult, ["st6"], ["st7"])
                    S.stt(st[:, 8:9], st[:, 4:5], 1.0 / 2048, st[:, 7:8], ALU.mult, ALU.subtract, ["st4", "st7"], ["st8"])
                    S.act(st[:, 9:10], st[:, 8:9], AF.Sqrt, ["st8"], ["st9"], bias=EPS, scale=1.0)
                    S.op("dve", lambda e: e.reciprocal(out=st[:, 10:11], in_=st[:, 9:10]), ["st9"], ["st10"])
                    S.ts(vns[s_][q][:], vt[:], st[:, 6:7], st[:, 10:11], ALU.subtract, ALU.mult, vtk + ["st6", "st10"], ["vn%d_%d" % (s_, q)])

            def stU(s_, b, t0, t1):
                n = t1 - t0
                nq = n // 128
                hTi = hTs[s_]; kh = "hT%d" % s_
                mT = mTs[s_]
                for c in range(16):
                    g = c // 2
                    cit = cnt["c"]; cnt["c"] += 1
                    pu = P[2 + cit % 2]; kpu = "pu%d" % (cit % 2)
                    psx = P[4 + cit % 2]; kps = "psx%d" % (cit % 2)
                    uxt = ux[cit % 2]; kux = "ux%d" % (cit % 2)
                    sxt = sx[cit % 2]; ksx = "sx%d" % (cit % 2)
                    for k in range(8):
                        S.mm(pu[:, :n], w_in[:, k, c * 128:(c + 1) * 128], hTi[:, k, :n], k == 0, k == 7, [kh, "w_in"], [kpu])
                    for q in range(nq):
                        S.mm(psx[:, q * 128:(q + 1) * 128], vns[s_][q][:, c * 128:(c + 1) * 128], wsT[:, g, :], True, True,
                             ["vn%d_%d" % (s_, q), "wsT"], [kps])
                    S.act(uxt[:, :n], pu[:, :n], AF.Gelu_apprx_tanh, [kpu, "bin_u"], [kux], bias=bin_u[:, j, c:c + 1])
                    S.stt(sxt[:, :n].rearrange("p (q m) -> p q m", m=128), psx[:, :n].rearrange("p (q m) -> p q m", m=128),
                          vgb[:, j, 0, c:c + 1], CB[:, c, :].rearrange("p (q m) -> p q m", q=1).to_broadcast([128, nq, 128]),
                          ALU.mult, ALU.add, [kps, "vgb", "CB"], [ksx])
                    S.tt(mT[:, c, :n], sxt[:, :n], uxt[:, :n], ALU.mult, [ksx, kux], [("mT%d" % s_, c)])

            def stO(s_, b, t0, t1):
                mi = 2 if t0 < CT else b
                n = t1 - t0
                mT = mTs[s_]
                xo = xos[s_]; kxo = "xo%d" % s_
                for d in range(8):
                    oi = cnt["o"]; cnt["o"] += 1
                    po = P[6 + oi % 2]; kpo = "po%d" % (oi % 2)
                    for c in range(16):
                        S.mm(po[:, :n], w_out[:, c, d * 128:(d + 1) * 128], mT[:, c, :n], c == 0, c == 15, [("mT%d" % s_, c), "w_out"], [kpo])
                    S.stt(xo[:, d, :n], po[:, :n], M[:, l, 2, d, mi:mi + 1], xo[:, d, :n], ALU.mult, ALU.add, [kpo, kxo, "MOD"], [kxo])
                S.dma("pool", lambda e, xo=xo, b=b, t0=t0, t1=t1, n=n: e.dma_start(out=dst[1][b, :, :, t0:t1], in_=xo[:, :, :n]),
                      [kxo], xkeys(dst[0], b, t0, t1))

            for s_ in range(NB):
                stN(s_, s_, *tiles[0])
            for i, (t0, t1) in enumerate(tiles):
                for s_ in range(NB):
                    stV(s_, s_, t0, t1)
                for s_ in range(NB):
                    stU(s_, s_, t0, t1)
                if i + 1 < len(tiles):
                    for s_ in range(NB):
                        stN(s_, s_, *tiles[i + 1])
                for s_ in range(NB):
                    stO(s_, s_, t0, t1)
            S.end_phase()

    def ffn_tiles(self):
        out = []
        out.append((1, CT, 0, [(0, 1, 0, 1)]))
        lat = []
        R = 6
        r = 0
        c_prev = 0
        while r < 64:
            r1 = min(r + R, 64)
            c1 = min(r1 + 1, 64)
            lat.append((c_prev, c1, r, r1))
            c_prev = c1
            r = r1
        out.append((64, 64, CT, lat))
        return out

    def phase_ffn(self, l, src, dst, last):
        S = self.S
        I = self.I
        M = self.MOD
        DG = self.DG
        with contextlib.ExitStack() as pst:
            S.stack = pst
            w_down = S.sb("w_down", [128, NF, 1024], BF16)
            cw = S.sb("cw", [128, 42, 9])
            cb = S.sb("cb", [128, 42])
            ident = S.sb("ident", [128, 128])
            for f in range(NF):
                S.load(w_down[:, f, :], I["ffn_down"][l, :, f, :], [], ["w_down"], q="pool")
            S.load(cw[:], I["ffn_cw"][:, l], [], ["cw"])
            S.load(cb[:], I["ffn_cb"][:, l], [], ["cb"])
            S.load(ident[:], I["ml_cst"][:, 0, :], [], ["ident"])
            dgst = [S.sb("dgst%d" % i, [128, 9, 128], BF16) for i in range(3)]
            for ch in range(42):
                h, f = ch // NF, ch % NF
                t_ = dgst[ch % 3]; kt = "dgst%d" % (ch % 3)
                S.tt(t_[:], ident[:].rearrange("p (t m) -> p t m", t=1).to_broadcast([128, 9, 128]),
                     cw[:, ch, :].rearrange("p (t m) -> p t m", m=1).to_broadcast([128, 9, 128]), ALU.mult, ["ident", "cw"], [kt])
                S.dma("sp", lambda e, t_=t_, f=f, h=h: e.dma_start(out=DG[f, :, h], in_=t_[:]), [kt], [("DG", f)])
            xts = [S.sb("xt%d" % i, [128, 8, 448]) for i in range(2)]
            xos = [S.sb("xo%d" % i, [128, 8, 384]) for i in range(1)]
            sq = S.sb("sq", [128, 8, 448], BF16)
            xn = [S.sb("xn%d" % i, [128, 448]) for i in range(2)]
            rstd = S.sb("rstd", [128, 448])
            hTs = [S.sb("hT%d" % i, [128, 8, 448], BF16) for i in range(2)]
            wups = [S.sb("wup%d" % i, [128, 2, 8, 128], BF16) for i in range(3)]
            dgs = [S.sb("dg%d" % i, [128, 2, 9, 128], BF16) for i in range(3)]
            zbs = [S.sb("zb%d" % i, [128, 2, 768], BF16) for i in range(2)]
            ZS = S.sb("ZS", [128, NF, 2, 128], BF16)
            sg = [S.sb("sg%d" % i, [128, 384]) for i in range(2)]
            aTs = [S.sb("aT%d" % i, [128, NF, 384], BF16) for i in range(2)]
            P = self.psum
            it = 0
            wit = 0
            fit = 0
            for b in range(NB):
                for (H, W, s0, tl) in self.ffn_tiles():
                    isctx = s0 == 0
                    if isctx and last:
                        continue
                    mi = 2 if isctx else b
                    for (c0, c1, or0, or1) in tl:
                        n = (c1 - c0) * W
                        nout = (or1 - or0) * W
                        t0 = s0 + c0 * W
                        o0 = s0 + or0 * W
                        zr0 = or0 - 1
                        nzr = or1 - or0 + 2
                        xt = xts[it % 2]; kx = "xt%d" % (it % 2)
                        xo = xos[0]; kxo = "xo0"
                        hT = hTs[it % 2]; kh = "hT%d" % (it % 2)
                        aT = aTs[it % 2]; ka = "aT%d" % (it % 2)
                        it += 1
                        S.load(xt[:, :, :n], src[1][b, :, :, t0:t0 + n], xkeys(src[0], b, t0, t0 + n), [kx])
                        S.load(xo[:, :, :nout], src[1][b, :, :, o0:o0 + nout], xkeys(src[0], b, o0, o0 + nout), [kxo])
                        self.norm_mod(xt, n, M[:, l, 3, :, mi], M[:, l, 4, :, mi], hT, sq, xn, rstd, P[7], kx, kh, "", kss="po1")

                        def up(f):
                            nonlocal wit
                            wu = wups[wit % 3]; kw = "wup%d" % (wit % 3)
                            dg = dgs[wit % 3]; kd = "dg%d" % (wit % 3)
                            wit += 1
                            S.load(wu[:], self.up_bf[l, f], [], [kw])
                            S.load(dg[:], DG[f], [("DG", f)], [kd])
                            zb = zbs[f % 2]; kzb = "zb%d" % (f % 2)
                            if c0 > 0:
                                S.op("pool", lambda e, zb=zb, f=f: e.tensor_copy(out=zb[:, :, 0:2 * W], in_=ZS[:, f, :, :]), [("ZS", f)], [(kzb, 0)])
                            for h in range(2):
                                pz = P[h]; kz = "pz%d" % h
                                for k in range(8):
                                    S.mm(pz[:, :n], wu[:, h, k, :], hT[:, k, :n], k == 0, k == 7, [kw, kh], [kz])
                                zoff = (c0 - zr0) * W
                                if h == 0:
                                    S.act(zb[:, h, zoff:zoff + n], pz[:, :n], AF.Copy, [kz], [(kzb, 1 + h)])
                                else:
                                    S.op("dve", lambda e, o=zb[:, h, zoff:zoff + n], i_=pz[:, :n]: e.tensor_copy(out=o, in_=i_), [kz], [(kzb, 1 + h)])
                            if c1 < H:
                                soff = (c1 - 2 - zr0) * W
                                S.op("pool", lambda e, zb=zb, f=f, soff=soff: e.tensor_copy(out=ZS[:, f, :, :], in_=zb[:, :, soff:soff + 2 * W]),
                                     [(kzb, 0), (kzb, 1), (kzb, 2)], [("ZS", f)])
                            return (zb, kzb, dg, kd)

                        def conv(f, st):
                            nonlocal fit
                            zb, kzb, dg, kd = st
                            pcs = []
                            for h in range(2):
                                pc = P[2 + 2 * h + fit % 2]; kpc = "pc%d_%d" % (h, fit % 2)
                                zv = zb[:, h, 0:nzr * W].rearrange("p (r w) -> p r w", w=W)
                                pv = pc[:, :nout].rearrange("p (r w) -> p r w", w=W)
                                taps = [(0, 0)] + [(dr, dc) for dr in (-1, 0, 1) for dc in (-1, 0, 1) if not (dr == 0 and dc == 0)]
                                todo = []
                                for (dr, dc) in taps:
                                    rlo = max(or0, -dr); rhi = min(or1, H - dr)
                                    clo = max(0, -dc); chi = min(W, W - dc)
                                    if rlo >= rhi or clo >= chi:
                                        continue
                                    todo.append((dr, dc, rlo, rhi, clo, chi))
                                for ti, (dr, dc, rlo, rhi, clo, chi) in enumerate(todo):
                                    tap = (dr + 1) * 3 + (dc + 1)
                                    S.mm(pv[:, rlo - or0:rhi - or0, clo:chi], dg[:, h, tap, :],
                                         zv[:, rlo + dr - zr0:rhi + dr - zr0, clo + dc:chi + dc], ti == 0, ti == len(todo) - 1,
                                         [kd, (kzb, 0), (kzb, 1 + h)], [kpc])
                                pcs.append((pc, kpc))
                            sgt = sg[fit % 2]; ksg = "sg%d" % (fit % 2)
                            S.act(sgt[:, :nout], pcs[0][0][:, :nout], AF.Silu, [pcs[0][1], "cb"], [ksg], bias=cb[:, f:f + 1])
                            S.stt(aT[:, f, :nout], pcs[1][0][:, :nout], cb[:, NF + f:NF + f + 1], sgt[:, :nout], ALU.add, ALU.mult,
                                  [pcs[1][1], "cb", ksg], [(ka, f)])
                            fit += 1

                        prev = None
                        for f in range(NF):
                            st = up(f)
                            if prev is not None:
                                conv(f - 1, prev)
                            prev = st
                        conv(NF - 1, prev)
                        for d in range(8):
                            po = P[6 + d % 2]; kpo = "po%d" % (d % 2)
                            for f in range(NF):
                                S.mm(po[:, :nout], w_down[:, f, d * 128:(d + 1) * 128], aT[:, f, :nout], f == 0, f == NF - 1,
                                     [(ka, f), "w_down"], [kpo])
                            S.stt(xo[:, d, :nout], po[:, :nout], M[:, l, 5, d, mi:mi + 1], xo[:, d, :nout], ALU.mult, ALU.add,
                                  [kpo, kxo, "MOD"], [kxo])
                        S.dma("pool", lambda e, xo=xo, b=b, o0=o0, nout=nout: e.dma_start(out=dst[1][b, :, :, o0:o0 + nout], in_=xo[:, :, :nout]),
                              [kxo], xkeys(dst[0], b, o0, o0 + nout))
            S.end_phase()

    def phase_lru(self, l, src, dst):
        S = self.S
        I = self.I
        M = self.MOD
        P = self.psum
        XR, YG, MT = self.XR, self.YG, self.MT
        tiles = self.seq_tiles(512)
        with contextlib.ExitStack() as pst:
            S.stack = pst
            w_in = S.sb("lru_w_in", [128, 8, 2048], BF16)
            for k in range(8):
                S.load(w_in[:, k, :], I["lru_in"][:, k, :], [], ["w_in"], q="pool")
            self.cast_up(l)
            xts = [S.sb("xt%d" % i, [128, 8, 512]) for i in range(2)]
            sq = S.sb("sq", [128, 8, 512], BF16)
            xn = [S.sb("xn%d" % i, [128, 512]) for i in range(2)]
            rstd = S.sb("rstd", [128, 512])
            hT = S.sb("hT", [128, 8, 512], BF16)
            ygs = [S.sb("yg%d" % i, [128, 8, 512]) for i in range(2)]
            xrs = [S.sb("xr%d" % i, [128, 8, 512]) for i in range(2)]
            it = 0
            pit = 0
            for b in range(NB):
                for (t0, t1) in tiles:
                    mi = 2 if t0 < CT else b
                    n = t1 - t0
                    xt = xts[it % 2]; kx = "xt%d" % (it % 2)
                    ygt = ygs[it % 2]; kyg = "yg%d" % (it % 2)
                    xrt = xrs[it % 2]; kxr = "xr%d" % (it % 2)
                    it += 1
                    S.load(xt[:, :, :n], src[1][b, :, :, t0:t1], xkeys(src[0], b, t0, t1), [kx])
                    self.norm_mod(xt, n, M[:, l, 0, :, mi], M[:, l, 1, :, mi], hT, sq, xn, rstd, P[7], kx, "hT", "")
                    for c in range(16):
                        pp = P[pit % 4]; kp = "pp%d" % (pit % 4); pit += 1
                        for k in range(8):
                            S.mm(pp[:, :n], w_in[:, k, c * 128:(c + 1) * 128], hT[:, k, :n], k == 0, k == 7, ["hT", "w_in"], [kp])
                        if c < 8:
                            S.act(ygt[:, c, :n], pp[:, :n], AF.Gelu_apprx_tanh, [kp], [kyg])
                        else:
                            S.op("dve", lambda e, o=xrt[:, c - 8, :n], i_=pp[:, :n]: e.tensor_copy(out=o, in_=i_), [kp], [kxr])
                    S.dma("pool", lambda e, t=ygt, b=b, t0=t0, t1=t1, n=n: e.dma_start(out=YG[b, :, :, t0:t1], in_=t[:, :, :n]),
                          [kyg], xkeys("YG", b, t0, t1))
                    S.dma("pool", lambda e, t=xrt, b=b, t0=t0, t1=t1, n=n: e.dma_start(out=XR[b, :, :, t0:t1], in_=t[:, :, :n]),
                          [kxr], xkeys("XR", b, t0, t1))
            S.end_phase()
        with contextlib.ExitStack() as pst:
            S.stack = pst
            gw = S.sb("lru_gw", [128, 2, 2, 4, 2, 256], BF16)
            for d in range(2):
                for g in range(2):
                    S.load(gw[:, d, g], I["lru_gw"][:, d, g], [], ["gw"], q="pool")
            vec = S.sb("lru_vec", [128, 8, 11])
            S.load(vec[:], I["lru_vec"], [], ["vec"])
            c8 = S.sb("c8", [128, 8, 2])
            S.act(c8[:], vec[:, :, 9:11], AF.Exp, ["vec"], ["c8"], scale=-1.0)
            S.act(c8[:], c8[:], AF.Ln, ["c8"], ["c8"], bias=1.0)
            S.ts(c8[:], c8[:], -8.0, None, ALU.mult, None, ["c8"], ["c8"])
            xraw = [S.sb("xraw%d" % i, [128, 2, 515]) for i in range(2)]
            XC = S.sb("XC", [128, 2, T])
            XCb = S.sb("XCb", [128, 2, T], BF16)
            A = S.sb("A", [128, T])
            Bt = S.sb("Bt", [128, T])
            Gi = S.sb("Gi", [128, T])
            Hf = S.sb("Hf", [128, T])
            Hb = S.sb("Hb", [128, T])
            Y = S.sb("Y", [128, T])
            mo = S.sb("mo", [128, T], BF16)
            rit = 0
            pit = 0
            for b in range(NB):
                for hd in range(4):
                    for (t0, t1) in tiles:
                        s0, s1 = (0, CT) if t0 < CT else (CT, T)
                        n = t1 - t0
                        lo = max(t0 - 2, s0); hi = min(t1 + 1, s1)
                        xr = xraw[rit % 2]; kr = "xraw%d" % (rit % 2); rit += 1
                        S.op("pool", lambda e, xr=xr: e.memset(xr[:], 0.0), [], [kr])
                        S.load(xr[:, :, lo - (t0 - 2):hi - (t0 - 2)], XR[b, :, 2 * hd:2 * hd + 2, lo:hi], xkeys("XR", b, lo, hi), [kr])
                        for jj in range(2):
                            ch = 2 * hd + jj
                            S.act(XC[:, jj, t0:t1], xr[:, jj, 2:2 + n], AF.Identity, [kr, "vec"], [("XC", jj)],
                                  bias=vec[:, ch, 4:5], scale=vec[:, ch, 2:3])
                            for k_, off in ((0, -2), (1, -1), (3, 1)):
                                S.stt(XC[:, jj, t0:t1], xr[:, jj, 2 + off:2 + off + n], vec[:, ch, k_:k_ + 1], XC[:, jj, t0:t1],
                                      ALU.mult, ALU.add, [kr, "vec", ("XC", jj)], [("XC", jj)])
                    S.op("dve", lambda e: e.tensor_copy(out=XCb[:], in_=XC[:]), [("XC", 0), ("XC", 1)], ["XCb"])
                    for jj in range(2):
                        ch = 2 * hd + jj
                        S.load(Y[:], YG[b, :, ch, :], xkeys("YG", b, 0, T), ["Y"])
                        for d in range(2):
                            for (t0, t1) in tiles:
                                n = t1 - t0
                                pr = P[pit % 2]; kpr = "pr%d" % (pit % 2)
                                pi = P[2 + pit % 2]; kpi = "pi%d" % (pit % 2)
                                pit += 1
                                for ii in range(2):
                                    S.mm(pr[:, :n], gw[:, d, 0, hd, ii, jj * 128:(jj + 1) * 128], XCb[:, ii, t0:t1], ii == 0, ii == 1,
                                         ["gw", "XCb"], [kpr])
                                for ii in range(2):
                                    S.mm(pi[:, :n], gw[:, d, 1, hd, ii, jj * 128:(jj + 1) * 128], XCb[:, ii, t0:t1], ii == 0, ii == 1,
                                         ["gw", "XCb"], [kpi])
                                S.act(A[:, t0:t1], pr[:, :n], AF.Sigmoid, [kpr, "vec"], ["A"], bias=vec[:, ch, 5 + d:6 + d])
                                S.act(Gi[:, t0:t1], pi[:, :n], AF.Sigmoid, [kpi, "vec"], ["Gi"], bias=vec[:, ch, 7 + d:8 + d])
                            S.act(A[:], A[:], AF.Exp, ["A", "c8"], ["A"], scale=c8[:, ch, d:d + 1])
                            S.tt(Bt[:], A[:], A[:], ALU.mult, ["A"], ["Bt"])
                            S.act(Bt[:], Bt[:], AF.Sqrt, ["Bt"], ["Bt"], bias=1.0, scale=-1.0)
                            S.tt(Gi[:], Gi[:], XC[:, jj, :], ALU.mult, ["Gi", ("XC", jj)], ["Gi"])
                            S.tt(Bt[:], Bt[:], Gi[:], ALU.mult, ["Bt", "Gi"], ["Bt"])
                            if d == 0:
                                S.op("dve", lambda e: e.tensor_tensor_scan(out=Hf[:], data0=A[:], data1=Bt[:], initial=0.0,
                                                                           op0=ALU.mult, op1=ALU.add), ["A", "Bt"], ["Hf"])
                            else:
                                S.op("dve", lambda e: e.tensor_tensor_scan(out=Hb[:, 0:CT][:, ::-1], data0=A[:, 0:CT][:, ::-1],
                                                                           data1=Bt[:, 0:CT][:, ::-1], initial=0.0,
                                                                           op0=ALU.mult, op1=ALU.add), ["A", "Bt"], ["Hb"])
                                S.op("dve", lambda e: e.tensor_tensor_scan(out=Hb[:, CT:T][:, ::-1], data0=A[:, CT:T][:, ::-1],
                                                                           data1=Bt[:, CT:T][:, ::-1], initial=Hb[:, 0:1],
                                                                           op0=ALU.mult, op1=ALU.add), ["A", "Bt", "Hb"], ["Hb"])
                        S.tt(Hf[:], Hf[:], Hb[:], ALU.add, ["Hf", "Hb"], ["Hf"])
                        S.tt(mo[:], Hf[:], Y[:], ALU.mult, ["Hf", "Y"], ["mo"])
                        S.dma("pool", lambda e, b=b, ch=ch: e.dma_start(out=MT[b, :, ch, :], in_=mo[:]), ["mo"], xkeys("MT", b, 0, T))
            S.end_phase()
        with contextlib.ExitStack() as pst:
            S.stack = pst
            w_out = S.sb("lru_w_out", [128, 8, 1024], BF16)
            for k in range(8):
                S.load(w_out[:, k, :], I["lru_out"][:, k, :], [], ["w_out"], q="pool")
            self.out_proj(l, src, dst, MT, "MT", w_out, 8, tiles)
            S.end_phase()

    def out_proj(self, l, src, dst, MTd, mname, w_out, nk, tiles):
        S = self.S
        M = self.MOD
        P = self.psum
        xts = [S.sb("xt%d" % i, [128, 8, 512]) for i in range(2)]
        mts = [S.sb("mt%d" % i, [128, nk, 512], BF16) for i in range(2)]
        it = 0
        pit = 0
        for b in range(NB):
            for (t0, t1) in tiles:
                mi = 2 if t0 < CT else b
                n = t1 - t0
                xt = xts[it % 2]; kx = "xt%d" % (it % 2)
                mt = mts[it % 2]; km = "mt%d" % (it % 2)
                it += 1
                S.load(xt[:, :, :n], src[1][b, :, :, t0:t1], xkeys(src[0], b, t0, t1), [kx])
                S.load(mt[:, :, :n], MTd[b, :, :, t0:t1], xkeys(mname, b, t0, t1), [km])
                for d in range(8):
                    po = P[pit % 4]; kpo = "po%d" % (pit % 4); pit += 1
                    for k in range(nk):
                        S.mm(po[:, :n], w_out[:, k, d * 128:(d + 1) * 128], mt[:, k, :n], k == 0, k == nk - 1, [km, "w_out"], [kpo])
                    S.stt(xt[:, d, :n], po[:, :n], M[:, l, 2, d, mi:mi + 1], xt[:, d, :n], ALU.mult, ALU.add, [kpo, kx, "MOD"], [kx])
                S.dma("pool", lambda e, xt=xt, b=b, t0=t0, t1=t1, n=n: e.dma_start(out=dst[1][b, :, :, t0:t1], in_=xt[:, :, :n]),
                      [kx], xkeys(dst[0], b, t0, t1))

    def phase_mlstm(self, l, src, dst):
        S = self.S
        I = self.I
        M = self.MOD
        P = self.psum
        QT, KT, KTOK, VTOK, OTOK, GT, HF, MT = self.QT, self.KT, self.KTOK, self.VTOK, self.OTOK, self.GT, self.HF, self.MT
        tiles = self.seq_tiles(512)
        NCH = T // 128
        with contextlib.ExitStack() as pst:
            S.stack = pst
            w_in = S.sb("ml_w_in", [128, 8, 3088], BF16)
            for k in range(8):
                S.load(w_in[:, k, 0:2048], I["ml_in"][:, k, 0:2048], [], ["w_in"], q="pool")
                S.load(w_in[:, k, 2048:3088], I["ml_in"][:, k, 2048:3088], [], ["w_in"], q="pool")
            self.cast_up(l)
            bg = S.sb("ml_bg", [128, 16])
            S.load(bg[:], I["ml_bg"], [], ["bg"])
            xts = [S.sb("xt%d" % i, [128, 8, 512]) for i in range(2)]
            sq = S.sb("sq", [128, 8, 512], BF16)
            xn = [S.sb("xn%d" % i, [128, 512]) for i in range(2)]
            rstd = S.sb("rstd", [128, 512])
            hT = S.sb("hT", [128, 8, 512], BF16)
            qk = [S.sb("qk%d" % i, [128, 8, 512]) for i in range(2)]
            tok = [S.sb("tok%d" % i, [128, 2560]) for i in range(2)]
            gt = [S.sb("gt%d" % i, [128, 16]) for i in range(2)]
            gtmp = S.sb("gtmp", [128, 2, 4])
            it = 0
            pit = 0
            qit = 0
            for b in range(NB):
                for (t0, t1) in tiles:
                    mi = 2 if t0 < CT else b
                    n = t1 - t0
                    xt = xts[it % 2]; kx = "xt%d" % (it % 2)
                    qkt = qk[it % 2]; kqk = "qk%d" % (it % 2)
                    it += 1
                    S.load(xt[:, :, :n], src[1][b, :, :, t0:t1], xkeys(src[0], b, t0, t1), [kx])
                    self.norm_mod(xt, n, M[:, l, 0, :, mi], M[:, l, 1, :, mi], hT, sq, xn, rstd, P[7], kx, "hT", "")
                    for c in range(8):
                        pp = P[pit % 4]; kp = "pp%d" % (pit % 4); pit += 1
                        for k in range(8):
                            S.mm(pp[:, :n], w_in[:, k, c * 128:(c + 1) * 128], hT[:, k, :n], k == 0, k == 7, ["hT", "w_in"], [kp])
                        S.act(qkt[:, c, :n], pp[:, :n], AF.Copy, [kp], [kqk], scale=(128.0 ** -0.5 if c < 4 else 1.0))
                    S.dma("pool", lambda e, t=qkt, b=b, t0=t0, t1=t1, n=n: e.dma_start(out=QT[b, :, :, t0:t1], in_=t[:, 0:4, :n]),
                          [kqk], xkeys("QT", b, t0, t1))
                    S.dma("pool", lambda e, t=qkt, b=b, t0=t0, t1=t1, n=n: e.dma_start(out=KT[b, :, :, t0:t1], in_=t[:, 4:8, :n]),
                          [kqk], xkeys("KT", b, t0, t1))
                    for q in range(n // 128):
                        ci = (t0 + q * 128) // 128
                        tk = tok[qit % 2]; ktk = "tok%d" % (qit % 2)
                        g_ = gt[qit % 2]; kg = "gt%d" % (qit % 2)
                        qit += 1
                        for grp in range(5):
                            pp = P[pit % 4]; kp = "pp%d" % (pit % 4); pit += 1
                            for k in range(8):
                                S.mm(pp[:, :], hT[:, k, q * 128:(q + 1) * 128], w_in[:, k, 512 + grp * 512:512 + (grp + 1) * 512],
                                     k == 0, k == 7, ["hT", "w_in"], [kp])
                            if grp < 3:
                                if grp % 2 == 0:
                                    S.act(tk[:, grp * 512:(grp + 1) * 512], pp[:, :], AF.Copy, [kp], [(ktk, grp)])
                                else:
                                    S.op("dve", lambda e, o=tk[:, grp * 512:(grp + 1) * 512], i_=pp[:, :]: e.tensor_copy(out=o, in_=i_), [kp], [(ktk, grp)])
                            else:
                                S.act(tk[:, grp * 512:(grp + 1) * 512], pp[:, :], AF.Sigmoid, [kp], [(ktk, grp)])
                        pp = P[4 + qit % 2]; kp = "pg%d" % (qit % 2)
                        for k in range(8):
                            S.mm(pp[:, 0:16], hT[:, k, q * 128:(q + 1) * 128], w_in[:, k, 3072:3088], k == 0, k == 7, ["hT", "w_in"], [kp])
                        S.tt(g_[:], pp[:, 0:16], bg[:], ALU.add, [kp, "bg"], [kg])
                        gv = g_[:].rearrange("p (d g h) -> p d g h", d=2, g=2)
                        S.act(gtmp[:], gv[:, :, 1, :], AF.Exp, [kg], ["gtmp"], scale=-1.0)
                        S.act(gtmp[:], gtmp[:], AF.Ln, ["gtmp"], ["gtmp"], bias=1.0)
                        S.ts(gv[:, :, 1, :], gtmp[:], -1.0, None, ALU.mult, None, ["gtmp", kg], [kg])
                        S.dma("pool", lambda e, tk=tk, b=b, ci=ci: e.dma_start(out=KTOK[b, ci], in_=tk[:, 0:512]), [(ktk, 0)], [("KTOK", b, ci)])
                        S.dma("pool", lambda e, tk=tk, b=b, ci=ci: e.dma_start(out=VTOK[b, ci], in_=tk[:, 512:1536]), [(ktk, 1), (ktk, 2)], [("VTOK", b, ci)])
                        S.dma("pool", lambda e, tk=tk, b=b, ci=ci: e.dma_start(out=OTOK[b, ci], in_=tk[:, 1536:2560]), [(ktk, 3), (ktk, 4)], [("OTOK", b, ci)])
                        S.dma("pool", lambda e, g_=g_, b=b, ci=ci: e.dma_start(out=GT[b, ci], in_=g_[:]), [kg], [("GT", b, ci)])
            S.end_phase()
        with contextlib.ExitStack() as pst:
            S.stack = pst
            cst = S.sb("ml_cst", [128, 4, 128])
            S.load(cst[:], I["ml_cst"], [], ["cst"])
            ident_bf = S.sb("ident_bf", [128, 128], BF16)
            S.op("dve", lambda e: e.tensor_copy(out=ident_bf[:], in_=cst[:, 0, :]), ["cst"], ["ident_bf"])
            ngt = S.sb("ml_ng", [128, 1024])
            S.load(ngt[:], I["ml_ng"], [], ["ngt"])
            qTs = [S.sb("qT%d" % i, [128, 4, 128]) for i in range(4)]
            kTs = [S.sb("kT%d" % i, [128, 4, 128]) for i in range(4)]
            kts = [S.sb("ktok%d" % i, [128, 512]) for i in range(4)]
            vts = [S.sb("vtok%d" % i, [128, 1024]) for i in range(4)]
            gs = [S.sb("g%d" % i, [128, 16]) for i in range(4)]
            hfs = [S.sb("hf%d" % i, [128, 1024]) for i in range(4)]
            ots = [S.sb("ot%d" % i, [128, 1024]) for i in range(4)]
            kTbs = [S.sb("kTb%d" % i, [128, 4, 128], BF16) for i in range(NB)]
            LFbs = [S.sb("LFb%d" % i, [128, 4, 128]) for i in range(NB)]
            Erows = [S.sb("Erow%d" % i, [128, 4, 128]) for i in range(NB)]
            sms = [S.sb("sm%d" % i, [128, 8, 4]) for i in range(NB)]
            qss = [S.sb("qs%d" % i, [128, 4, 128], BF16) for i in range(NB)]
            kss = [S.sb("ks%d" % i, [128, 4, 128], BF16) for i in range(NB)]
            vexts = [S.sb("vext%d" % i, [128, 4, 260], BF16) for i in range(NB)]
            STss = [S.sb("STs%d" % i, [128, 4, 128], BF16) for i in range(NB)]
            STfs = [S.sb("STf%d" % i, [128, 4, 128]) for i in range(NB)]
            Cs = [S.sb("C%d" % i, [128, 4, 260]) for i in range(NB)]
            Cbfs = [S.sb("Cbf%d" % i, [128, 4, 260], BF16) for i in range(NB)]
            Hds = [S.sb("Hd%d" % i, [128, 1024]) for i in range(NB)]
            dns = [S.sb("dn%d" % i, [128, 4, 2]) for i in range(NB)]
            sss = [S.sb("ss%d" % i, [128, 8]) for i in range(NB)]
            junks = [S.sb("junk%d" % i, [128, 256], BF16) for i in range(NB)]
            mtoks = [S.sb("mtok%d" % i, [128, 1024], BF16) for i in range(NB)]
            mTts = [S.sb("mTt%d" % i, [128, 8, 128], BF16) for i in range(NB)]
            for b in range(NB):
                S.op("dve", lambda e, b=b: e.memset(vexts[b][:], 1.0), [], ["vext%d" % b])
            its = [0, 0]
            for d in range(2):
                order = list(range(NCH)) if d == 0 else [1, 0] + list(range(NCH - 1, 1, -1))
                tri = cst[:, 1 + d, :]
                for b in range(NB):
                    S.op("dve", lambda e, b=b: e.memset(Cs[b][:], 0.0), [], [("C%d" % b, hd) for hd in range(4)])
                    S.op("pool", lambda e, b=b: e.memset(Cbfs[b][:], 0.0), [], [("Cbf%d" % b, hd) for hd in range(4)])
                for ci in order:
                    for b in range(NB):
                        B_ = "%d" % b
                        kTb, LFb, Erow, sm, qs, ks, vext, STs = kTbs[b], LFbs[b], Erows[b], sms[b], qss[b], kss[b], vexts[b], STss[b]
                        STf = STfs[b]
                        C, Cbf, Hdt, dn, ss, junk, mtok, mT_ = Cs[b], Cbfs[b], Hds[b], dns[b], sss[b], junks[b], mtoks[b], mTts[b]
                        pbc = P[0][:, b * 8:b * 8 + 8]
                        pbrow = P[1 + b]
                        pstp = P[1 + b]
                        tk0 = ci * 128
                        sl = 2 * b + its[b] % 2; its[b] += 1
                        qT, kT, kt, vt, g_ = qTs[sl], kTs[sl], kts[sl], vts[sl], gs[sl]
                        kq, kk, kkt, kvt, kg = "qT%d" % sl, "kT%d" % sl, "ktok%d" % sl, "vtok%d" % sl, "g%d" % sl
                        S.load(qT[:], QT[b, :, :, tk0:tk0 + 128], xkeys("QT", b, tk0, tk0 + 128), [kq])
                        S.load(kT[:], KT[b, :, :, tk0:tk0 + 128], xkeys("KT", b, tk0, tk0 + 128), [kk])
                        S.load(kt[:], KTOK[b, ci], [("KTOK", b, ci)], [kkt])
                        S.load(vt[:], VTOK[b, ci], [("VTOK", b, ci)], [kvt])
                        S.load(g_[:], GT[b, ci], [("GT", b, ci)], [kg])
                        if d == 1:
                            hf, ot = hfs[sl], ots[sl]
                            khf, kot = "hf%d" % sl, "ot%d" % sl
                            S.load(hf[:], HF[b, ci], [("HF", b, ci)], [khf])
                            S.load(ot[:], OTOK[b, ci], [("OTOK", b, ci)], [kot])
                        ig = g_[:, d * 8:d * 8 + 4]
                        lf = g_[:, d * 8 + 4:d * 8 + 8]
                        S.mm(pbc[:, 0:4], tri, lf, True, True, ["cst", kg], ["bc"])
                        S.mm(pbc[:, 4:8], cst[:, 3, :], lf, True, True, ["cst", kg], ["bc"])
                        S.op("dve", lambda e, lf=lf, LFb=LFb: e.tensor_copy(out=LFb[:], in_=lf.rearrange("p (h n) -> p h n", n=1).to_broadcast([128, 4, 128])),
                             [kg], ["LFb" + B_])
                        for hd in range(4):
                            S.mm(pbrow[:, hd * 128:(hd + 1) * 128], LFb[:, hd, :], tri, True, True, ["LFb" + B_, "cst"], ["bst" + B_])
                        S.act(Erow[:], pbrow[:, :].rearrange("p (h n) -> p h n", h=4), AF.Exp, ["bst" + B_], ["Erow" + B_, "bst" + B_])
                        S.tt(sm[:, 0, :], ig, pbc[:, 0:4], ALU.subtract, [kg, "bc"], ["sm0" + B_, "bc"])
                        S.act(sm[:, 1, :], sm[:, 0, :], AF.Exp, ["sm0" + B_], ["ek" + B_])
                        S.act(sm[:, 2, :], pbc[:, 4:8], AF.Exp, ["bc"], ["ebl" + B_, "bc"])
                        S.tt(qs[:], qT[:], Erow[:], ALU.mult, [kq, "Erow" + B_], ["qs" + B_])
                        S.op("pool", lambda e, kT=kT, kTb=kTb: e.tensor_copy(out=kTb[:], in_=kT[:]), [kk], ["kTb" + B_])
                        S.tt(ks[:], kt[:].rearrange("p (h n) -> p h n", h=4),
                             sm[:, 1, :].rearrange("p (h n) -> p h n", n=1).to_broadcast([128, 4, 128]), ALU.mult, [kkt, "ek" + B_], ["ks" + B_])
                        S.op("pool", lambda e, vt=vt, vext=vext: e.tensor_copy(out=vext[:, :, 0:256], in_=vt[:].rearrange("p (h n) -> p h n", h=4)),
                             [kvt], ["vext" + B_])
                        for hd in range(4):
                            S.mm(pstp[:, hd * 128:(hd + 1) * 128], kTb[:, hd, :], qs[:, hd, :], True, True, ["kTb" + B_, "qs" + B_], ["bst" + B_])
                        S.tt(STf[:], pstp[:, :].rearrange("p (h n) -> p h n", h=4),
                             sm[:, 1, :].rearrange("p (h n) -> p h n", n=1).to_broadcast([128, 4, 128]), ALU.mult, ["bst" + B_, "ek" + B_], ["STf" + B_, "bst" + B_])
                        S.tt(STs[:], STf[:], tri.rearrange("p (h n) -> p h n", h=1).to_broadcast([128, 4, 128]), ALU.mult,
                             ["STf" + B_, "cst"], [("STs" + B_, hd) for hd in range(4)])
                        pden = P[0][:, 16 + b * 4:16 + b * 4 + 4]
                        for hd in range(4):
                            S.mm(pden[:, hd:hd + 1], qs[:, hd, :], Cbf[:, hd, 256:257], True, False, ["qs" + B_, ("Cbf" + B_, hd)], ["bc"])
                            S.mm(pden[:, hd:hd + 1], STs[:, hd, :], vext[:, hd, 256:257], False, True, [("STs" + B_, hd), "vext" + B_], ["bc"])
                        S.act(dn[:, :, 0], pden, AF.Abs, ["bc"], ["dn" + B_, "bc"])
                        S.ts(dn[:, :, 0], dn[:, :, 0], 1.0, None, ALU.max, None, ["dn" + B_], ["dn" + B_])
                        S.op("dve", lambda e, dn=dn: e.reciprocal(out=dn[:, :, 1], in_=dn[:, :, 0]), ["dn" + B_], ["dn" + B_])
                        khd = "Hd" + B_
                        for hd in range(4):
                            nh = P[3 + hd % 2]; knh = "nh%d" % (hd % 2)
                            up = P[5 + hd % 2]; kup = "up%d" % (hd % 2)
                            S.mm(nh[:, 0:257], qs[:, hd, :], Cbf[:, hd, 0:257], True, False, ["qs" + B_, ("Cbf" + B_, hd)], [knh])
                            S.mm(nh[:, 0:257], STs[:, hd, :], vext[:, hd, 0:257], False, True, [("STs" + B_, hd), "vext" + B_], [knh])
                            S.mm(up[:, 0:257], ks[:, hd, :], vext[:, hd, 0:257], True, True, ["ks" + B_, "vext" + B_], [kup])
                            S.ts(Hdt[:, hd * 256:(hd + 1) * 256], nh[:, 0:256], dn[:, hd, 1:2], None, ALU.mult, None, [knh, "dn" + B_], [(khd, hd)])
                            S.tt(C[:, hd, 0:257], C[:, hd, 0:257], up[:, 0:257], ALU.add, [("C" + B_, hd), kup], [("C" + B_, hd)])
                            S.act(C[:, hd, 0:257], C[:, hd, 0:257], AF.Copy, [("C" + B_, hd), "ebl" + B_], [("C" + B_, hd)], scale=sm[:, 2, hd:hd + 1])
                            S.op("pool", lambda e, hd=hd, C=C, Cbf=Cbf: e.tensor_copy(out=Cbf[:, hd, 0:257], in_=C[:, hd, 0:257]), [("C" + B_, hd)], [("Cbf" + B_, hd)])
                        hkeys = [(khd, hd) for hd in range(4)]
                        if d == 0:
                            S.dma("pool", lambda e, Hdt=Hdt, b=b, ci=ci: e.dma_start(out=HF[b, ci], in_=Hdt[:]), hkeys, [("HF", b, ci)])
                        else:
                            S.tt(Hdt[:], Hdt[:], hf[:], ALU.add, hkeys + [khf], hkeys)
                            for hd in range(4):
                                S.act(junk[:], Hdt[:, hd * 256:(hd + 1) * 256], AF.Square, [(khd, hd)], ["junk" + B_], accum_out=ss[:, hd:hd + 1])
                            S.act(ss[:, 4:8], ss[:, 0:4], AF.Sqrt, ["junk" + B_], ["ss4" + B_], bias=EPS, scale=1.0 / 256)
                            S.op("dve", lambda e, ss=ss: e.reciprocal(out=ss[:, 4:8], in_=ss[:, 4:8]), ["ss4" + B_], ["ss4" + B_])
                            S.tt(Hdt[:].rearrange("p (h n) -> p h n", h=4), Hdt[:].rearrange("p (h n) -> p h n", h=4),
                                 ss[:, 4:8].rearrange("p (h n) -> p h n", n=1).to_broadcast([128, 4, 256]), ALU.mult, hkeys + ["ss4" + B_], hkeys)
                            S.tt(Hdt[:], Hdt[:], ngt[:], ALU.mult, hkeys + ["ngt"], hkeys)
                            S.tt(mtok[:], Hdt[:], ot[:], ALU.mult, hkeys + [kot], ["mtok" + B_])
                            kmt = "mTt" + B_
                            for half in range(2):
                                for c4 in range(4):
                                    c = half * 4 + c4
                                    S.mm(P[7][:, c4 * 128:(c4 + 1) * 128], mtok[:, c * 128:(c + 1) * 128], ident_bf[:], True, True,
                                         ["mtok" + B_, "ident_bf"], ["trp"])
                                S.act(mT_[:, half * 4:half * 4 + 4, :], P[7][:, :].rearrange("p (c n) -> p c n", c=4), AF.Copy, ["trp"], [kmt])
                            S.dma("pool", lambda e, mT_=mT_, b=b, tk0=tk0: e.dma_start(out=MT[b, :, :, tk0:tk0 + 128], in_=mT_[:]),
                                  [kmt], xkeys("MT", b, tk0, tk0 + 128))
            S.end_phase()
        with contextlib.ExitStack() as pst:
            S.stack = pst
            w_out = S.sb("ml_w_out", [128, 8, 1024], BF16)
            for k in range(8):
                S.load(w_out[:, k, :], I["ml_out"][:, k, :], [], ["w_out"], q="pool")
            self.out_proj(l, src, dst, MT, "MT", w_out, 8, tiles)
            S.end_phase()

    def phase_final(self, cur):
        S = self.S
        I = self.I
        with contextlib.ExitStack() as pst:
            S.stack = pst
            fg = S.sb("fg", [128, 8])
            S.load(fg[:], I["final_g"], [], ["fg"])
            xts = [S.sb("xt%d" % i, [128, 8, 512]) for i in range(2)]
            sq = S.sb("sq", [128, 8, 512], BF16)
            xn = S.sb("xn", [128, 8, 512])
            rstd = S.sb("rstd", [128, 512])
            xo = [S.sb("xo%d" % i, [128, 8, 512]) for i in range(2)]
            it = 0
            for b in range(NB):
                for i in range(LT // 512):
                    t0 = CT + i * 512
                    n = 512
                    xt = xts[it % 2]; kx = "xt%d" % (it % 2)
                    xot = xo[it % 2]; kxo = "xo%d" % (it % 2)
                    it += 1
                    S.load(xt[:], cur[1][b, :, :, t0:t0 + n], xkeys(cur[0], b, t0, t0 + n), [kx])
                    ssps = self.psum[it % 2]; kss = "ssps%d" % (it % 2)
                    S.act(sq[:], xt[:], AF.Square, [kx], ["sq"])
                    for c in range(8):
                        S.mm(ssps[:, :n], self.ones_bf[:], sq[:, c, :n], c == 0, c == 7, ["sq", "ones_bf"], [kss])
                    S.act(rstd[:], ssps[:, :n], AF.Sqrt, [kss], ["rstd"], bias=EPS, scale=1.0 / D)
                    S.op("dve", lambda e: e.reciprocal(out=rstd[:], in_=rstd[:]), ["rstd"], ["rstd"])
                    S.tt(xn[:], xt[:], rstd[:].rearrange("p (c n) -> p c n", c=1).to_broadcast([128, 8, n]), ALU.mult, [kx, "rstd"], ["xn"])
                    S.tt(xot[:], xn[:], fg[:].rearrange("p (c n) -> p c n", n=1).to_broadcast([128, 8, n]), ALU.mult, ["xn", "fg"], [kxo])
                    S.dma("pool", lambda e, xot=xot, b=b, i=i: e.dma_start(out=self.outT[b, :, :, i * 512:(i + 1) * 512], in_=xot[:]),
                          [kxo], [("out", b, i)])
            S.end_phase()


INPUT_SHAPES = {
    "xin": (NB, 128, 8, T),
    "cond": (128, 8, 4),
    "mod_w": (4, 128, 8, 6144),
    "mod_b": (128, 4, 48),
    "norm_g": (128, 4, 2, 8),
    "final_g": (128, 8),
    "ffn_up": (4, NF, 128, 2, 8, 128),
    "ffn_down": (4, 128, NF, 1024),
    "ffn_cw": (128, 4, 42, 9),
    "ffn_cb": (128, 4, 42),
    "cm_in": (2, 128, 8, 4096),
    "cm_out": (2, 128, 16, 1024),
    "cm_wsT": (2, 128, 8, 128),
    "cm_bin_u": (128, 2, 16),
    "cm_rep": (2, 3, 128, 2048),
    "cm_vgb": (128, 2, 2, 16),
    "cm_bs": (2, 128, 8, 128),
    "lru_in": (128, 8, 2048),
    "lru_out": (128, 8, 1024),
    "lru_gw": (128, 2, 2, 4, 2, 256),
    "lru_vec": (128, 8, 11),
    "ml_in": (128, 8, 3088),
    "ml_out": (128, 8, 1024),
    "ml_bg": (128, 16),
    "ml_ng": (128, 1024),
    "ml_cst": (128, 4, 128),
}


def fm(v, nch):
    v = np.asarray(v, np.float32)
    lead = v.shape[:-1]
    a = v.reshape(lead + (nch, 128))
    a = np.moveaxis(a, -1, 0)
    return np.ascontiguousarray(a)


def layout_shared(inp):
    f32 = np.float32
    W = {}
    W["mod_w"] = np.ascontiguousarray(inp["mod_w"].reshape(4, 8, 128, 6144).transpose(0, 2, 1, 3))
    W["mod_b"] = fm(inp["mod_b"], 48)
    W["norm_g"] = np.ascontiguousarray(np.stack([fm(inp["norm1_g"], 8), fm(inp["norm2_g"], 8)], axis=2))
    W["final_g"] = fm(inp["final_norm_g"], 8)
    up = inp["ffn_w_up"].reshape(4, 8, 128, 2, NF, 128)
    W["ffn_up"] = np.ascontiguousarray(up.transpose(0, 4, 2, 3, 1, 5))
    W["ffn_down"] = np.ascontiguousarray(inp["ffn_w_down"].reshape(4, NF, 128, 1024).transpose(0, 2, 1, 3))
    cw = inp["ffn_conv_w"].reshape(4, 9, 42, 128)
    W["ffn_cw"] = np.ascontiguousarray(cw.transpose(3, 0, 2, 1))
    W["ffn_cb"] = fm(inp["ffn_conv_b"], 42)
    W["cm_in"] = np.ascontiguousarray(inp["cm_w_in"].reshape(2, 8, 128, 4096).transpose(0, 2, 1, 3))
    W["cm_out"] = np.ascontiguousarray(inp["cm_w_out"].reshape(2, 16, 128, 1024).transpose(0, 2, 1, 3))
    W["cm_wsT"] = np.ascontiguousarray(inp["cm_w_s"].transpose(0, 3, 1, 2))
    W["cm_bin_u"] = fm(inp["cm_b_in"][:, :2048], 16)
    W["cm_vgb"] = np.ascontiguousarray(np.stack([fm(inp["cm_v_g"], 16), fm(inp["cm_v_b"], 16)], axis=2))
    rep = np.stack([inp["cm_b_in"][:, 2048:], inp["cm_v_g"], inp["cm_v_b"]], axis=1)
    W["cm_rep"] = np.ascontiguousarray(np.broadcast_to(rep[:, :, None, :], (2, 3, 128, 2048)))
    bs = np.tile(inp["cm_b_s"][:, None, :, :], (1, 128, 1, 1))
    W["cm_bs"] = np.ascontiguousarray(bs)
    W["lru_in"] = np.ascontiguousarray(inp["lru_w_in"][0].reshape(8, 128, 2048).transpose(1, 0, 2))
    W["lru_out"] = np.ascontiguousarray(inp["lru_w_out"][0].reshape(8, 128, 1024).transpose(1, 0, 2))
    gw = np.stack([inp["lru_w_rg"][0], inp["lru_w_ig"][0]], axis=1)
    gw = gw.reshape(2, 2, 4, 2, 128, 256)
    W["lru_gw"] = np.ascontiguousarray(gw.transpose(4, 0, 1, 2, 3, 5))
    vec = np.concatenate([fm(inp["lru_conv_w"][0], 8).transpose(0, 2, 1),
                          fm(inp["lru_conv_b"][0], 8)[:, :, None],
                          fm(inp["lru_b_rg"][0], 8).transpose(0, 2, 1),
                          fm(inp["lru_b_ig"][0], 8).transpose(0, 2, 1),
                          fm(inp["lru_lambda"][0], 8).transpose(0, 2, 1)], axis=2)
    W["lru_vec"] = np.ascontiguousarray(vec)
    W["ml_in"] = np.ascontiguousarray(inp["ml_w_in"][0].reshape(8, 128, 3088).transpose(1, 0, 2))
    W["ml_out"] = np.ascontiguousarray(inp["ml_w_out"][0].reshape(8, 128, 1024).transpose(1, 0, 2))
    W["ml_bg"] = np.ascontiguousarray(np.broadcast_to(inp["ml_b_gate"][0].reshape(1, 16), (128, 16)))
    W["ml_ng"] = np.ascontiguousarray(np.broadcast_to(inp["ml_norm_g"][0].reshape(1, 1024), (128, 1024)))
    r = np.arange(128)
    cst = np.stack([np.eye(128), (r[:, None] <= r[None, :]), (r[:, None] >= r[None, :]), np.ones((128, 128))], axis=1)
    W["ml_cst"] = np.ascontiguousarray(cst.astype(np.float32))
    return {k: np.ascontiguousarray(v, dtype=f32) for k, v in W.items()}


def layout_core(inp, i):
    b0 = NB * i
    seq = np.concatenate([inp["ctx"][b0:b0 + NB], inp["x"][b0:b0 + NB]], axis=1)
    xin = np.ascontiguousarray(seq.reshape(NB, T, 8, 128).transpose(0, 3, 2, 1))
    cond = np.zeros((4, D), np.float32)
    cond[0:NB] = inp["c"][b0:b0 + NB]
    cond[2] = inp["c_ctx"]
    condT = np.ascontiguousarray(cond.reshape(4, 8, 128).transpose(2, 1, 0))
    return {"xin": xin.astype(np.float32), "cond": condT}


_CACHE = {}


def kernel(**inputs):
    inp = {k: np.asarray(v) for k, v in inputs.items()}
    n_cores = 8
    shared = layout_shared(inp)
    if "nc" not in _CACHE:
        _CACHE["nc"] = Prog().build()
    nc = _CACHE["nc"]
    in_maps = []
    for i in range(n_cores):
        m = dict(shared)
        m.update(layout_core(inp, i))
        in_maps.append(m)
    res = run_bass_kernel_spmd(nc, in_maps, core_ids=list(range(n_cores)))
    outs = []
    for i in range(n_cores):
        oT = res.results[i]["outT"]
        outs.append(np.ascontiguousarray(oT.transpose(0, 3, 2, 1)).reshape(NB, LT, D))
    return np.concatenate(outs, axis=0).astype(np.float32)
```

```python
import contextlib
import os
import numpy as np
import concourse.bass as bass
import concourse.mybir as mybir
from concourse.bass_utils import run_bass_kernel_spmd

F32 = mybir.dt.float32
BF16 = mybir.dt.bfloat16
AF = mybir.ActivationFunctionType
ALU = mybir.AluOpType

D = 1024
CT = 256
LT = 4096
T = CT + LT
NB = 2
FH = 2688
NF = 21
EPS = 1e-6
ENGS = ["pe", "act", "dve", "pool", "sp"]
HANDLE = {"pe": "tensor", "act": "scalar", "dve": "vector", "pool": "gpsimd", "sp": "sync"}


class Sched:
    def __init__(self, nc, gstack, n_dma_sems=16):
        self.nc = nc
        self.stack = gstack
        self.ops = {e: [] for e in ENGS}
        self.sem = {e: gstack.enter_context(nc.semaphore("s_" + e)) for e in ENGS}
        self.count = {e: 0 for e in ENGS}
        self.waited = {e: {} for e in ENGS}
        self.same_sync = {"act", "dve", "pool"}
        self.last_writer = {}
        self.readers = {}
        self.dma_sems = {}
        self.dma_rr = {}
        for q in ("sp", "pool"):
            self.dma_sems[q] = [[gstack.enter_context(nc.semaphore("d_%s%d" % (q, i))), 0] for i in range(n_dma_sems)]
            self.dma_rr[q] = 0
        self.n_ops = 0

    def sb(self, name, shape, dt=F32):
        self.n_sb = getattr(self, "n_sb", 0) + 1
        return self.stack.enter_context(self.nc.sbuf_tensor("sb%d_%s" % (self.n_sb, name), shape, dt))

    def _deps(self, reads, writes):
        deps = []
        for k in reads:
            w = self.last_writer.get(k)
            if w is not None:
                deps.append(w)
        for k in writes:
            w = self.last_writer.get(k)
            if w is not None:
                deps.append(w)
            deps.extend(self.readers.get(k, {}).values())
        return deps

    def _record(self, tok, reads, writes):
        for k in reads:
            self.readers.setdefault(k, {})[id(tok[0])] = tok
        for k in writes:
            self.last_writer[k] = tok
            self.readers[k] = {}

    def _waits(self, eng, deps):
        waits = []
        wd = self.waited[eng]
        own = self.sem[eng]
        for (sem, val) in deps:
            if sem is own and eng not in self.same_sync:
                continue
            if wd.get(id(sem), 0) < val:
                wd[id(sem)] = val
                waits.append((sem, val))
        return waits

    def op(self, eng, fn, reads=(), writes=()):
        deps = self._deps(reads, writes)
        waits = self._waits(eng, deps)
        self.count[eng] += 1
        tok = (self.sem[eng], self.count[eng])
        self.ops[eng].append((fn, waits, (self.sem[eng], 1)))
        self._record(tok, reads, writes)
        self.n_ops += 1
        return tok

    def dma(self, q, fn, reads=(), writes=()):
        deps = self._deps(reads, writes)
        slot = self.dma_sems[q][self.dma_rr[q]]
        self.dma_rr[q] = (self.dma_rr[q] + 1) % len(self.dma_sems[q])
        if slot[1] > 0:
            deps.append((slot[0], slot[1]))
        waits = self._waits(q, deps)
        slot[1] += 16
        tok = (slot[0], slot[1])
        self.ops[q].append((fn, waits, (slot[0], 16)))
        self._record(tok, reads, writes)
        self.n_ops += 1
        return tok

    def end_phase(self):
        toks = []
        for q in self.dma_sems:
            for sem, val in self.dma_sems[q]:
                if val > 0:
                    toks.append((sem, val))
        for e in ENGS:
            if self.count[e] > 0:
                toks.append((self.sem[e], self.count[e]))
        for e in ENGS:
            waits = self._waits(e, list(toks))
            self.ops[e].append((None, waits, None))
        nc = self.nc
        with nc.Block() as block:
            for e in ENGS:
                ops = self.ops[e]

                def body(eng, ops=ops):
                    for fn, waits, inc in ops:
                        for (sem, val) in waits:
                            eng.wait_ge(sem, val)
                        if fn is not None:
                            inst = fn(eng)
                            inst.then_inc(inc[0], inc[1])

                getattr(block, HANDLE[e])(body)
        self.ops = {e: [] for e in ENGS}
        self.last_writer = {}
        self.readers = {}

    def mm(self, out, lhsT, rhs, start, stop, reads, writes):
        return self.op("pe", lambda e: e.matmul(out, lhsT=lhsT, rhs=rhs, start=start, stop=stop), reads, writes)

    def act(self, out, in_, func, reads, writes, bias=0.0, scale=1.0, accum_out=None):
        if accum_out is None:
            return self.op("act", lambda e: e.activation(out=out, in_=in_, func=func, bias=bias, scale=scale), reads, writes)
        return self.op("act", lambda e: e.activation(out=out, in_=in_, func=func, bias=bias, scale=scale, accum_out=accum_out), reads, writes)

    def tt(self, out, in0, in1, op, reads, writes, eng="dve"):
        return self.op(eng, lambda e: e.tensor_tensor(out=out, in0=in0, in1=in1, op=op), reads, writes)

    def ts(self, out, in0, s1, s2, op0, op1, reads, writes, eng="dve"):
        if s2 is None:
            return self.op(eng, lambda e: e.tensor_scalar(out=out, in0=in0, scalar1=s1, scalar2=None, op0=op0), reads, writes)
        return self.op(eng, lambda e: e.tensor_scalar(out=out, in0=in0, scalar1=s1, scalar2=s2, op0=op0, op1=op1), reads, writes)

    def stt(self, out, in0, scalar, in1, op0, op1, reads, writes, accum_out=None):
        if accum_out is None:
            return self.op("dve", lambda e: e.scalar_tensor_tensor(out=out, in0=in0, scalar=scalar, in1=in1, op0=op0, op1=op1), reads, writes)
        return self.op("dve", lambda e: e.scalar_tensor_tensor(out=out, in0=in0, scalar=scalar, in1=in1, op0=op0, op1=op1, accum_out=accum_out), reads, writes)

    def load(self, out, in_, reads, writes, q="sp"):
        return self.dma(q, lambda e: e.dma_start(out=out, in_=in_), reads, writes)


def xkeys(name, b, t0, t1):
    return [(name, b, i) for i in range(t0 // 64, (t1 + 63) // 64)]


class Prog:
    def __init__(self, n_layers=4, dbg=None):
        self.n_layers = n_layers
        self.dbg = dbg

    def dram_in(self, name, shape, dt=F32):
        return self.nc.dram_tensor(name, list(shape), dt, kind="ExternalInput").ap()

    def dram_tmp(self, name, shape, dt=F32):
        return self.nc.dram_tensor(name, list(shape), dt, kind="Internal").ap()

    def build(self):
        nc = bass.Bass("TRN2", target_bir_lowering=False)
        self.nc = nc
        I = {}
        for name, shape in INPUT_SHAPES.items():
            I[name] = self.dram_in(name, shape)
        self.I = I
        self.outT = nc.dram_tensor("outT", [NB, 128, 8, LT], F32, kind="ExternalOutput").ap()
        self.XA = self.dram_tmp("XA", [NB, 128, 8, T])
        self.XB = self.dram_tmp("XB", [NB, 128, 8, T])
        self.up_bf = self.dram_tmp("up_bf", [4, NF, 128, 2, 8, 128], BF16)
        self.DG = self.dram_tmp("DGs", [NF, 128, 2, 9, 128], BF16)
        self.XR = self.dram_tmp("XRs", [NB, 128, 8, T])
        self.YG = self.dram_tmp("YGs", [NB, 128, 8, T])
        self.MT = self.dram_tmp("MTs", [NB, 128, 8, T], BF16)
        NCH = T // 128
        self.QT = self.dram_tmp("QTs", [NB, 128, 4, T])
        self.KT = self.dram_tmp("KTs", [NB, 128, 4, T])
        self.KTOK = self.dram_tmp("KTOKs", [NB, NCH, 128, 512])
        self.VTOK = self.dram_tmp("VTOKs", [NB, NCH, 128, 1024])
        self.OTOK = self.dram_tmp("OTOKs", [NB, NCH, 128, 1024])
        self.GT = self.dram_tmp("GTs", [NB, NCH, 128, 16])
        self.HF = self.dram_tmp("HFs", [NB, NCH, 128, 1024])
        if self.dbg is not None:
            self.dbgX = nc.dram_tensor("dbgX", [NB, 128, 8, T], F32, kind="ExternalOutput").ap()
        with contextlib.ExitStack() as gst:
            S = Sched(nc, gst)
            self.S = S
            self.ones_bf = S.sb("ones_bf", [128, 128], BF16)
            self.MOD = S.sb("MOD", [128, 4, 6, 8, 4])
            self.psum = [gst.enter_context(nc.psum_tensor("ps%d" % i, [128, 512], F32)) for i in range(8)]
            S.op("dve", lambda e: e.memset(self.ones_bf[:], 1.0), writes=["ones_bf"])
            self.phase_setup()
            cur = ("xin", I["xin"])
            bufs = [("XA", self.XA), ("XB", self.XB)]
            bi = 0
            for l in range(self.n_layers):
                last = l == 3
                kind = l % 3
                dst = bufs[bi]; bi ^= 1
                if kind == 0:
                    self.phase_cm(l, cur, dst, last)
                elif kind == 1:
                    self.phase_lru(l, cur, dst)
                else:
                    self.phase_mlstm(l, cur, dst)
                cur = dst
                if self.dbg == ("mix", l):
                    self.phase_dump(cur)
                dst = bufs[bi]; bi ^= 1
                self.phase_ffn(l, cur, dst, last)
                cur = dst
                if self.dbg == ("ffn", l):
                    self.phase_dump(cur)
            self.phase_final(cur)
        return nc

    def phase_dump(self, cur):
        S = self.S
        for b in range(NB):
            S.dma("sp", lambda e, b=b: e.dma_start(out=self.dbgX[b], in_=cur[1][b]), reads=xkeys(cur[0], b, 0, T), writes=[("dbg", b)])
        S.end_phase()

    def cast_up(self, l):
        S = self.S
        src = self.I["ffn_up"][l].rearrange("f p two k m -> (f p) (two k m)")
        dst = self.up_bf[l].rearrange("f p two k m -> (f p) (two k m)")
        rows = NF * 128
        RSTEP = 384
        for r0 in range(0, rows, RSTEP):
            S.dma("pool", lambda e, r0=r0: e.dma_start(out=dst[r0:r0 + RSTEP, :], in_=src[r0:r0 + RSTEP, :]), writes=[("up_bf", l)])

    def phase_setup(self):
        S = self.S
        I = self.I
        with contextlib.ExitStack() as pst:
            S.stack = pst
            cond = S.sb("cond", [128, 8, 4])
            condr = S.sb("condr", [128, 8, 4])
            modb = S.sb("modb", [128, 4, 48])
            ng = S.sb("ng", [128, 4, 2, 8])
            raw = S.sb("raw", [128, 48, 4])
            wbuf = [S.sb("modw%d" % i, [128, 8, 512]) for i in range(2)]
            S.load(condr[:], I["cond"], [], ["condr"])
            S.load(modb[:], I["mod_b"], [], ["modb"])
            S.load(ng[:], I["norm_g"], [], ["ng"])
            S.act(cond[:], condr[:], AF.Silu, ["condr"], ["cond"])
            mps = self.psum[0]
            mpsv = mps[:, 0:192].rearrange("p (c n) -> p c n", n=4)
            it = 0
            for l in range(4):
                for g in range(12):
                    wb = wbuf[it % 2]; wk = "modw%d" % (it % 2); it += 1
                    S.load(wb[:], I["mod_w"][l, :, :, g * 512:(g + 1) * 512], [], [wk])
                    for m in range(4):
                        ch = g * 4 + m
                        for k in range(8):
                            S.mm(mpsv[:, ch, :], wb[:, k, m * 128:(m + 1) * 128], cond[:, k, :], k == 0, k == 7,
                                 [wk, "cond"], ["mps"])
                S.tt(raw[:], mpsv, modb[:, l, :].rearrange("p (c n) -> p c n", n=1).to_broadcast([128, 48, 4]), ALU.add,
                     ["mps", "modb"], ["raw"])
                rv = raw[:].rearrange("p (s c) n -> p s c n", s=6)
                M = self.MOD
                for half in range(2):
                    sh, sc, gg = rv[:, 3 * half + 0], rv[:, 3 * half + 1], rv[:, 3 * half + 2]
                    gb = ng[:, l, half, :].rearrange("p (c n) -> p c n", n=1).to_broadcast([128, 8, 4])
                    S.stt(M[:, l, 3 * half + 0], sc, 1.0, gb, ALU.add, ALU.mult, ["raw", "ng"], ["MOD"])
                    S.op("dve", lambda e, o=M[:, l, 3 * half + 1], i=sh: e.tensor_copy(out=o, in_=i), ["raw"], ["MOD"])
                    S.op("dve", lambda e, o=M[:, l, 3 * half + 2], i=gg: e.tensor_copy(out=o, in_=i), ["raw"], ["MOD"])
            S.end_phase()

    def norm_mod(self, xt, n, A, B, hT, sq, xn, rstd, ssps, kx, kh, tag, kss=None):
        S = self.S
        kss = kss or ("ssps" + tag)
        S.act(sq[:, :, :n], xt[:, :, :n], AF.Square, [kx], ["sq" + tag])
        for c in range(8):
            S.mm(ssps[:, :n], self.ones_bf[:], sq[:, c, :n], c == 0, c == 7, ["sq" + tag, "ones_bf"], [kss])
        S.act(rstd[:, :n], ssps[:, :n], AF.Sqrt, [kss], ["rstd" + tag], bias=EPS, scale=1.0 / D)
        S.op("dve", lambda e: e.reciprocal(out=rstd[:, :n], in_=rstd[:, :n]), ["rstd" + tag], ["rstd" + tag])
        for c in range(8):
            tmp = xn[c % 2]; kt = "xn%d%s" % (c % 2, tag)
            S.tt(tmp[:, :n], xt[:, c, :n], rstd[:, :n], ALU.mult, [kx, "rstd" + tag], [kt])
            S.act(hT[:, c, :n], tmp[:, :n], AF.Identity, [kt, "MOD"], [kh], bias=B[:, c:c + 1], scale=A[:, c:c + 1])

    def seq_tiles(self, step=512):
        tiles = [(0, CT)]
        for i in range(LT // step):
            tiles.append((CT + i * step, CT + (i + 1) * step))
        return tiles

    def phase_cm(self, l, src, dst, last):
        S = self.S
        I = self.I
        j = l // 3
        M = self.MOD
        P = self.psum
        with contextlib.ExitStack() as pst:
            S.stack = pst
            NT = 256
            w_in = S.sb("cm_w_in", [128, 8, 4096], BF16)
            w_out = S.sb("cm_w_out", [128, 16, 1024], BF16)
            wsT = S.sb("cm_wsT", [128, 8, 128], BF16)
            bin_u = S.sb("cm_bin_u", [128, 2, 16])
            vgb = S.sb("cm_vgb", [128, 2, 2, 16])
            rep = S.sb("cm_rep", [128, 2048])
            CB = S.sb("cm_CB", [128, 16, 128])
            vt = S.sb("vt", [128, 2048])
            bs = vt[:, 0:1024].rearrange("p (g m) -> p g m", g=8)
            for k in range(8):
                for hh in range(2):
                    S.load(w_in[:, k, hh * 2048:(hh + 1) * 2048], I["cm_in"][j, :, k, hh * 2048:(hh + 1) * 2048], [], ["w_in"], q="pool")
            for c in range(16):
                S.load(w_out[:, c, :], I["cm_out"][j, :, c, :], [], ["w_out"], q="pool")
            S.load(wsT[:], I["cm_wsT"][j], [], ["wsT"], q="pool")
            self.cast_up(l)
            S.load(bin_u[:], I["cm_bin_u"], [], ["bin_u"])
            S.load(vgb[:], I["cm_vgb"], [], ["vgb"])
            S.load(rep[:], I["cm_rep"][j, 0], [], ["rep"])
            S.load(bs, I["cm_bs"][j], [], [("vt", 0), ("vt", 1)])
            for hh in range(2):
                S.mm(P[hh][:, :], self.ones_bf[:], wsT[:, 4 * hh:4 * hh + 4, :], True, True, ["ones_bf", "wsT"], ["pv%d" % hh])
            for c in range(16):
                g = c // 2
                S.stt(CB[:, c, :], P[g // 4][:, (g % 4) * 128:(g % 4 + 1) * 128], vgb[:, j, 1, c:c + 1], bs[:, g, :], ALU.mult, ALU.add,
                      ["pv%d" % (g // 4), "vgb", ("vt", 0), ("vt", 1)], ["CB"])
            xts = [S.sb("xt%d" % i, [128, 8, NT]) for i in range(1)]
            xos = [S.sb("xo%d" % i, [128, 8, NT]) for i in range(2)]
            hTs = [S.sb("hT%d" % i, [128, 8, NT], BF16) for i in range(2)]
            vns = [[S.sb("vn%d_%d" % (i, q), [128, 2048], BF16) for q in range(2)] for i in range(2)]
            mTs = [S.sb("mT%d" % i, [128, 16, NT], BF16) for i in range(2)]
            sq = S.sb("sq", [128, 8, NT], BF16)
            xn = [S.sb("xn%d" % i, [128, NT]) for i in range(2)]
            rstd = S.sb("rstd", [128, NT])
            vxs = [S.sb("vx%d" % i, [128, 512]) for i in range(2)]
            junk = S.sb("junk", [128, 1024], BF16)
            st = S.sb("st", [128, 16])
            ux = [S.sb("ux%d" % i, [128, NT]) for i in range(2)]
            sx = [S.sb("sx%d" % i, [128, NT]) for i in range(2)]
            tiles = [t for t in self.seq_tiles(NT) if not (t[0] < CT and last)]
            cnt = {"c": 0, "o": 0}

            def stN(s_, b, t0, t1):
                mi = 2 if t0 < CT else b
                n = t1 - t0
                xt = xts[0]; kx = "xt0"
                S.load(xt[:, :, :n], src[1][b, :, :, t0:t1], xkeys(src[0], b, t0, t1), [kx])
                self.norm_mod(xt, n, M[:, l, 0, :, mi], M[:, l, 1, :, mi], hTs[s_], sq, xn, rstd, P[0], kx, "hT%d" % s_, "", kss="pv0")

            def stV(s_, b, t0, t1):
                n = t1 - t0
                hTi = hTs[s_]; kh = "hT%d" % s_
                S.load(xos[s_][:, :, :n], src[1][b, :, :, t0:t1], xkeys(src[0], b, t0, t1), ["xo%d" % s_])
                for q in range(n // 128):
                    for cg in range(4):
                        pv = P[cg % 2]; kpv = "pv%d" % (cg % 2)
                        for k in range(8):
                            S.mm(pv[:, :], hTi[:, k, q * 128:(q + 1) * 128], w_in[:, k, 2048 + cg * 512:2048 + (cg + 1) * 512],
                                 k == 0, k == 7, [kh, "w_in"], [kpv])
                        S.tt(vxs[cg % 2][:], pv[:, :], rep[:, cg * 512:(cg + 1) * 512], ALU.add, [kpv, "rep"], [("vx", cg % 2)])
                        S.act(vt[:, cg * 512:(cg + 1) * 512], vxs[cg % 2][:], AF.Gelu_apprx_tanh,
                              [("vx", cg % 2)], [("vt", cg)], accum_out=st[:, cg:cg + 1])
                    vtk = [("vt", c_) for c_ in range(4)]
                    for hh in range(2):
                        S.act(junk[:], vt[:, hh * 1024:(hh + 1) * 1024], AF.Square, vtk, ["junk"], accum_out=st[:, 11 + hh:12 + hh])
                    S.op("dve", lambda e: e.reduce_sum(out=st[:, 5:6], in_=st[:, 0:4], axis=mybir.AxisListType.X), vtk, ["st5"])
                    S.tt(st[:, 4:5], st[:, 11:12], st[:, 12:13], ALU.add, ["junk"], ["st4"])
                    S.ts(st[:, 6:7], st[:, 5:6], 1.0 / 2048, None, ALU.mult, None, ["st5"], ["st6"])
                    S.tt(st[:, 7:8], st[:, 6:7], st[:, 6:7], ALU.mult, ["st6"], ["st7"])
                    S.stt(st[:, 8:9], st[:, 4:5], 1.0 / 2048, st[:, 7:8], ALU.mult, ALU.subtract, ["st4", "st7"], ["st8"])
                    S.act(st[:, 9:10], st[:, 8:9], AF.Sqrt, ["st8"], ["st9"], bias=EPS, scale=1.0)
                    S.op("dve", lambda e: e.reciprocal(out=st[:, 10:11], in_=st[:, 9:10]), ["st9"], ["st10"])
                    S.ts(vns[s_][q][:], vt[:], st[:, 6:7], st[:, 10:11], ALU.subtract, ALU.mult, vtk + ["st6", "st10"], ["vn%d_%d" % (s_, q)])

            def stU(s_, b, t0, t1):
                n = t1 - t0
                nq = n // 128
                hTi = hTs[s_]; kh = "hT%d" % s_
                mT = mTs[s_]
                for c in range(16):
                    g = c // 2
                    cit = cnt["c"]; cnt["c"] += 1
                    pu = P[2 + cit % 2]; kpu = "pu%d" % (cit % 2)
                    psx = P[4 + cit % 2]; kps = "psx%d" % (cit % 2)
                    uxt = ux[cit % 2]; kux = "ux%d" % (cit % 2)
                    sxt = sx[cit % 2]; ksx = "sx%d" % (cit % 2)
                    for k in range(8):
                        S.mm(pu[:, :n], w_in[:, k, c * 128:(c + 1) * 128], hTi[:, k, :n], k == 0, k == 7, [kh, "w_in"], [kpu])
                    for q in range(nq):
                        S.mm(psx[:, q * 128:(q + 1) * 128], vns[s_][q][:, c * 128:(c + 1) * 128], wsT[:, g, :], True, True,
                             ["vn%d_%d" % (s_, q), "wsT"], [kps])
                    S.act(uxt[:, :n], pu[:, :n], AF.Gelu_apprx_tanh, [kpu, "bin_u"], [kux], bias=bin_u[:, j, c:c + 1])
                    S.stt(sxt[:, :n].rearrange("p (q m) -> p q m", m=128), psx[:, :n].rearrange("p (q m) -> p q m", m=128),
                          vgb[:, j, 0, c:c + 1], CB[:, c, :].rearrange("p (q m) -> p q m", q=1).to_broadcast([128, nq, 128]),
                          ALU.mult, ALU.add, [kps, "vgb", "CB"], [ksx])
                    S.tt(mT[:, c, :n], sxt[:, :n], uxt[:, :n], ALU.mult, [ksx, kux], [("mT%d" % s_, c)])

            def stO(s_, b, t0, t1):
                mi = 2 if t0 < CT else b
                n = t1 - t0
                mT = mTs[s_]
                xo = xos[s_]; kxo = "xo%d" % s_
                for d in range(8):
                    oi = cnt["o"]; cnt["o"] += 1
                    po = P[6 + oi % 2]; kpo = "po%d" % (oi % 2)
                    for c in range(16):
                        S.mm(po[:, :n], w_out[:, c, d * 128:(d + 1) * 128], mT[:, c, :n], c == 0, c == 15, [("mT%d" % s_, c), "w_out"], [kpo])
                    S.stt(xo[:, d, :n], po[:, :n], M[:, l, 2, d, mi:mi + 1], xo[:, d, :n], ALU.mult, ALU.add, [kpo, kxo, "MOD"], [kxo])
                S.dma("pool", lambda e, xo=xo, b=b, t0=t0, t1=t1, n=n: e.dma_start(out=dst[1][b, :, :, t0:t1], in_=xo[:, :, :n]),
                      [kxo], xkeys(dst[0], b, t0, t1))

            for s_ in range(NB):
                stN(s_, s_, *tiles[0])
            for i, (t0, t1) in enumerate(tiles):
                for s_ in range(NB):
                    stV(s_, s_, t0, t1)
                for s_ in range(NB):
                    stU(s_, s_, t0, t1)
                if i + 1 < len(tiles):
                    for s_ in range(NB):
                        stN(s_, s_, *tiles[i + 1])
                for s_ in range(NB):
                    stO(s_, s_, t0, t1)
            S.end_phase()

    def ffn_tiles(self):
        out = []
        out.append((1, CT, 0, [(0, 1, 0, 1)]))
        lat = []
        R = 6
        r = 0
        c_prev = 0
        while r < 64:
            r1 = min(r + R, 64)
            c1 = min(r1 + 1, 64)
            lat.append((c_prev, c1, r, r1))
            c_prev = c1
            r = r1
        out.append((64, 64, CT, lat))
        return out

    def phase_ffn(self, l, src, dst, last):
        S = self.S
        I = self.I
        M = self.MOD
        DG = self.DG
        with contextlib.ExitStack() as pst:
            S.stack = pst
            w_down = S.sb("w_down", [128, NF, 1024], BF16)
            cw = S.sb("cw", [128, 42, 9])
            cb = S.sb("cb", [128, 42])
            ident = S.sb("ident", [128, 128])
            for f in range(NF):
                S.load(w_down[:, f, :], I["ffn_down"][l, :, f, :], [], ["w_down"], q="pool")
            S.load(cw[:], I["ffn_cw"][:, l], [], ["cw"])
            S.load(cb[:], I["ffn_cb"][:, l], [], ["cb"])
            S.load(ident[:], I["ml_cst"][:, 0, :], [], ["ident"])
            dgst = [S.sb("dgst%d" % i, [128, 9, 128], BF16) for i in range(3)]
            for ch in range(42):
                h, f = ch // NF, ch % NF
                t_ = dgst[ch % 3]; kt = "dgst%d" % (ch % 3)
                S.tt(t_[:], ident[:].rearrange("p (t m) -> p t m", t=1).to_broadcast([128, 9, 128]),
                     cw[:, ch, :].rearrange("p (t m) -> p t m", m=1).to_broadcast([128, 9, 128]), ALU.mult, ["ident", "cw"], [kt])
                S.dma("sp", lambda e, t_=t_, f=f, h=h: e.dma_start(out=DG[f, :, h], in_=t_[:]), [kt], [("DG", f)])
            xts = [S.sb("xt%d" % i, [128, 8, 448]) for i in range(2)]
            xos = [S.sb("xo%d" % i, [128, 8, 384]) for i in range(1)]
            sq = S.sb("sq", [128, 8, 448], BF16)
            xn = [S.sb("xn%d" % i, [128, 448]) for i in range(2)]
            rstd = S.sb("rstd", [128, 448])
            hTs = [S.sb("hT%d" % i, [128, 8, 448], BF16) for i in range(2)]
            wups = [S.sb("wup%d" % i, [128, 2, 8, 128], BF16) for i in range(3)]
            dgs = [S.sb("dg%d" % i, [128, 2, 9, 128], BF16) for i in range(3)]
            zbs = [S.sb("zb%d" % i, [128, 2, 768], BF16) for i in range(2)]
            ZS = S.sb("ZS", [128, NF, 2, 128], BF16)
            sg = [S.sb("sg%d" % i, [128, 384]) for i in range(2)]
            aTs = [S.sb("aT%d" % i, [128, NF, 384], BF16) for i in range(2)]
            P = self.psum
            it = 0
            wit = 0
            fit = 0
            for b in range(NB):
                for (H, W, s0, tl) in self.ffn_tiles():
                    isctx = s0 == 0
                    if isctx and last:
                        continue
                    mi = 2 if isctx else b
                    for (c0, c1, or0, or1) in tl:
                        n = (c1 - c0) * W
                        nout = (or1 - or0) * W
                        t0 = s0 + c0 * W
                        o0 = s0 + or0 * W
                        zr0 = or0 - 1
                        nzr = or1 - or0 + 2
                        xt = xts[it % 2]; kx = "xt%d" % (it % 2)
                        xo = xos[0]; kxo = "xo0"
                        hT = hTs[it % 2]; kh = "hT%d" % (it % 2)
                        aT = aTs[it % 2]; ka = "aT%d" % (it % 2)
                        it += 1
                        S.load(xt[:, :, :n], src[1][b, :, :, t0:t0 + n], xkeys(src[0], b, t0, t0 + n), [kx])
                        S.load(xo[:, :, :nout], src[1][b, :, :, o0:o0 + nout], xkeys(src[0], b, o0, o0 + nout), [kxo])
                        self.norm_mod(xt, n, M[:, l, 3, :, mi], M[:, l, 4, :, mi], hT, sq, xn, rstd, P[0], kx, kh, "", kss="pz0")

                        def up(f):
                            nonlocal wit
                            wu = wups[wit % 3]; kw = "wup%d" % (wit % 3)
                            dg = dgs[wit % 3]; kd = "dg%d" % (wit % 3)
                            wit += 1
                            S.load(wu[:], self.up_bf[l, f], [], [kw])
                            S.load(dg[:], DG[f], [("DG", f)], [kd])
                            zb = zbs[f % 2]; kzb = "zb%d" % (f % 2)
                            if c0 > 0:
                                S.op("pool", lambda e, zb=zb, f=f: e.tensor_copy(out=zb[:, :, 0:2 * W], in_=ZS[:, f, :, :]), [("ZS", f)], [(kzb, 0)])
                            for h in range(2):
                                pz = P[h]; kz = "pz%d" % h
                                for k in range(8):
                                    S.mm(pz[:, :n], wu[:, h, k, :], hT[:, k, :n], k == 0, k == 7, [kw, kh], [kz])
                                zoff = (c0 - zr0) * W
                                if h == 0:
                                    S.act(zb[:, h, zoff:zoff + n], pz[:, :n], AF.Copy, [kz], [(kzb, 1 + h)])
                                else:
                                    S.op("dve", lambda e, o=zb[:, h, zoff:zoff + n], i_=pz[:, :n]: e.tensor_copy(out=o, in_=i_), [kz], [(kzb, 1 + h)])
                            if c1 < H:
                                soff = (c1 - 2 - zr0) * W
                                S.op("pool", lambda e, zb=zb, f=f, soff=soff: e.tensor_copy(out=ZS[:, f, :, :], in_=zb[:, :, soff:soff + 2 * W]),
                                     [(kzb, 0), (kzb, 1), (kzb, 2)], [("ZS", f)])
                            return (zb, kzb, dg, kd)

                        def conv(f, st):
                            nonlocal fit
                            zb, kzb, dg, kd = st
                            pcs = []
                            for h in range(2):
                                pc = P[2 + 2 * h + fit % 2]; kpc = "pc%d_%d" % (h, fit % 2)
                                zv = zb[:, h, 0:nzr * W].rearrange("p (r w) -> p r w", w=W)
                                pv = pc[:, :nout].rearrange("p (r w) -> p r w", w=W)
                                taps = [(0, 0)] + [(dr, dc) for dr in (-1, 0, 1) for dc in (-1, 0, 1) if not (dr == 0 and dc == 0)]
                                todo = []
                                for (dr, dc) in taps:
                                    rlo = max(or0, -dr); rhi = min(or1, H - dr)
                                    clo = max(0, -dc); chi = min(W, W - dc)
                                    if rlo >= rhi or clo >= chi:
                                        continue
                                    todo.append((dr, dc, rlo, rhi, clo, chi))
                                for ti, (dr, dc, rlo, rhi, clo, chi) in enumerate(todo):
                                    tap = (dr + 1) * 3 + (dc + 1)
                                    S.mm(pv[:, rlo - or0:rhi - or0, clo:chi], dg[:, h, tap, :],
                                         zv[:, rlo + dr - zr0:rhi + dr - zr0, clo + dc:chi + dc], ti == 0, ti == len(todo) - 1,
                                         [kd, (kzb, 0), (kzb, 1 + h)], [kpc])
                                pcs.append((pc, kpc))
                            sgt = sg[fit % 2]; ksg = "sg%d" % (fit % 2)
                            S.act(sgt[:, :nout], pcs[0][0][:, :nout], AF.Silu, [pcs[0][1], "cb"], [ksg], bias=cb[:, f:f + 1])
                            S.stt(aT[:, f, :nout], pcs[1][0][:, :nout], cb[:, NF + f:NF + f + 1], sgt[:, :nout], ALU.add, ALU.mult,
                                  [pcs[1][1], "cb", ksg], [(ka, f)])
                            fit += 1

                        prev = None
                        for f in range(NF):
                            st = up(f)
                            if prev is not None:
                                conv(f - 1, prev)
                            prev = st
                        conv(NF - 1, prev)
                        for d in range(8):
                            po = P[6 + d % 2]; kpo = "po%d" % (d % 2)
                            for f in range(NF):
                                S.mm(po[:, :nout], w_down[:, f, d * 128:(d + 1) * 128], aT[:, f, :nout], f == 0, f == NF - 1,
                                     [(ka, f), "w_down"], [kpo])
                            S.stt(xo[:, d, :nout], po[:, :nout], M[:, l, 5, d, mi:mi + 1], xo[:, d, :nout], ALU.mult, ALU.add,
                                  [kpo, kxo, "MOD"], [kxo])
                        S.dma("pool", lambda e, xo=xo, b=b, o0=o0, nout=nout: e.dma_start(out=dst[1][b, :, :, o0:o0 + nout], in_=xo[:, :, :nout]),
                              [kxo], xkeys(dst[0], b, o0, o0 + nout))
            S.end_phase()

    def phase_lru(self, l, src, dst):
        S = self.S
        I = self.I
        M = self.MOD
        P = self.psum
        XR, YG, MT = self.XR, self.YG, self.MT
        tiles = self.seq_tiles(512)
        with contextlib.ExitStack() as pst:
            S.stack = pst
            w_in = S.sb("lru_w_in", [128, 8, 2048], BF16)
            for k in range(8):
                S.load(w_in[:, k, :], I["lru_in"][:, k, :], [], ["w_in"], q="pool")
            self.cast_up(l)
            xts = [S.sb("xt%d" % i, [128, 8, 512]) for i in range(2)]
            sq = S.sb("sq", [128, 8, 512], BF16)
            xn = [S.sb("xn%d" % i, [128, 512]) for i in range(2)]
            rstd = S.sb("rstd", [128, 512])
            hT = S.sb("hT", [128, 8, 512], BF16)
            ygs = [S.sb("yg%d" % i, [128, 8, 512]) for i in range(2)]
            xrs = [S.sb("xr%d" % i, [128, 8, 512]) for i in range(2)]
            it = 0
            pit = 0
            for b in range(NB):
                for (t0, t1) in tiles:
                    mi = 2 if t0 < CT else b
                    n = t1 - t0
                    xt = xts[it % 2]; kx = "xt%d" % (it % 2)
                    ygt = ygs[it % 2]; kyg = "yg%d" % (it % 2)
                    xrt = xrs[it % 2]; kxr = "xr%d" % (it % 2)
                    it += 1
                    S.load(xt[:, :, :n], src[1][b, :, :, t0:t1], xkeys(src[0], b, t0, t1), [kx])
                    self.norm_mod(xt, n, M[:, l, 0, :, mi], M[:, l, 1, :, mi], hT, sq, xn, rstd, P[7], kx, "hT", "")
                    for c in range(16):
                        pp = P[pit % 4]; kp = "pp%d" % (pit % 4); pit += 1
                        for k in range(8):
                            S.mm(pp[:, :n], w_in[:, k, c * 128:(c + 1) * 128], hT[:, k, :n], k == 0, k == 7, ["hT", "w_in"], [kp])
                        if c < 8:
                            S.act(ygt[:, c, :n], pp[:, :n], AF.Gelu_apprx_tanh, [kp], [kyg])
                        else:
                            S.op("dve", lambda e, o=xrt[:, c - 8, :n], i_=pp[:, :n]: e.tensor_copy(out=o, in_=i_), [kp], [kxr])
                    S.dma("pool", lambda e, t=ygt, b=b, t0=t0, t1=t1, n=n: e.dma_start(out=YG[b, :, :, t0:t1], in_=t[:, :, :n]),
                          [kyg], xkeys("YG", b, t0, t1))
                    S.dma("pool", lambda e, t=xrt, b=b, t0=t0, t1=t1, n=n: e.dma_start(out=XR[b, :, :, t0:t1], in_=t[:, :, :n]),
                          [kxr], xkeys("XR", b, t0, t1))
            S.end_phase()
        with contextlib.ExitStack() as pst:
            S.stack = pst
            gw = S.sb("lru_gw", [128, 2, 2, 4, 2, 256], BF16)
            for d in range(2):
                for g in range(2):
                    S.load(gw[:, d, g], I["lru_gw"][:, d, g], [], ["gw"], q="pool")
            vec = S.sb("lru_vec", [128, 8, 11])
            S.load(vec[:], I["lru_vec"], [], ["vec"])
            c8 = S.sb("c8", [128, 8, 2])
            S.act(c8[:], vec[:, :, 9:11], AF.Exp, ["vec"], ["c8"], scale=-1.0)
            S.act(c8[:], c8[:], AF.Ln, ["c8"], ["c8"], bias=1.0)
            S.ts(c8[:], c8[:], -8.0, None, ALU.mult, None, ["c8"], ["c8"])
            xraw = [S.sb("xraw%d" % i, [128, 2, 515]) for i in range(2)]
            XC = S.sb("XC", [128, 2, T])
            XCb = S.sb("XCb", [128, 2, T], BF16)
            A = S.sb("A", [128, T])
            Bt = S.sb("Bt", [128, T])
            Gi = S.sb("Gi", [128, T])
            Hf = S.sb("Hf", [128, T])
            Hb = S.sb("Hb", [128, T])
            Y = S.sb("Y", [128, T])
            mo = S.sb("mo", [128, T], BF16)
            rit = 0
            pit = 0
            for b in range(NB):
                for hd in range(4):
                    for (t0, t1) in tiles:
                        s0, s1 = (0, CT) if t0 < CT else (CT, T)
                        n = t1 - t0
                        lo = max(t0 - 2, s0); hi = min(t1 + 1, s1)
                        xr = xraw[rit % 2]; kr = "xraw%d" % (rit % 2); rit += 1
                        S.op("pool", lambda e, xr=xr: e.memset(xr[:], 0.0), [], [kr])
                        S.load(xr[:, :, lo - (t0 - 2):hi - (t0 - 2)], XR[b, :, 2 * hd:2 * hd + 2, lo:hi], xkeys("XR", b, lo, hi), [kr])
                        for jj in range(2):
                            ch = 2 * hd + jj
                            S.act(XC[:, jj, t0:t1], xr[:, jj, 2:2 + n], AF.Identity, [kr, "vec"], [("XC", jj)],
                                  bias=vec[:, ch, 4:5], scale=vec[:, ch, 2:3])
                            for k_, off in ((0, -2), (1, -1), (3, 1)):
                                S.stt(XC[:, jj, t0:t1], xr[:, jj, 2 + off:2 + off + n], vec[:, ch, k_:k_ + 1], XC[:, jj, t0:t1],
                                      ALU.mult, ALU.add, [kr, "vec", ("XC", jj)], [("XC", jj)])
                    S.op("dve", lambda e: e.tensor_copy(out=XCb[:], in_=XC[:]), [("XC", 0), ("XC", 1)], ["XCb"])
                    for jj in range(2):
                        ch = 2 * hd + jj
                        S.load(Y[:], YG[b, :, ch, :], xkeys("YG", b, 0, T), ["Y"])
                        for d in range(2):
                            for (t0, t1) in tiles:
                                n = t1 - t0
                                pr = P[pit % 2]; kpr = "pr%d" % (pit % 2)
                                pi = P[2 + pit % 2]; kpi = "pi%d" % (pit % 2)
                                pit += 1
                                for ii in range(2):
                                    S.mm(pr[:, :n], gw[:, d, 0, hd, ii, jj * 128:(jj + 1) * 128], XCb[:, ii, t0:t1], ii == 0, ii == 1,
                                         ["gw", "XCb"], [kpr])
                                for ii in range(2):
                                    S.mm(pi[:, :n], gw[:, d, 1, hd, ii, jj * 128:(jj + 1) * 128], XCb[:, ii, t0:t1], ii == 0, ii == 1,
                                         ["gw", "XCb"], [kpi])
                                S.act(A[:, t0:t1], pr[:, :n], AF.Sigmoid, [kpr, "vec"], ["A"], bias=vec[:, ch, 5 + d:6 + d])
                                S.act(Gi[:, t0:t1], pi[:, :n], AF.Sigmoid, [kpi, "vec"], ["Gi"], bias=vec[:, ch, 7 + d:8 + d])
                            S.act(A[:], A[:], AF.Exp, ["A", "c8"], ["A"], scale=c8[:, ch, d:d + 1])
                            S.tt(Bt[:], A[:], A[:], ALU.mult, ["A"], ["Bt"])
                            S.act(Bt[:], Bt[:], AF.Sqrt, ["Bt"], ["Bt"], bias=1.0, scale=-1.0)
                            S.tt(Gi[:], Gi[:], XC[:, jj, :], ALU.mult, ["Gi", ("XC", jj)], ["Gi"])
                            S.tt(Bt[:], Bt[:], Gi[:], ALU.mult, ["Bt", "Gi"], ["Bt"])
                            if d == 0:
                                S.op("dve", lambda e: e.tensor_tensor_scan(out=Hf[:], data0=A[:], data1=Bt[:], initial=0.0,
                                                                           op0=ALU.mult, op1=ALU.add), ["A", "Bt"], ["Hf"])
                            else:
                                S.op("dve", lambda e: e.tensor_tensor_scan(out=Hb[:, 0:CT][:, ::-1], data0=A[:, 0:CT][:, ::-1],
                                                                           data1=Bt[:, 0:CT][:, ::-1], initial=0.0,
                                                                           op0=ALU.mult, op1=ALU.add), ["A", "Bt"], ["Hb"])
                                S.op("dve", lambda e: e.tensor_tensor_scan(out=Hb[:, CT:T][:, ::-1], data0=A[:, CT:T][:, ::-1],
                                                                           data1=Bt[:, CT:T][:, ::-1], initial=Hb[:, 0:1],
                                                                           op0=ALU.mult, op1=ALU.add), ["A", "Bt", "Hb"], ["Hb"])
                        S.tt(Hf[:], Hf[:], Hb[:], ALU.add, ["Hf", "Hb"], ["Hf"])
                        S.tt(mo[:], Hf[:], Y[:], ALU.mult, ["Hf", "Y"], ["mo"])
                        S.dma("pool", lambda e, b=b, ch=ch: e.dma_start(out=MT[b, :, ch, :], in_=mo[:]), ["mo"], xkeys("MT", b, 0, T))
            S.end_phase()
        with contextlib.ExitStack() as pst:
            S.stack = pst
            w_out = S.sb("lru_w_out", [128, 8, 1024], BF16)
            for k in range(8):
                S.load(w_out[:, k, :], I["lru_out"][:, k, :], [], ["w_out"], q="pool")
            self.out_proj(l, src, dst, MT, "MT", w_out, 8, tiles)
            S.end_phase()

    def out_proj(self, l, src, dst, MTd, mname, w_out, nk, tiles):
        S = self.S
        M = self.MOD
        P = self.psum
        xts = [S.sb("xt%d" % i, [128, 8, 512]) for i in range(2)]
        mts = [S.sb("mt%d" % i, [128, nk, 512], BF16) for i in range(2)]
        it = 0
        pit = 0
        for b in range(NB):
            for (t0, t1) in tiles:
                mi = 2 if t0 < CT else b
                n = t1 - t0
                xt = xts[it % 2]; kx = "xt%d" % (it % 2)
                mt = mts[it % 2]; km = "mt%d" % (it % 2)
                it += 1
                S.load(xt[:, :, :n], src[1][b, :, :, t0:t1], xkeys(src[0], b, t0, t1), [kx])
                S.load(mt[:, :, :n], MTd[b, :, :, t0:t1], xkeys(mname, b, t0, t1), [km])
                for d in range(8):
                    po = P[pit % 4]; kpo = "po%d" % (pit % 4); pit += 1
                    for k in range(nk):
                        S.mm(po[:, :n], w_out[:, k, d * 128:(d + 1) * 128], mt[:, k, :n], k == 0, k == nk - 1, [km, "w_out"], [kpo])
                    S.stt(xt[:, d, :n], po[:, :n], M[:, l, 2, d, mi:mi + 1], xt[:, d, :n], ALU.mult, ALU.add, [kpo, kx, "MOD"], [kx])
                S.dma("pool", lambda e, xt=xt, b=b, t0=t0, t1=t1, n=n: e.dma_start(out=dst[1][b, :, :, t0:t1], in_=xt[:, :, :n]),
                      [kx], xkeys(dst[0], b, t0, t1))

    def phase_mlstm(self, l, src, dst):
        S = self.S
        I = self.I
        M = self.MOD
        P = self.psum
        QT, KT, KTOK, VTOK, OTOK, GT, HF, MT = self.QT, self.KT, self.KTOK, self.VTOK, self.OTOK, self.GT, self.HF, self.MT
        tiles = self.seq_tiles(512)
        NCH = T // 128
        with contextlib.ExitStack() as pst:
            S.stack = pst
            w_in = S.sb("ml_w_in", [128, 8, 3088], BF16)
            for k in range(8):
                S.load(w_in[:, k, 0:2048], I["ml_in"][:, k, 0:2048], [], ["w_in"], q="pool")
                S.load(w_in[:, k, 2048:3088], I["ml_in"][:, k, 2048:3088], [], ["w_in"], q="pool")
            self.cast_up(l)
            bg = S.sb("ml_bg", [128, 16])
            S.load(bg[:], I["ml_bg"], [], ["bg"])
            xts = [S.sb("xt%d" % i, [128, 8, 512]) for i in range(2)]
            sq = S.sb("sq", [128, 8, 512], BF16)
            xn = [S.sb("xn%d" % i, [128, 512]) for i in range(2)]
            rstd = S.sb("rstd", [128, 512])
            hT = S.sb("hT", [128, 8, 512], BF16)
            qk = [S.sb("qk%d" % i, [128, 8, 512]) for i in range(2)]
            tok = [S.sb("tok%d" % i, [128, 2560]) for i in range(2)]
            gt = [S.sb("gt%d" % i, [128, 16]) for i in range(2)]
            gtmp = S.sb("gtmp", [128, 2, 4])
            it = 0
            pit = 0
            qit = 0
            for b in range(NB):
                for (t0, t1) in tiles:
                    mi = 2 if t0 < CT else b
                    n = t1 - t0
                    xt = xts[it % 2]; kx = "xt%d" % (it % 2)
                    qkt = qk[it % 2]; kqk = "qk%d" % (it % 2)
                    it += 1
                    S.load(xt[:, :, :n], src[1][b, :, :, t0:t1], xkeys(src[0], b, t0, t1), [kx])
                    self.norm_mod(xt, n, M[:, l, 0, :, mi], M[:, l, 1, :, mi], hT, sq, xn, rstd, P[7], kx, "hT", "")
                    for c in range(8):
                        pp = P[pit % 4]; kp = "pp%d" % (pit % 4); pit += 1
                        for k in range(8):
                            S.mm(pp[:, :n], w_in[:, k, c * 128:(c + 1) * 128], hT[:, k, :n], k == 0, k == 7, ["hT", "w_in"], [kp])
                        S.act(qkt[:, c, :n], pp[:, :n], AF.Copy, [kp], [kqk], scale=(128.0 ** -0.5 if c < 4 else 1.0))
                    S.dma("pool", lambda e, t=qkt, b=b, t0=t0, t1=t1, n=n: e.dma_start(out=QT[b, :, :, t0:t1], in_=t[:, 0:4, :n]),
                          [kqk], xkeys("QT", b, t0, t1))
                    S.dma("pool", lambda e, t=qkt, b=b, t0=t0, t1=t1, n=n: e.dma_start(out=KT[b, :, :, t0:t1], in_=t[:, 4:8, :n]),
                          [kqk], xkeys("KT", b, t0, t1))
                    for q in range(n // 128):
                        ci = (t0 + q * 128) // 128
                        tk = tok[qit % 2]; ktk = "tok%d" % (qit % 2)
                        g_ = gt[qit % 2]; kg = "gt%d" % (qit % 2)
                        qit += 1
                        for grp in range(5):
                            pp = P[pit % 4]; kp = "pp%d" % (pit % 4); pit += 1
                            for k in range(8):
                                S.mm(pp[:, :], hT[:, k, q * 128:(q + 1) * 128], w_in[:, k, 512 + grp * 512:512 + (grp + 1) * 512],
                                     k == 0, k == 7, ["hT", "w_in"], [kp])
                            if grp < 3:
                                if grp % 2 == 0:
                                    S.act(tk[:, grp * 512:(grp + 1) * 512], pp[:, :], AF.Copy, [kp], [(ktk, grp)])
                                else:
                                    S.op("dve", lambda e, o=tk[:, grp * 512:(grp + 1) * 512], i_=pp[:, :]: e.tensor_copy(out=o, in_=i_), [kp], [(ktk, grp)])
                            else:
                                S.act(tk[:, grp * 512:(grp + 1) * 512], pp[:, :], AF.Sigmoid, [kp], [(ktk, grp)])
                        pp = P[4 + qit % 2]; kp = "pg%d" % (qit % 2)
                        for k in range(8):
                            S.mm(pp[:, 0:16], hT[:, k, q * 128:(q + 1) * 128], w_in[:, k, 3072:3088], k == 0, k == 7, ["hT", "w_in"], [kp])
                        S.tt(g_[:], pp[:, 0:16], bg[:], ALU.add, [kp, "bg"], [kg])
                        gv = g_[:].rearrange("p (d g h) -> p d g h", d=2, g=2)
                        S.act(gtmp[:], gv[:, :, 1, :], AF.Exp, [kg], ["gtmp"], scale=-1.0)
                        S.act(gtmp[:], gtmp[:], AF.Ln, ["gtmp"], ["gtmp"], bias=1.0)
                        S.ts(gv[:, :, 1, :], gtmp[:], -1.0, None, ALU.mult, None, ["gtmp", kg], [kg])
                        S.dma("pool", lambda e, tk=tk, b=b, ci=ci: e.dma_start(out=KTOK[b, ci], in_=tk[:, 0:512]), [(ktk, 0)], [("KTOK", b, ci)])
                        S.dma("pool", lambda e, tk=tk, b=b, ci=ci: e.dma_start(out=VTOK[b, ci], in_=tk[:, 512:1536]), [(ktk, 1), (ktk, 2)], [("VTOK", b, ci)])
                        S.dma("pool", lambda e, tk=tk, b=b, ci=ci: e.dma_start(out=OTOK[b, ci], in_=tk[:, 1536:2560]), [(ktk, 3), (ktk, 4)], [("OTOK", b, ci)])
                        S.dma("pool", lambda e, g_=g_, b=b, ci=ci: e.dma_start(out=GT[b, ci], in_=g_[:]), [kg], [("GT", b, ci)])
            S.end_phase()
        with contextlib.ExitStack() as pst:
            S.stack = pst
            cst = S.sb("ml_cst", [128, 4, 128])
            S.load(cst[:], I["ml_cst"], [], ["cst"])
            ident_bf = S.sb("ident_bf", [128, 128], BF16)
            S.op("dve", lambda e: e.tensor_copy(out=ident_bf[:], in_=cst[:, 0, :]), ["cst"], ["ident_bf"])
            ngt = S.sb("ml_ng", [128, 1024])
            S.load(ngt[:], I["ml_ng"], [], ["ngt"])
            qTs = [S.sb("qT%d" % i, [128, 4, 128]) for i in range(4)]
            kTs = [S.sb("kT%d" % i, [128, 4, 128]) for i in range(4)]
            kts = [S.sb("ktok%d" % i, [128, 512]) for i in range(4)]
            vts = [S.sb("vtok%d" % i, [128, 1024]) for i in range(4)]
            gs = [S.sb("g%d" % i, [128, 16]) for i in range(4)]
            hfs = [S.sb("hf%d" % i, [128, 1024]) for i in range(4)]
            ots = [S.sb("ot%d" % i, [128, 1024]) for i in range(4)]
            kTbs = [S.sb("kTb%d" % i, [128, 4, 128], BF16) for i in range(NB)]
            LFbs = [S.sb("LFb%d" % i, [128, 4, 128]) for i in range(NB)]
            Erows = [S.sb("Erow%d" % i, [128, 4, 128]) for i in range(NB)]
            sms = [S.sb("sm%d" % i, [128, 8, 4]) for i in range(NB)]
            qss = [S.sb("qs%d" % i, [128, 4, 128], BF16) for i in range(NB)]
            kss = [S.sb("ks%d" % i, [128, 4, 128], BF16) for i in range(NB)]
            vexts = [S.sb("vext%d" % i, [128, 4, 260], BF16) for i in range(NB)]
            STss = [S.sb("STs%d" % i, [128, 4, 128], BF16) for i in range(NB)]
            STfs = [S.sb("STf%d" % i, [128, 4, 128]) for i in range(NB)]
            Cs = [S.sb("C%d" % i, [128, 4, 260]) for i in range(NB)]
            Cbfs = [S.sb("Cbf%d" % i, [128, 4, 260], BF16) for i in range(NB)]
            Hds = [S.sb("Hd%d" % i, [128, 1024]) for i in range(NB)]
            dns = [S.sb("dn%d" % i, [128, 4, 2]) for i in range(NB)]
            sss = [S.sb("ss%d" % i, [128, 8]) for i in range(NB)]
            junks = [S.sb("junk%d" % i, [128, 256], BF16) for i in range(NB)]
            mtoks = [S.sb("mtok%d" % i, [128, 1024], BF16) for i in range(NB)]
            mTts = [S.sb("mTt%d" % i, [128, 8, 128], BF16) for i in range(NB)]
            for b in range(NB):
                S.op("dve", lambda e, b=b: e.memset(vexts[b][:], 1.0), [], ["vext%d" % b])
            its = [0, 0]
            for d in range(2):
                order = list(range(NCH)) if d == 0 else [1, 0] + list(range(NCH - 1, 1, -1))
                tri = cst[:, 1 + d, :]
                for b in range(NB):
                    S.op("dve", lambda e, b=b: e.memset(Cs[b][:], 0.0), [], [("C%d" % b, hd) for hd in range(4)])
                    S.op("pool", lambda e, b=b: e.memset(Cbfs[b][:], 0.0), [], [("Cbf%d" % b, hd) for hd in range(4)])
                for ci in order:
                    for b in range(NB):
                        B_ = "%d" % b
                        kTb, LFb, Erow, sm, qs, ks, vext, STs = kTbs[b], LFbs[b], Erows[b], sms[b], qss[b], kss[b], vexts[b], STss[b]
                        STf = STfs[b]
                        C, Cbf, Hdt, dn, ss, junk, mtok, mT_ = Cs[b], Cbfs[b], Hds[b], dns[b], sss[b], junks[b], mtoks[b], mTts[b]
                        pbc = P[0][:, b * 8:b * 8 + 8]
                        pbrow = P[1 + b]
                        pstp = P[1 + b]
                        tk0 = ci * 128
                        sl = 2 * b + its[b] % 2; its[b] += 1
                        qT, kT, kt, vt, g_ = qTs[sl], kTs[sl], kts[sl], vts[sl], gs[sl]
                        kq, kk, kkt, kvt, kg = "qT%d" % sl, "kT%d" % sl, "ktok%d" % sl, "vtok%d" % sl, "g%d" % sl
                        S.load(qT[:], QT[b, :, :, tk0:tk0 + 128], xkeys("QT", b, tk0, tk0 + 128), [kq])
                        S.load(kT[:], KT[b, :, :, tk0:tk0 + 128], xkeys("KT", b, tk0, tk0 + 128), [kk])
                        S.load(kt[:], KTOK[b, ci], [("KTOK", b, ci)], [kkt])
                        S.load(vt[:], VTOK[b, ci], [("VTOK", b, ci)], [kvt])
                        S.load(g_[:], GT[b, ci], [("GT", b, ci)], [kg])
                        if d == 1:
                            hf, ot = hfs[sl], ots[sl]
                            khf, kot = "hf%d" % sl, "ot%d" % sl
                            S.load(hf[:], HF[b, ci], [("HF", b, ci)], [khf])
                            S.load(ot[:], OTOK[b, ci], [("OTOK", b, ci)], [kot])
                        ig = g_[:, d * 8:d * 8 + 4]
                        lf = g_[:, d * 8 + 4:d * 8 + 8]
                        S.mm(pbc[:, 0:4], tri, lf, True, True, ["cst", kg], ["bc"])
                        S.mm(pbc[:, 4:8], cst[:, 3, :], lf, True, True, ["cst", kg], ["bc"])
                        S.op("dve", lambda e, lf=lf, LFb=LFb: e.tensor_copy(out=LFb[:], in_=lf.rearrange("p (h n) -> p h n", n=1).to_broadcast([128, 4, 128])),
                             [kg], ["LFb" + B_])
                        for hd in range(4):
                            S.mm(pbrow[:, hd * 128:(hd + 1) * 128], LFb[:, hd, :], tri, True, True, ["LFb" + B_, "cst"], ["bst" + B_])
                        S.act(Erow[:], pbrow[:, :].rearrange("p (h n) -> p h n", h=4), AF.Exp, ["bst" + B_], ["Erow" + B_, "bst" + B_])
                        S.tt(sm[:, 0, :], ig, pbc[:, 0:4], ALU.subtract, [kg, "bc"], ["sm0" + B_, "bc"])
                        S.act(sm[:, 1, :], sm[:, 0, :], AF.Exp, ["sm0" + B_], ["ek" + B_])
                        S.act(sm[:, 2, :], pbc[:, 4:8], AF.Exp, ["bc"], ["ebl" + B_, "bc"])
                        S.tt(qs[:], qT[:], Erow[:], ALU.mult, [kq, "Erow" + B_], ["qs" + B_])
                        S.op("pool", lambda e, kT=kT, kTb=kTb: e.tensor_copy(out=kTb[:], in_=kT[:]), [kk], ["kTb" + B_])
                        S.tt(ks[:], kt[:].rearrange("p (h n) -> p h n", h=4),
                             sm[:, 1, :].rearrange("p (h n) -> p h n", n=1).to_broadcast([128, 4, 128]), ALU.mult, [kkt, "ek" + B_], ["ks" + B_])
                        S.op("pool", lambda e, vt=vt, vext=vext: e.tensor_copy(out=vext[:, :, 0:256], in_=vt[:].rearrange("p (h n) -> p h n", h=4)),
                             [kvt], ["vext" + B_])
                        for hd in range(4):
                            S.mm(pstp[:, hd * 128:(hd + 1) * 128], kTb[:, hd, :], qs[:, hd, :], True, True, ["kTb" + B_, "qs" + B_], ["bst" + B_])
                        S.tt(STf[:], pstp[:, :].rearrange("p (h n) -> p h n", h=4),
                             sm[:, 1, :].rearrange("p (h n) -> p h n", n=1).to_broadcast([128, 4, 128]), ALU.mult, ["bst" + B_, "ek" + B_], ["STf" + B_, "bst" + B_])
                        S.tt(STs[:], STf[:], tri.rearrange("p (h n) -> p h n", h=1).to_broadcast([128, 4, 128]), ALU.mult,
                             ["STf" + B_, "cst"], [("STs" + B_, hd) for hd in range(4)])
                        pden = P[0][:, 16 + b * 4:16 + b * 4 + 4]
                        for hd in range(4):
                            S.mm(pden[:, hd:hd + 1], qs[:, hd, :], Cbf[:, hd, 256:257], True, False, ["qs" + B_, ("Cbf" + B_, hd)], ["bc"])
                            S.mm(pden[:, hd:hd + 1], STs[:, hd, :], vext[:, hd, 256:257], False, True, [("STs" + B_, hd), "vext" + B_], ["bc"])
                        S.act(dn[:, :, 0], pden, AF.Abs, ["bc"], ["dn" + B_, "bc"])
                        S.ts(dn[:, :, 0], dn[:, :, 0], 1.0, None, ALU.max, None, ["dn" + B_], ["dn" + B_])
                        S.op("dve", lambda e, dn=dn: e.reciprocal(out=dn[:, :, 1], in_=dn[:, :, 0]), ["dn" + B_], ["dn" + B_])
                        khd = "Hd" + B_
                        for hd in range(4):
                            nh = P[3 + hd % 2]; knh = "nh%d" % (hd % 2)
                            up = P[5 + hd % 2]; kup = "up%d" % (hd % 2)
                            S.mm(nh[:, 0:257], qs[:, hd, :], Cbf[:, hd, 0:257], True, False, ["qs" + B_, ("Cbf" + B_, hd)], [knh])
                            S.mm(nh[:, 0:257], STs[:, hd, :], vext[:, hd, 0:257], False, True, [("STs" + B_, hd), "vext" + B_], [knh])
                            S.mm(up[:, 0:257], ks[:, hd, :], vext[:, hd, 0:257], True, True, ["ks" + B_, "vext" + B_], [kup])
                            S.ts(Hdt[:, hd * 256:(hd + 1) * 256], nh[:, 0:256], dn[:, hd, 1:2], None, ALU.mult, None, [knh, "dn" + B_], [(khd, hd)])
                            S.tt(C[:, hd, 0:257], C[:, hd, 0:257], up[:, 0:257], ALU.add, [("C" + B_, hd), kup], [("C" + B_, hd)])
                            S.act(C[:, hd, 0:257], C[:, hd, 0:257], AF.Copy, [("C" + B_, hd), "ebl" + B_], [("C" + B_, hd)], scale=sm[:, 2, hd:hd + 1])
                            S.op("pool", lambda e, hd=hd, C=C, Cbf=Cbf: e.tensor_copy(out=Cbf[:, hd, 0:257], in_=C[:, hd, 0:257]), [("C" + B_, hd)], [("Cbf" + B_, hd)])
                        hkeys = [(khd, hd) for hd in range(4)]
                        if d == 0:
                            S.dma("pool", lambda e, Hdt=Hdt, b=b, ci=ci: e.dma_start(out=HF[b, ci], in_=Hdt[:]), hkeys, [("HF", b, ci)])
                        else:
                            S.tt(Hdt[:], Hdt[:], hf[:], ALU.add, hkeys + [khf], hkeys)
                            for hd in range(4):
                                S.act(junk[:], Hdt[:, hd * 256:(hd + 1) * 256], AF.Square, [(khd, hd)], ["junk" + B_], accum_out=ss[:, hd:hd + 1])
                            S.act(ss[:, 4:8], ss[:, 0:4], AF.Sqrt, ["junk" + B_], ["ss4" + B_], bias=EPS, scale=1.0 / 256)
                            S.op("dve", lambda e, ss=ss: e.reciprocal(out=ss[:, 4:8], in_=ss[:, 4:8]), ["ss4" + B_], ["ss4" + B_])
                            S.tt(Hdt[:].rearrange("p (h n) -> p h n", h=4), Hdt[:].rearrange("p (h n) -> p h n", h=4),
                                 ss[:, 4:8].rearrange("p (h n) -> p h n", n=1).to_broadcast([128, 4, 256]), ALU.mult, hkeys + ["ss4" + B_], hkeys)
                            S.tt(Hdt[:], Hdt[:], ngt[:], ALU.mult, hkeys + ["ngt"], hkeys)
                            S.tt(mtok[:], Hdt[:], ot[:], ALU.mult, hkeys + [kot], ["mtok" + B_])
                            kmt = "mTt" + B_
                            for half in range(2):
                                for c4 in range(4):
                                    c = half * 4 + c4
                                    S.mm(P[7][:, c4 * 128:(c4 + 1) * 128], mtok[:, c * 128:(c + 1) * 128], ident_bf[:], True, True,
                                         ["mtok" + B_, "ident_bf"], ["trp"])
                                S.act(mT_[:, half * 4:half * 4 + 4, :], P[7][:, :].rearrange("p (c n) -> p c n", c=4), AF.Copy, ["trp"], [kmt])
                            S.dma("pool", lambda e, mT_=mT_, b=b, tk0=tk0: e.dma_start(out=MT[b, :, :, tk0:tk0 + 128], in_=mT_[:]),
                                  [kmt], xkeys("MT", b, tk0, tk0 + 128))
            S.end_phase()
        with contextlib.ExitStack() as pst:
            S.stack = pst
            w_out = S.sb("ml_w_out", [128, 8, 1024], BF16)
            for k in range(8):
                S.load(w_out[:, k, :], I["ml_out"][:, k, :], [], ["w_out"], q="pool")
            self.out_proj(l, src, dst, MT, "MT", w_out, 8, tiles)
            S.end_phase()

    def phase_final(self, cur):
        S = self.S
        I = self.I
        with contextlib.ExitStack() as pst:
            S.stack = pst
            fg = S.sb("fg", [128, 8])
            S.load(fg[:], I["final_g"], [], ["fg"])
            xts = [S.sb("xt%d" % i, [128, 8, 512]) for i in range(2)]
            sq = S.sb("sq", [128, 8, 512], BF16)
            xn = S.sb("xn", [128, 8, 512])
            rstd = S.sb("rstd", [128, 512])
            xo = [S.sb("xo%d" % i, [128, 8, 512]) for i in range(2)]
            it = 0
            for b in range(NB):
                for i in range(LT // 512):
                    t0 = CT + i * 512
                    n = 512
                    xt = xts[it % 2]; kx = "xt%d" % (it % 2)
                    xot = xo[it % 2]; kxo = "xo%d" % (it % 2)
                    it += 1
                    S.load(xt[:], cur[1][b, :, :, t0:t0 + n], xkeys(cur[0], b, t0, t0 + n), [kx])
                    ssps = self.psum[it % 2]; kss = "ssps%d" % (it % 2)
                    S.act(sq[:], xt[:], AF.Square, [kx], ["sq"])
                    for c in range(8):
                        S.mm(ssps[:, :n], self.ones_bf[:], sq[:, c, :n], c == 0, c == 7, ["sq", "ones_bf"], [kss])
                    S.act(rstd[:], ssps[:, :n], AF.Sqrt, [kss], ["rstd"], bias=EPS, scale=1.0 / D)
                    S.op("dve", lambda e: e.reciprocal(out=rstd[:], in_=rstd[:]), ["rstd"], ["rstd"])
                    S.tt(xn[:], xt[:], rstd[:].rearrange("p (c n) -> p c n", c=1).to_broadcast([128, 8, n]), ALU.mult, [kx, "rstd"], ["xn"])
                    S.tt(xot[:], xn[:], fg[:].rearrange("p (c n) -> p c n", n=1).to_broadcast([128, 8, n]), ALU.mult, ["xn", "fg"], [kxo])
                    S.dma("pool", lambda e, xot=xot, b=b, i=i: e.dma_start(out=self.outT[b, :, :, i * 512:(i + 1) * 512], in_=xot[:]),
                          [kxo], [("out", b, i)])
            S.end_phase()


INPUT_SHAPES = {
    "xin": (NB, 128, 8, T),
    "cond": (128, 8, 4),
    "mod_w": (4, 128, 8, 6144),
    "mod_b": (128, 4, 48),
    "norm_g": (128, 4, 2, 8),
    "final_g": (128, 8),
    "ffn_up": (4, NF, 128, 2, 8, 128),
    "ffn_down": (4, 128, NF, 1024),
    "ffn_cw": (128, 4, 42, 9),
    "ffn_cb": (128, 4, 42),
    "cm_in": (2, 128, 8, 4096),
    "cm_out": (2, 128, 16, 1024),
    "cm_wsT": (2, 128, 8, 128),
    "cm_bin_u": (128, 2, 16),
    "cm_rep": (2, 3, 128, 2048),
    "cm_vgb": (128, 2, 2, 16),
    "cm_bs": (2, 128, 8, 128),
    "lru_in": (128, 8, 2048),
    "lru_out": (128, 8, 1024),
    "lru_gw": (128, 2, 2, 4, 2, 256),
    "lru_vec": (128, 8, 11),
    "ml_in": (128, 8, 3088),
    "ml_out": (128, 8, 1024),
    "ml_bg": (128, 16),
    "ml_ng": (128, 1024),
    "ml_cst": (128, 4, 128),
}


def fm(v, nch):
    v = np.asarray(v, np.float32)
    lead = v.shape[:-1]
    a = v.reshape(lead + (nch, 128))
    a = np.moveaxis(a, -1, 0)
    return np.ascontiguousarray(a)


def layout_shared(inp):
    f32 = np.float32
    W = {}
    W["mod_w"] = np.ascontiguousarray(inp["mod_w"].reshape(4, 8, 128, 6144).transpose(0, 2, 1, 3))
    W["mod_b"] = fm(inp["mod_b"], 48)
    W["norm_g"] = np.ascontiguousarray(np.stack([fm(inp["norm1_g"], 8), fm(inp["norm2_g"], 8)], axis=2))
    W["final_g"] = fm(inp["final_norm_g"], 8)
    up = inp["ffn_w_up"].reshape(4, 8, 128, 2, NF, 128)
    W["ffn_up"] = np.ascontiguousarray(up.transpose(0, 4, 2, 3, 1, 5))
    W["ffn_down"] = np.ascontiguousarray(inp["ffn_w_down"].reshape(4, NF, 128, 1024).transpose(0, 2, 1, 3))
    cw = inp["ffn_conv_w"].reshape(4, 9, 42, 128)
    W["ffn_cw"] = np.ascontiguousarray(cw.transpose(3, 0, 2, 1))
    W["ffn_cb"] = fm(inp["ffn_conv_b"], 42)
    W["cm_in"] = np.ascontiguousarray(inp["cm_w_in"].reshape(2, 8, 128, 4096).transpose(0, 2, 1, 3))
    W["cm_out"] = np.ascontiguousarray(inp["cm_w_out"].reshape(2, 16, 128, 1024).transpose(0, 2, 1, 3))
    W["cm_wsT"] = np.ascontiguousarray(inp["cm_w_s"].transpose(0, 3, 1, 2))
    W["cm_bin_u"] = fm(inp["cm_b_in"][:, :2048], 16)
    W["cm_vgb"] = np.ascontiguousarray(np.stack([fm(inp["cm_v_g"], 16), fm(inp["cm_v_b"], 16)], axis=2))
    rep = np.stack([inp["cm_b_in"][:, 2048:], inp["cm_v_g"], inp["cm_v_b"]], axis=1)
    W["cm_rep"] = np.ascontiguousarray(np.broadcast_to(rep[:, :, None, :], (2, 3, 128, 2048)))
    bs = np.tile(inp["cm_b_s"][:, None, :, :], (1, 128, 1, 1))
    W["cm_bs"] = np.ascontiguousarray(bs)
    W["lru_in"] = np.ascontiguousarray(inp["lru_w_in"][0].reshape(8, 128, 2048).transpose(1, 0, 2))
    W["lru_out"] = np.ascontiguousarray(inp["lru_w_out"][0].reshape(8, 128, 1024).transpose(1, 0, 2))
    gw = np.stack([inp["lru_w_rg"][0], inp["lru_w_ig"][0]], axis=1)
    gw = gw.reshape(2, 2, 4, 2, 128, 256)
    W["lru_gw"] = np.ascontiguousarray(gw.transpose(4, 0, 1, 2, 3, 5))
    vec = np.concatenate([fm(inp["lru_conv_w"][0], 8).transpose(0, 2, 1),
                          fm(inp["lru_conv_b"][0], 8)[:, :, None],
                          fm(inp["lru_b_rg"][0], 8).transpose(0, 2, 1),
                          fm(inp["lru_b_ig"][0], 8).transpose(0, 2, 1),
                          fm(inp["lru_lambda"][0], 8).transpose(0, 2, 1)], axis=2)
    W["lru_vec"] = np.ascontiguousarray(vec)
    W["ml_in"] = np.ascontiguousarray(inp["ml_w_in"][0].reshape(8, 128, 3088).transpose(1, 0, 2))
    W["ml_out"] = np.ascontiguousarray(inp["ml_w_out"][0].reshape(8, 128, 1024).transpose(1, 0, 2))
    W["ml_bg"] = np.ascontiguousarray(np.broadcast_to(inp["ml_b_gate"][0].reshape(1, 16), (128, 16)))
    W["ml_ng"] = np.ascontiguousarray(np.broadcast_to(inp["ml_norm_g"][0].reshape(1, 1024), (128, 1024)))
    r = np.arange(128)
    cst = np.stack([np.eye(128), (r[:, None] <= r[None, :]), (r[:, None] >= r[None, :]), np.ones((128, 128))], axis=1)
    W["ml_cst"] = np.ascontiguousarray(cst.astype(np.float32))
    return {k: np.ascontiguousarray(v, dtype=f32) for k, v in W.items()}


def layout_core(inp, i):
    b0 = NB * i
    seq = np.concatenate([inp["ctx"][b0:b0 + NB], inp["x"][b0:b0 + NB]], axis=1)
    xin = np.ascontiguousarray(seq.reshape(NB, T, 8, 128).transpose(0, 3, 2, 1))
    cond = np.zeros((4, D), np.float32)
    cond[0:NB] = inp["c"][b0:b0 + NB]
    cond[2] = inp["c_ctx"]
    condT = np.ascontiguousarray(cond.reshape(4, 8, 128).transpose(2, 1, 0))
    return {"xin": xin.astype(np.float32), "cond": condT}


_CACHE = {}


def kernel(**inputs):
    inp = {k: np.asarray(v) for k, v in inputs.items()}
    n_cores = 8
    shared = layout_shared(inp)
    if "nc" not in _CACHE:
        _CACHE["nc"] = Prog().build()
    nc = _CACHE["nc"]
    in_maps = []
    for i in range(n_cores):
        m = dict(shared)
        m.update(layout_core(inp, i))
        in_maps.append(m)
    res = run_bass_kernel_spmd(nc, in_maps, core_ids=list(range(n_cores)))
    outs = []
    for i in range(n_cores):
        oT = res.results[i]["outT"]
        outs.append(np.ascontiguousarray(oT.transpose(0, 3, 2, 1)).reshape(NB, LT, D))
    return np.concatenate(outs, axis=0).astype(np.float32)
```

```python
import contextlib
import os
import numpy as np
import concourse.bass as bass
import concourse.mybir as mybir
from concourse.bass_utils import run_bass_kernel_spmd

F32 = mybir.dt.float32
BF16 = mybir.dt.bfloat16
AF = mybir.ActivationFunctionType
ALU = mybir.AluOpType

D = 1024
CT = 256
LT = 4096
T = CT + LT
NB = 2
FH = 2688
NF = 21
EPS = 1e-6
ENGS = ["pe", "act", "dve", "pool", "sp"]
HANDLE = {"pe": "tensor", "act": "scalar", "dve": "vector", "pool": "gpsimd", "sp": "sync"}


class Sched:
    def __init__(self, nc, gstack, n_dma_sems=16):
        self.nc = nc
        self.stack = gstack
        self.ops = {e: [] for e in ENGS}
        self.sem = {e: gstack.enter_context(nc.semaphore("s_" + e)) for e in ENGS}
        self.count = {e: 0 for e in ENGS}
        self.waited = {e: {} for e in ENGS}
        self.same_sync = {"act", "dve", "pool"}
        self.last_writer = {}
        self.readers = {}
        self.dma_sems = {}
        self.dma_rr = {}
        for q in ("sp", "pool"):
            self.dma_sems[q] = [[gstack.enter_context(nc.semaphore("d_%s%d" % (q, i))), 0] for i in range(n_dma_sems)]
            self.dma_rr[q] = 0
        self.n_ops = 0

    def sb(self, name, shape, dt=F32):
        self.n_sb = getattr(self, "n_sb", 0) + 1
        return self.stack.enter_context(self.nc.sbuf_tensor("sb%d_%s" % (self.n_sb, name), shape, dt))

    def _deps(self, reads, writes):
        deps = []
        for k in reads:
            w = self.last_writer.get(k)
            if w is not None:
                deps.append(w)
        for k in writes:
            w = self.last_writer.get(k)
            if w is not None:
                deps.append(w)
            deps.extend(self.readers.get(k, {}).values())
        return deps

    def _record(self, tok, reads, writes):
        for k in reads:
            self.readers.setdefault(k, {})[id(tok[0])] = tok
        for k in writes:
            self.last_writer[k] = tok
            self.readers[k] = {}

    def _waits(self, eng, deps):
        waits = []
        wd = self.waited[eng]
        own = self.sem[eng]
        for (sem, val) in deps:
            if sem is own and eng not in self.same_sync:
                continue
            if wd.get(id(sem), 0) < val:
                wd[id(sem)] = val
                waits.append((sem, val))
        return waits

    def op(self, eng, fn, reads=(), writes=()):
        deps = self._deps(reads, writes)
        waits = self._waits(eng, deps)
        self.count[eng] += 1
        tok = (self.sem[eng], self.count[eng])
        self.ops[eng].append((fn, waits, (self.sem[eng], 1)))
        self._record(tok, reads, writes)
        self.n_ops += 1
        return tok

    def dma(self, q, fn, reads=(), writes=()):
        deps = self._deps(reads, writes)
        slot = self.dma_sems[q][self.dma_rr[q]]
        self.dma_rr[q] = (self.dma_rr[q] + 1) % len(self.dma_sems[q])
        if slot[1] > 0:
            deps.append((slot[0], slot[1]))
        waits = self._waits(q, deps)
        slot[1] += 16
        tok = (slot[0], slot[1])
        self.ops[q].append((fn, waits, (slot[0], 16)))
        self._record(tok, reads, writes)
        self.n_ops += 1
        return tok

    def end_phase(self):
        toks = []
        for q in self.dma_sems:
            for sem, val in self.dma_sems[q]:
                if val > 0:
                    toks.append((sem, val))
        for e in ENGS:
            if self.count[e] > 0:
                toks.append((self.sem[e], self.count[e]))
        for e in ENGS:
            waits = self._waits(e, list(toks))
            self.ops[e].append((None, waits, None))
        nc = self.nc
        with nc.Block() as block:
            for e in ENGS:
                ops = self.ops[e]

                def body(eng, ops=ops):
                    for fn, waits, inc in ops:
                        for (sem, val) in waits:
                            eng.wait_ge(sem, val)
                        if fn is not None:
                            inst = fn(eng)
                            inst.then_inc(inc[0], inc[1])

                getattr(block, HANDLE[e])(body)
        self.ops = {e: [] for e in ENGS}
        self.last_writer = {}
        self.readers = {}

    def mm(self, out, lhsT, rhs, start, stop, reads, writes):
        return self.op("pe", lambda e: e.matmul(out, lhsT=lhsT, rhs=rhs, start=start, stop=stop), reads, writes)

    def act(self, out, in_, func, reads, writes, bias=0.0, scale=1.0, accum_out=None):
        if accum_out is None:
            return self.op("act", lambda e: e.activation(out=out, in_=in_, func=func, bias=bias, scale=scale), reads, writes)
        return self.op("act", lambda e: e.activation(out=out, in_=in_, func=func, bias=bias, scale=scale, accum_out=accum_out), reads, writes)

    def tt(self, out, in0, in1, op, reads, writes, eng="dve"):
        return self.op(eng, lambda e: e.tensor_tensor(out=out, in0=in0, in1=in1, op=op), reads, writes)

    def ts(self, out, in0, s1, s2, op0, op1, reads, writes, eng="dve"):
        if s2 is None:
            return self.op(eng, lambda e: e.tensor_scalar(out=out, in0=in0, scalar1=s1, scalar2=None, op0=op0), reads, writes)
        return self.op(eng, lambda e: e.tensor_scalar(out=out, in0=in0, scalar1=s1, scalar2=s2, op0=op0, op1=op1), reads, writes)

    def stt(self, out, in0, scalar, in1, op0, op1, reads, writes, accum_out=None):
        if accum_out is None:
            return self.op("dve", lambda e: e.scalar_tensor_tensor(out=out, in0=in0, scalar=scalar, in1=in1, op0=op0, op1=op1), reads, writes)
        return self.op("dve", lambda e: e.scalar_tensor_tensor(out=out, in0=in0, scalar=scalar, in1=in1, op0=op0, op1=op1, accum_out=accum_out), reads, writes)

    def load(self, out, in_, reads, writes, q="sp"):
        return self.dma(q, lambda e: e.dma_start(out=out, in_=in_), reads, writes)


def xkeys(name, b, t0, t1):
    return [(name, b, i) for i in range(t0 // 64, (t1 + 63) // 64)]


class Prog:
    def __init__(self, n_layers=4, dbg=None):
        self.n_layers = n_layers
        self.dbg = dbg

    def dram_in(self, name, shape, dt=F32):
        return self.nc.dram_tensor(name, list(shape), dt, kind="ExternalInput").ap()

    def dram_tmp(self, name, shape, dt=F32):
        return self.nc.dram_tensor(name, list(shape), dt, kind="Internal").ap()

    def build(self):
        nc = bass.Bass("TRN2", target_bir_lowering=False)
        self.nc = nc
        I = {}
        for name, shape in INPUT_SHAPES.items():
            I[name] = self.dram_in(name, shape)
        self.I = I
        self.outT = nc.dram_tensor("outT", [NB, 128, 8, LT], F32, kind="ExternalOutput").ap()
        self.XA = self.dram_tmp("XA", [NB, 128, 8, T])
        self.XB = self.dram_tmp("XB", [NB, 128, 8, T])
        self.up_bf = self.dram_tmp("up_bf", [4, NF, 128, 2, 8, 128], BF16)
        self.DG = self.dram_tmp("DGs", [NF, 128, 2, 9, 128], BF16)
        self.XR = self.dram_tmp("XRs", [NB, 128, 8, T])
        self.YG = self.dram_tmp("YGs", [NB, 128, 8, T])
        self.MT = self.dram_tmp("MTs", [NB, 128, 8, T], BF16)
        NCH = T // 128
        self.QT = self.dram_tmp("QTs", [NB, 128, 4, T])
        self.KT = self.dram_tmp("KTs", [NB, 128, 4, T])
        self.KTOK = self.dram_tmp("KTOKs", [NB, NCH, 128, 512])
        self.VTOK = self.dram_tmp("VTOKs", [NB, NCH, 128, 1024])
        self.OTOK = self.dram_tmp("OTOKs", [NB, NCH, 128, 1024])
        self.GT = self.dram_tmp("GTs", [NB, NCH, 128, 16])
        self.HF = self.dram_tmp("HFs", [NB, NCH, 128, 1024])
        if self.dbg is not None:
            self.dbgX = nc.dram_tensor("dbgX", [NB, 128, 8, T], F32, kind="ExternalOutput").ap()
        with contextlib.ExitStack() as gst:
            S = Sched(nc, gst)
            self.S = S
            self.ones_bf = S.sb("ones_bf", [128, 128], BF16)
            self.MOD = S.sb("MOD", [128, 4, 6, 8, 4])
            self.psum = [gst.enter_context(nc.psum_tensor("ps%d" % i, [128, 512], F32)) for i in range(8)]
            S.op("dve", lambda e: e.memset(self.ones_bf[:], 1.0), writes=["ones_bf"])
            self.phase_setup()
            cur = ("xin", I["xin"])
            bufs = [("XA", self.XA), ("XB", self.XB)]
            bi = 0
            for l in range(self.n_layers):
                last = l == 3
                kind = l % 3
                dst = bufs[bi]; bi ^= 1
                if kind == 0:
                    self.phase_cm(l, cur, dst, last)
                elif kind == 1:
                    self.phase_lru(l, cur, dst)
                else:
                    self.phase_mlstm(l, cur, dst)
                cur = dst
                if self.dbg == ("mix", l):
                    self.phase_dump(cur)
                dst = bufs[bi]; bi ^= 1
                self.phase_ffn(l, cur, dst, last)
                cur = dst
                if self.dbg == ("ffn", l):
                    self.phase_dump(cur)
            self.phase_final(cur)
        return nc

    def phase_dump(self, cur):
        S = self.S
        for b in range(NB):
            S.dma("sp", lambda e, b=b: e.dma_start(out=self.dbgX[b], in_=cur[1][b]), reads=xkeys(cur[0], b, 0, T), writes=[("dbg", b)])
        S.end_phase()

    def cast_up(self, l):
        S = self.S
        src = self.I["ffn_up"][l].rearrange("f p two k m -> (f p) (two k m)")
        dst = self.up_bf[l].rearrange("f p two k m -> (f p) (two k m)")
        rows = NF * 128
        RSTEP = 384
        for r0 in range(0, rows, RSTEP):
            S.dma("pool", lambda e, r0=r0: e.dma_start(out=dst[r0:r0 + RSTEP, :], in_=src[r0:r0 + RSTEP, :]), writes=[("up_bf", l)])

    def phase_setup(self):
        S = self.S
        I = self.I
        with contextlib.ExitStack() as pst:
            S.stack = pst
            cond = S.sb("cond", [128, 8, 4])
            condr = S.sb("condr", [128, 8, 4])
            modb = S.sb("modb", [128, 4, 48])
            ng = S.sb("ng", [128, 4, 2, 8])
            raw = S.sb("raw", [128, 48, 4])
            wbuf = [S.sb("modw%d" % i, [128, 8, 512]) for i in range(2)]
            S.load(condr[:], I["cond"], [], ["condr"])
            S.load(modb[:], I["mod_b"], [], ["modb"])
            S.load(ng[:], I["norm_g"], [], ["ng"])
            S.act(cond[:], condr[:], AF.Silu, ["condr"], ["cond"])
            mps = self.psum[0]
            mpsv = mps[:, 0:192].rearrange("p (c n) -> p c n", n=4)
            it = 0
            for l in range(4):
                for g in range(12):
                    wb = wbuf[it % 2]; wk = "modw%d" % (it % 2); it += 1
                    S.load(wb[:], I["mod_w"][l, :, :, g * 512:(g + 1) * 512], [], [wk])
                    for m in range(4):
                        ch = g * 4 + m
                        for k in range(8):
                            S.mm(mpsv[:, ch, :], wb[:, k, m * 128:(m + 1) * 128], cond[:, k, :], k == 0, k == 7,
                                 [wk, "cond"], ["mps"])
                S.tt(raw[:], mpsv, modb[:, l, :].rearrange("p (c n) -> p c n", n=1).to_broadcast([128, 48, 4]), ALU.add,
                     ["mps", "modb"], ["raw"])
                rv = raw[:].rearrange("p (s c) n -> p s c n", s=6)
                M = self.MOD
                for half in range(2):
                    sh, sc, gg = rv[:, 3 * half + 0], rv[:, 3 * half + 1], rv[:, 3 * half + 2]
                    gb = ng[:, l, half, :].rearrange("p (c n) -> p c n", n=1).to_broadcast([128, 8, 4])
                    S.stt(M[:, l, 3 * half + 0], sc, 1.0, gb, ALU.add, ALU.mult, ["raw", "ng"], ["MOD"])
                    S.op("dve", lambda e, o=M[:, l, 3 * half + 1], i=sh: e.tensor_copy(out=o, in_=i), ["raw"], ["MOD"])
                    S.op("dve", lambda e, o=M[:, l, 3 * half + 2], i=gg: e.tensor_copy(out=o, in_=i), ["raw"], ["MOD"])
            S.end_phase()

    def norm_mod(self, xt, n, A, B, hT, sq, xn, rstd, ssps, kx, kh, tag, kss=None):
        S = self.S
        kss = kss or ("ssps" + tag)
        S.act(sq[:, :, :n], xt[:, :, :n], AF.Square, [kx], ["sq" + tag])
        for c in range(8):
            S.mm(ssps[:, :n], self.ones_bf[:], sq[:, c, :n], c == 0, c == 7, ["sq" + tag, "ones_bf"], [kss])
        S.act(rstd[:, :n], ssps[:, :n], AF.Sqrt, [kss], ["rstd" + tag], bias=EPS, scale=1.0 / D)
        S.op("dve", lambda e: e.reciprocal(out=rstd[:, :n], in_=rstd[:, :n]), ["rstd" + tag], ["rstd" + tag])
        for c in range(8):
            tmp = xn[c % 2]; kt = "xn%d%s" % (c % 2, tag)
            S.tt(tmp[:, :n], xt[:, c, :n], rstd[:, :n], ALU.mult, [kx, "rstd" + tag], [kt])
            S.act(hT[:, c, :n], tmp[:, :n], AF.Identity, [kt, "MOD"], [kh], bias=B[:, c:c + 1], scale=A[:, c:c + 1])

    def norm_mod_steps(self, xt, n, A, B, hT, sq, xn, rstd, ssps, kx, kh, kss):
        S = self.S

        def s1():
            S.act(sq[:, :, :n], xt[:, :, :n], AF.Square, [kx], ["sq"])
            for c in range(8):
                S.mm(ssps[:, :n], self.ones_bf[:], sq[:, c, :n], c == 0, c == 7, ["sq", "ones_bf"], [kss])

        def s2():
            S.act(rstd[:, :n], ssps[:, :n], AF.Sqrt, [kss], ["rstd"], bias=EPS, scale=1.0 / D)
            S.op("dve", lambda e: e.reciprocal(out=rstd[:, :n], in_=rstd[:, :n]), ["rstd"], ["rstd"])

        def mk(c0_, c1_):
            def f_():
                for c in range(c0_, c1_):
                    tmp = xn[c % 2]; kt = "xn%d" % (c % 2)
                    S.tt(tmp[:, :n], xt[:, c, :n], rstd[:, :n], ALU.mult, [kx, "rstd"], [kt])
                    S.act(hT[:, c, :n], tmp[:, :n], AF.Identity, [kt, "MOD"], [kh], bias=B[:, c:c + 1], scale=A[:, c:c + 1])
            return f_

        return [s1, s2, mk(0, 4), mk(4, 8)]

    def seq_tiles(self, step=512):
        tiles = [(0, CT)]
        for i in range(LT // step):
            tiles.append((CT + i * step, CT + (i + 1) * step))
        return tiles

    def phase_cm(self, l, src, dst, last):
        S = self.S
        I = self.I
        j = l // 3
        M = self.MOD
        P = self.psum
        with contextlib.ExitStack() as pst:
            S.stack = pst
            NT = 256
            w_in = S.sb("cm_w_in", [128, 8, 4096], BF16)
            w_out = S.sb("cm_w_out", [128, 16, 1024], BF16)
            wsT = S.sb("cm_wsT", [128, 8, 128], BF16)
            bin_u = S.sb("cm_bin_u", [128, 2, 16])
            vgb = S.sb("cm_vgb", [128, 2, 2, 16])
            rep = S.sb("cm_rep", [128, 2048])
            CB = S.sb("cm_CB", [128, 16, 128])
            vt = S.sb("vt", [128, 2048])
            bs = vt[:, 0:1024].rearrange("p (g m) -> p g m", g=8)
            for k in range(8):
                for hh in range(2):
                    S.load(w_in[:, k, hh * 2048:(hh + 1) * 2048], I["cm_in"][j, :, k, hh * 2048:(hh + 1) * 2048], [], ["w_in"], q="pool")
            for c in range(16):
                S.load(w_out[:, c, :], I["cm_out"][j, :, c, :], [], ["w_out"], q="pool")
            S.load(wsT[:], I["cm_wsT"][j], [], ["wsT"], q="pool")
            self.cast_up(l)
            S.load(bin_u[:], I["cm_bin_u"], [], ["bin_u"])
            S.load(vgb[:], I["cm_vgb"], [], ["vgb"])
            S.load(rep[:], I["cm_rep"][j, 0], [], ["rep"])
            S.load(bs, I["cm_bs"][j], [], [("vt", 0), ("vt", 1)])
            for hh in range(2):
                S.mm(P[hh][:, :], self.ones_bf[:], wsT[:, 4 * hh:4 * hh + 4, :], True, True, ["ones_bf", "wsT"], ["pv%d" % hh])
            for c in range(16):
                g = c // 2
                S.stt(CB[:, c, :], P[g // 4][:, (g % 4) * 128:(g % 4 + 1) * 128], vgb[:, j, 1, c:c + 1], bs[:, g, :], ALU.mult, ALU.add,
                      ["pv%d" % (g // 4), "vgb", ("vt", 0), ("vt", 1)], ["CB"])
            xts = [S.sb("xt%d" % i, [128, 8, NT]) for i in range(1)]
            xos = [S.sb("xo%d" % i, [128, 8, NT]) for i in range(2)]
            hTs = [S.sb("hT%d" % i, [128, 8, NT], BF16) for i in range(2)]
            vns = [[S.sb("vn%d_%d" % (i, q), [128, 2048], BF16) for q in range(2)] for i in range(2)]
            mTs = [S.sb("mT%d" % i, [128, 16, NT], BF16) for i in range(2)]
            sq = S.sb("sq", [128, 8, NT], BF16)
            xn = [S.sb("xn%d" % i, [128, NT]) for i in range(2)]
            rstd = S.sb("rstd", [128, NT])
            vxs = [S.sb("vx%d" % i, [128, 512]) for i in range(2)]
            junk = S.sb("junk", [128, 1024], BF16)
            st = S.sb("st", [128, 16])
            ux = [S.sb("ux%d" % i, [128, NT]) for i in range(2)]
            sx = [S.sb("sx%d" % i, [128, NT]) for i in range(2)]
            tiles = [t for t in self.seq_tiles(NT) if not (t[0] < CT and last)]
            cnt = {"c": 0, "o": 0}

            def stN(s_, b, t0, t1):
                mi = 2 if t0 < CT else b
                n = t1 - t0
                xt = xts[0]; kx = "xt0"
                S.load(xt[:, :, :n], src[1][b, :, :, t0:t1], xkeys(src[0], b, t0, t1), [kx])
                self.norm_mod(xt, n, M[:, l, 0, :, mi], M[:, l, 1, :, mi], hTs[s_], sq, xn, rstd, P[0], kx, "hT%d" % s_, "", kss="pv0")

            def stV(s_, b, t0, t1):
                n = t1 - t0
                hTi = hTs[s_]; kh = "hT%d" % s_
                S.load(xos[s_][:, :, :n], src[1][b, :, :, t0:t1], xkeys(src[0], b, t0, t1), ["xo%d" % s_])
                for q in range(n // 128):
                    for cg in range(4):
                        pv = P[cg % 2]; kpv = "pv%d" % (cg % 2)
                        for k in range(8):
                            S.mm(pv[:, :], hTi[:, k, q * 128:(q + 1) * 128], w_in[:, k, 2048 + cg * 512:2048 + (cg + 1) * 512],
                                 k == 0, k == 7, [kh, "w_in"], [kpv])
                        S.tt(vxs[cg % 2][:], pv[:, :], rep[:, cg * 512:(cg + 1) * 512], ALU.add, [kpv, "rep"], [("vx", cg % 2)])
                        S.act(vt[:, cg * 512:(cg + 1) * 512], vxs[cg % 2][:], AF.Gelu_apprx_tanh,
                              [("vx", cg % 2)], [("vt", cg)], accum_out=st[:, cg:cg + 1])
                    vtk = [("vt", c_) for c_ in range(4)]
                    for hh in range(2):
                        S.act(junk[:], vt[:, hh * 1024:(hh + 1) * 1024], AF.Square, vtk, ["junk"], accum_out=st[:, 11 + hh:12 + hh])
                    S.op("dve", lambda e: e.reduce_sum(out=st[:, 5:6], in_=st[:, 0:4], axis=mybir.AxisListType.X), vtk, ["st5"])
                    S.tt(st[:, 4:5], st[:, 11:12], st[:, 12:13], ALU.add, ["junk"], ["st4"])
                    S.ts(st[:, 6:7], st[:, 5:6], 1.0 / 2048, None, ALU.mult, None, ["st5"], ["st6"])
                    S.tt(st[:, 7:8], st[:, 6:7], st[:, 6:7], ALU.mult, ["st6"], ["st7"])
                    S.stt(st[:, 8:9], st[:, 4:5], 1.0 / 2048, st[:, 7:8], ALU.mult, ALU.subtract, ["st4", "st7"], ["st8"])
                    S.act(st[:, 9:10], st[:, 8:9], AF.Sqrt, ["st8"], ["st9"], bias=EPS, scale=1.0)
                    S.op("dve", lambda e: e.reciprocal(out=st[:, 10:11], in_=st[:, 9:10]), ["st9"], ["st10"])
                    S.ts(vns[s_][q][:], vt[:], st[:, 6:7], st[:, 10:11], ALU.subtract, ALU.mult, vtk + ["st6", "st10"], ["vn%d_%d" % (s_, q)])

            def stU(s_, b, t0, t1):
                n = t1 - t0
                nq = n // 128
                hTi = hTs[s_]; kh = "hT%d" % s_
                mT = mTs[s_]
                for c in range(16):
                    g = c // 2
                    cit = cnt["c"]; cnt["c"] += 1
                    pu = P[2 + cit % 2]; kpu = "pu%d" % (cit % 2)
                    psx = P[4 + cit % 2]; kps = "psx%d" % (cit % 2)
                    uxt = ux[cit % 2]; kux = "ux%d" % (cit % 2)
                    sxt = sx[cit % 2]; ksx = "sx%d" % (cit % 2)
                    for k in range(8):
                        S.mm(pu[:, :n], w_in[:, k, c * 128:(c + 1) * 128], hTi[:, k, :n], k == 0, k == 7, [kh, "w_in"], [kpu])
                    for q in range(nq):
                        S.mm(psx[:, q * 128:(q + 1) * 128], vns[s_][q][:, c * 128:(c + 1) * 128], wsT[:, g, :], True, True,
                             ["vn%d_%d" % (s_, q), "wsT"], [kps])
                    S.act(uxt[:, :n], pu[:, :n], AF.Gelu_apprx_tanh, [kpu, "bin_u"], [kux], bias=bin_u[:, j, c:c + 1])
                    S.stt(sxt[:, :n].rearrange("p (q m) -> p q m", m=128), psx[:, :n].rearrange("p (q m) -> p q m", m=128),
                          vgb[:, j, 0, c:c + 1], CB[:, c, :].rearrange("p (q m) -> p q m", q=1).to_broadcast([128, nq, 128]),
                          ALU.mult, ALU.add, [kps, "vgb", "CB"], [ksx])
                    S.tt(mT[:, c, :n], sxt[:, :n], uxt[:, :n], ALU.mult, [ksx, kux], [("mT%d" % s_, c)])

            def stO(s_, b, t0, t1):
                mi = 2 if t0 < CT else b
                n = t1 - t0
                mT = mTs[s_]
                xo = xos[s_]; kxo = "xo%d" % s_
                for d in range(8):
                    oi = cnt["o"]; cnt["o"] += 1
                    po = P[6 + oi % 2]; kpo = "po%d" % (oi % 2)
                    for c in range(16):
                        S.mm(po[:, :n], w_out[:, c, d * 128:(d + 1) * 128], mT[:, c, :n], c == 0, c == 15, [("mT%d" % s_, c), "w_out"], [kpo])
                    S.stt(xo[:, d, :n], po[:, :n], M[:, l, 2, d, mi:mi + 1], xo[:, d, :n], ALU.mult, ALU.add, [kpo, kxo, "MOD"], [kxo])
                S.dma("pool", lambda e, xo=xo, b=b, t0=t0, t1=t1, n=n: e.dma_start(out=dst[1][b, :, :, t0:t1], in_=xo[:, :, :n]),
                      [kxo], xkeys(dst[0], b, t0, t1))

            for s_ in range(NB):
                stN(s_, s_, *tiles[0])
            for i, (t0, t1) in enumerate(tiles):
                for s_ in range(NB):
                    stV(s_, s_, t0, t1)
                for s_ in range(NB):
                    stU(s_, s_, t0, t1)
                if i + 1 < len(tiles):
                    for s_ in range(NB):
                        stN(s_, s_, *tiles[i + 1])
                for s_ in range(NB):
                    stO(s_, s_, t0, t1)
            S.end_phase()

    def ffn_tiles(self):
        out = []
        out.append((1, CT, 0, [(0, 1, 0, 1)]))
        lat = []
        R = 6
        r = 0
        c_prev = 0
        while r < 64:
            r1 = min(r + R, 64)
            c1 = min(r1 + 1, 64)
            lat.append((c_prev, c1, r, r1))
            c_prev = c1
            r = r1
        out.append((64, 64, CT, lat))
        return out

    def phase_ffn(self, l, src, dst, last):
        S = self.S
        I = self.I
        M = self.MOD
        DG = self.DG
        with contextlib.ExitStack() as pst:
            S.stack = pst
            w_down = S.sb("w_down", [128, NF, 1024], BF16)
            cw = S.sb("cw", [128, 42, 9])
            cb = S.sb("cb", [128, 42])
            ident = S.sb("ident", [128, 128])
            for f in range(NF):
                S.load(w_down[:, f, :], I["ffn_down"][l, :, f, :], [], ["w_down"], q="pool")
            S.load(cw[:], I["ffn_cw"][:, l], [], ["cw"])
            S.load(cb[:], I["ffn_cb"][:, l], [], ["cb"])
            S.load(ident[:], I["ml_cst"][:, 0, :], [], ["ident"])
            dgst = [S.sb("dgst%d" % i, [128, 9, 128], BF16) for i in range(3)]
            for ch in range(42):
                h, f = ch // NF, ch % NF
                t_ = dgst[ch % 3]; kt = "dgst%d" % (ch % 3)
                S.tt(t_[:], ident[:].rearrange("p (t m) -> p t m", t=1).to_broadcast([128, 9, 128]),
                     cw[:, ch, :].rearrange("p (t m) -> p t m", m=1).to_broadcast([128, 9, 128]), ALU.mult, ["ident", "cw"], [kt])
                S.dma("sp", lambda e, t_=t_, f=f, h=h: e.dma_start(out=DG[f, :, h], in_=t_[:]), [kt], [("DG", f)])
            xts = [S.sb("xt%d" % i, [128, 8, 448]) for i in range(2)]
            xos = [S.sb("xo%d" % i, [128, 8, 384]) for i in range(1)]
            sq = S.sb("sq", [128, 8, 448], BF16)
            xn = [S.sb("xn%d" % i, [128, 448]) for i in range(2)]
            rstd = S.sb("rstd", [128, 448])
            hTs = [S.sb("hT%d" % i, [128, 8, 448], BF16) for i in range(2)]
            wups = [S.sb("wup%d" % i, [128, 2, 8, 128], BF16) for i in range(3)]
            dgs = [S.sb("dg%d" % i, [128, 2, 9, 128], BF16) for i in range(3)]
            zbs = [S.sb("zb%d" % i, [128, 2, 768], BF16) for i in range(2)]
            ZS = S.sb("ZS", [128, NF, 2, 128], BF16)
            sg = [S.sb("sg%d" % i, [128, 384]) for i in range(2)]
            aTs = [S.sb("aT%d" % i, [128, NF, 384], BF16) for i in range(2)]
            P = self.psum
            wit = 0
            fit = 0
            work = []
            for b in range(NB):
                for (H, W, s0, tl) in self.ffn_tiles():
                    if s0 == 0 and last:
                        continue
                    for t_ in tl:
                        work.append((b, H, W, s0) + tuple(t_))

            def prep_steps(i):
                b, H, W, s0, c0, c1, or0, or1 = work[i]
                mi = 2 if s0 == 0 else b
                n = (c1 - c0) * W
                t0 = s0 + c0 * W
                xt = xts[i % 2]; kx = "xt%d" % (i % 2)
                S.load(xt[:, :, :n], src[1][b, :, :, t0:t0 + n], xkeys(src[0], b, t0, t0 + n), [kx])
                return self.norm_mod_steps(xt, n, M[:, l, 3, :, mi], M[:, l, 4, :, mi], hTs[i % 2], sq, xn, rstd, P[0], kx,
                                           "hT%d" % (i % 2), "pz0")

            for st_ in prep_steps(0):
                st_()
            for wi in range(len(work)):
                if True:
                    if True:
                        b, H, W, s0, c0, c1, or0, or1 = work[wi]
                        isctx = s0 == 0
                        mi = 2 if isctx else b
                        n = (c1 - c0) * W
                        nout = (or1 - or0) * W
                        t0 = s0 + c0 * W
                        o0 = s0 + or0 * W
                        zr0 = or0 - 1
                        nzr = or1 - or0 + 2
                        xo = xos[0]; kxo = "xo0"
                        hT = hTs[wi % 2]; kh = "hT%d" % (wi % 2)
                        aT = aTs[wi % 2]; ka = "aT%d" % (wi % 2)
                        S.load(xo[:, :, :nout], src[1][b, :, :, o0:o0 + nout], xkeys(src[0], b, o0, o0 + nout), [kxo])

                        def up(f):
                            nonlocal wit
                            wu = wups[wit % 3]; kw = "wup%d" % (wit % 3)
                            dg = dgs[wit % 3]; kd = "dg%d" % (wit % 3)
                            wit += 1
                            S.load(wu[:], self.up_bf[l, f], [], [kw])
                            S.load(dg[:], DG[f], [("DG", f)], [kd])
                            zb = zbs[f % 2]; kzb = "zb%d" % (f % 2)
                            if c0 > 0:
                                S.op("pool", lambda e, zb=zb, f=f: e.tensor_copy(out=zb[:, :, 0:2 * W], in_=ZS[:, f, :, :]), [("ZS", f)], [(kzb, 0)])
                            for h in range(2):
                                pz = P[h]; kz = "pz%d" % h
                                for k in range(8):
                                    S.mm(pz[:, :n], wu[:, h, k, :], hT[:, k, :n], k == 0, k == 7, [kw, kh], [kz])
                                zoff = (c0 - zr0) * W
                                if h == 0:
                                    S.act(zb[:, h, zoff:zoff + n], pz[:, :n], AF.Copy, [kz], [(kzb, 1 + h)])
                                else:
                                    S.op("dve", lambda e, o=zb[:, h, zoff:zoff + n], i_=pz[:, :n]: e.tensor_copy(out=o, in_=i_), [kz], [(kzb, 1 + h)])
                            if c1 < H:
                                soff = (c1 - 2 - zr0) * W
                                S.op("pool", lambda e, zb=zb, f=f, soff=soff: e.tensor_copy(out=ZS[:, f, :, :], in_=zb[:, :, soff:soff + 2 * W]),
                                     [(kzb, 0), (kzb, 1), (kzb, 2)], [("ZS", f)])
                            return (zb, kzb, dg, kd)

                        def conv(f, st):
                            nonlocal fit
                            zb, kzb, dg, kd = st
                            pcs = []
                            for h in range(2):
                                pc = P[2 + 2 * h + fit % 2]; kpc = "pc%d_%d" % (h, fit % 2)
                                zv = zb[:, h, 0:nzr * W].rearrange("p (r w) -> p r w", w=W)
                                pv = pc[:, :nout].rearrange("p (r w) -> p r w", w=W)
                                taps = [(0, 0)] + [(dr, dc) for dr in (-1, 0, 1) for dc in (-1, 0, 1) if not (dr == 0 and dc == 0)]
                                todo = []
                                for (dr, dc) in taps:
                                    rlo = max(or0, -dr); rhi = min(or1, H - dr)
                                    clo = max(0, -dc); chi = min(W, W - dc)
                                    if rlo >= rhi or clo >= chi:
                                        continue
                                    todo.append((dr, dc, rlo, rhi, clo, chi))
                                for ti, (dr, dc, rlo, rhi, clo, chi) in enumerate(todo):
                                    tap = (dr + 1) * 3 + (dc + 1)
                                    S.mm(pv[:, rlo - or0:rhi - or0, clo:chi], dg[:, h, tap, :],
                                         zv[:, rlo + dr - zr0:rhi + dr - zr0, clo + dc:chi + dc], ti == 0, ti == len(todo) - 1,
                                         [kd, (kzb, 0), (kzb, 1 + h)], [kpc])
                                pcs.append((pc, kpc))
                            sgt = sg[fit % 2]; ksg = "sg%d" % (fit % 2)
                            S.act(sgt[:, :nout], pcs[0][0][:, :nout], AF.Silu, [pcs[0][1], "cb"], [ksg], bias=cb[:, f:f + 1])
                            S.stt(aT[:, f, :nout], pcs[1][0][:, :nout], cb[:, NF + f:NF + f + 1], sgt[:, :nout], ALU.add, ALU.mult,
                                  [pcs[1][1], "cb", ksg], [(ka, f)])
                            fit += 1

                        prev = None
                        for f in range(NF):
                            st = up(f)
                            if prev is not None:
                                conv(f - 1, prev)
                            prev = st
                        conv(NF - 1, prev)
                        nsteps = prep_steps(wi + 1) if wi + 1 < len(work) else []
                        for d in range(8):
                            if d % 2 == 0 and d // 2 < len(nsteps):
                                nsteps[d // 2]()
                            po = P[6 + d % 2]; kpo = "po%d" % (d % 2)
                            for f in range(NF):
                                S.mm(po[:, :nout], w_down[:, f, d * 128:(d + 1) * 128], aT[:, f, :nout], f == 0, f == NF - 1,
                                     [(ka, f), "w_down"], [kpo])
                            S.stt(xo[:, d, :nout], po[:, :nout], M[:, l, 5, d, mi:mi + 1], xo[:, d, :nout], ALU.mult, ALU.add,
                                  [kpo, kxo, "MOD"], [kxo])
                        S.dma("pool", lambda e, xo=xo, b=b, o0=o0, nout=nout: e.dma_start(out=dst[1][b, :, :, o0:o0 + nout], in_=xo[:, :, :nout]),
                              [kxo], xkeys(dst[0], b, o0, o0 + nout))
            S.end_phase()

    def phase_lru(self, l, src, dst):
        S = self.S
        I = self.I
        M = self.MOD
        P = self.psum
        XR, YG, MT = self.XR, self.YG, self.MT
        tiles = self.seq_tiles(512)
        with contextlib.ExitStack() as pst:
            S.stack = pst
            w_in = S.sb("lru_w_in", [128, 8, 2048], BF16)
            for k in range(8):
                S.load(w_in[:, k, :], I["lru_in"][:, k, :], [], ["w_in"], q="pool")
            self.cast_up(l)
            xts = [S.sb("xt%d" % i, [128, 8, 512]) for i in range(2)]
            sq = S.sb("sq", [128, 8, 512], BF16)
            xn = [S.sb("xn%d" % i, [128, 512]) for i in range(2)]
            rstd = S.sb("rstd", [128, 512])
            hT = S.sb("hT", [128, 8, 512], BF16)
            ygs = [S.sb("yg%d" % i, [128, 8, 512]) for i in range(2)]
            xrs = [S.sb("xr%d" % i, [128, 8, 512]) for i in range(2)]
            it = 0
            pit = 0
            for b in range(NB):
                for (t0, t1) in tiles:
                    mi = 2 if t0 < CT else b
                    n = t1 - t0
                    xt = xts[it % 2]; kx = "xt%d" % (it % 2)
                    ygt = ygs[it % 2]; kyg = "yg%d" % (it % 2)
                    xrt = xrs[it % 2]; kxr = "xr%d" % (it % 2)
                    it += 1
                    S.load(xt[:, :, :n], src[1][b, :, :, t0:t1], xkeys(src[0], b, t0, t1), [kx])
                    self.norm_mod(xt, n, M[:, l, 0, :, mi], M[:, l, 1, :, mi], hT, sq, xn, rstd, P[7], kx, "hT", "")
                    for c in range(16):
                        pp = P[pit % 4]; kp = "pp%d" % (pit % 4); pit += 1
                        for k in range(8):
                            S.mm(pp[:, :n], w_in[:, k, c * 128:(c + 1) * 128], hT[:, k, :n], k == 0, k == 7, ["hT", "w_in"], [kp])
                        if c < 8:
                            S.act(ygt[:, c, :n], pp[:, :n], AF.Gelu_apprx_tanh, [kp], [kyg])
                        else:
                            S.op("dve", lambda e, o=xrt[:, c - 8, :n], i_=pp[:, :n]: e.tensor_copy(out=o, in_=i_), [kp], [kxr])
                    S.dma("pool", lambda e, t=ygt, b=b, t0=t0, t1=t1, n=n: e.dma_start(out=YG[b, :, :, t0:t1], in_=t[:, :, :n]),
                          [kyg], xkeys("YG", b, t0, t1))
                    S.dma("pool", lambda e, t=xrt, b=b, t0=t0, t1=t1, n=n: e.dma_start(out=XR[b, :, :, t0:t1], in_=t[:, :, :n]),
                          [kxr], xkeys("XR", b, t0, t1))
            S.end_phase()
        with contextlib.ExitStack() as pst:
            S.stack = pst
            gw = S.sb("lru_gw", [128, 2, 2, 4, 2, 256], BF16)
            for d in range(2):
                for g in range(2):
                    S.load(gw[:, d, g], I["lru_gw"][:, d, g], [], ["gw"], q="pool")
            vec = S.sb("lru_vec", [128, 8, 11])
            S.load(vec[:], I["lru_vec"], [], ["vec"])
            c8 = S.sb("c8", [128, 8, 2])
            S.act(c8[:], vec[:, :, 9:11], AF.Exp, ["vec"], ["c8"], scale=-1.0)
            S.act(c8[:], c8[:], AF.Ln, ["c8"], ["c8"], bias=1.0)
            S.ts(c8[:], c8[:], -8.0, None, ALU.mult, None, ["c8"], ["c8"])
            xraw = [S.sb("xraw%d" % i, [128, 2, 515]) for i in range(2)]
            XC = S.sb("XC", [128, 2, T])
            XCb = S.sb("XCb", [128, 2, T], BF16)
            A = S.sb("A", [128, T])
            Bt = S.sb("Bt", [128, T])
            Gi = S.sb("Gi", [128, T])
            Hf = S.sb("Hf", [128, T])
            Hb = S.sb("Hb", [128, T])
            Y = S.sb("Y", [128, T])
            mo = S.sb("mo", [128, T], BF16)
            rit = 0
            pit = 0
            for b in range(NB):
                for hd in range(4):
                    for (t0, t1) in tiles:
                        s0, s1 = (0, CT) if t0 < CT else (CT, T)
                        n = t1 - t0
                        lo = max(t0 - 2, s0); hi = min(t1 + 1, s1)
                        xr = xraw[rit % 2]; kr = "xraw%d" % (rit % 2); rit += 1
                        S.op("pool", lambda e, xr=xr: e.memset(xr[:], 0.0), [], [kr])
                        S.load(xr[:, :, lo - (t0 - 2):hi - (t0 - 2)], XR[b, :, 2 * hd:2 * hd + 2, lo:hi], xkeys("XR", b, lo, hi), [kr])
                        for jj in range(2):
                            ch = 2 * hd + jj
                            S.act(XC[:, jj, t0:t1], xr[:, jj, 2:2 + n], AF.Identity, [kr, "vec"], [("XC", jj)],
                                  bias=vec[:, ch, 4:5], scale=vec[:, ch, 2:3])
                            for k_, off in ((0, -2), (1, -1), (3, 1)):
                                S.stt(XC[:, jj, t0:t1], xr[:, jj, 2 + off:2 + off + n], vec[:, ch, k_:k_ + 1], XC[:, jj, t0:t1],
                                      ALU.mult, ALU.add, [kr, "vec", ("XC", jj)], [("XC", jj)])
                    S.op("dve", lambda e: e.tensor_copy(out=XCb[:], in_=XC[:]), [("XC", 0), ("XC", 1)], ["XCb"])
                    for jj in range(2):
                        ch = 2 * hd + jj
                        S.load(Y[:], YG[b, :, ch, :], xkeys("YG", b, 0, T), ["Y"])
                        for d in range(2):
                            for (t0, t1) in tiles:
                                n = t1 - t0
                                pr = P[pit % 2]; kpr = "pr%d" % (pit % 2)
                                pi = P[2 + pit % 2]; kpi = "pi%d" % (pit % 2)
                                pit += 1
                                for ii in range(2):
                                    S.mm(pr[:, :n], gw[:, d, 0, hd, ii, jj * 128:(jj + 1) * 128], XCb[:, ii, t0:t1], ii == 0, ii == 1,
                                         ["gw", "XCb"], [kpr])
                                for ii in range(2):
                                    S.mm(pi[:, :n], gw[:, d, 1, hd, ii, jj * 128:(jj + 1) * 128], XCb[:, ii, t0:t1], ii == 0, ii == 1,
                                         ["gw", "XCb"], [kpi])
                                S.act(A[:, t0:t1], pr[:, :n], AF.Sigmoid, [kpr, "vec"], ["A"], bias=vec[:, ch, 5 + d:6 + d])
                                S.act(Gi[:, t0:t1], pi[:, :n], AF.Sigmoid, [kpi, "vec"], ["Gi"], bias=vec[:, ch, 7 + d:8 + d])
                            S.act(A[:], A[:], AF.Exp, ["A", "c8"], ["A"], scale=c8[:, ch, d:d + 1])
                            S.tt(Bt[:], A[:], A[:], ALU.mult, ["A"], ["Bt"])
                            S.act(Bt[:], Bt[:], AF.Sqrt, ["Bt"], ["Bt"], bias=1.0, scale=-1.0)
                            S.tt(Gi[:], Gi[:], XC[:, jj, :], ALU.mult, ["Gi", ("XC", jj)], ["Gi"])
                            S.tt(Bt[:], Bt[:], Gi[:], ALU.mult, ["Bt", "Gi"], ["Bt"])
                            if d == 0:
                                S.op("dve", lambda e: e.tensor_tensor_scan(out=Hf[:], data0=A[:], data1=Bt[:], initial=0.0,
                                                                           op0=ALU.mult, op1=ALU.add), ["A", "Bt"], ["Hf"])
                            else:
                                S.op("dve", lambda e: e.tensor_tensor_scan(out=Hb[:, 0:CT][:, ::-1], data0=A[:, 0:CT][:, ::-1],
                                                                           data1=Bt[:, 0:CT][:, ::-1], initial=0.0,
                                                                           op0=ALU.mult, op1=ALU.add), ["A", "Bt"], ["Hb"])
                                S.op("dve", lambda e: e.tensor_tensor_scan(out=Hb[:, CT:T][:, ::-1], data0=A[:, CT:T][:, ::-1],
                                                                           data1=Bt[:, CT:T][:, ::-1], initial=Hb[:, 0:1],
                                                                           op0=ALU.mult, op1=ALU.add), ["A", "Bt", "Hb"], ["Hb"])
                        S.tt(Hf[:], Hf[:], Hb[:], ALU.add, ["Hf", "Hb"], ["Hf"])
                        S.tt(mo[:], Hf[:], Y[:], ALU.mult, ["Hf", "Y"], ["mo"])
                        S.dma("pool", lambda e, b=b, ch=ch: e.dma_start(out=MT[b, :, ch, :], in_=mo[:]), ["mo"], xkeys("MT", b, 0, T))
            S.end_phase()
        with contextlib.ExitStack() as pst:
            S.stack = pst
            w_out = S.sb("lru_w_out", [128, 8, 1024], BF16)
            for k in range(8):
                S.load(w_out[:, k, :], I["lru_out"][:, k, :], [], ["w_out"], q="pool")
            self.out_proj(l, src, dst, MT, "MT", w_out, 8, tiles)
            S.end_phase()

    def out_proj(self, l, src, dst, MTd, mname, w_out, nk, tiles):
        S = self.S
        M = self.MOD
        P = self.psum
        xts = [S.sb("xt%d" % i, [128, 8, 512]) for i in range(2)]
        mts = [S.sb("mt%d" % i, [128, nk, 512], BF16) for i in range(2)]
        it = 0
        pit = 0
        for b in range(NB):
            for (t0, t1) in tiles:
                mi = 2 if t0 < CT else b
                n = t1 - t0
                xt = xts[it % 2]; kx = "xt%d" % (it % 2)
                mt = mts[it % 2]; km = "mt%d" % (it % 2)
                it += 1
                S.load(xt[:, :, :n], src[1][b, :, :, t0:t1], xkeys(src[0], b, t0, t1), [kx])
                S.load(mt[:, :, :n], MTd[b, :, :, t0:t1], xkeys(mname, b, t0, t1), [km])
                for d in range(8):
                    po = P[pit % 4]; kpo = "po%d" % (pit % 4); pit += 1
                    for k in range(nk):
                        S.mm(po[:, :n], w_out[:, k, d * 128:(d + 1) * 128], mt[:, k, :n], k == 0, k == nk - 1, [km, "w_out"], [kpo])
                    S.stt(xt[:, d, :n], po[:, :n], M[:, l, 2, d, mi:mi + 1], xt[:, d, :n], ALU.mult, ALU.add, [kpo, kx, "MOD"], [kx])
                S.dma("pool", lambda e, xt=xt, b=b, t0=t0, t1=t1, n=n: e.dma_start(out=dst[1][b, :, :, t0:t1], in_=xt[:, :, :n]),
                      [kx], xkeys(dst[0], b, t0, t1))

    def phase_mlstm(self, l, src, dst):
        S = self.S
        I = self.I
        M = self.MOD
        P = self.psum
        QT, KT, KTOK, VTOK, OTOK, GT, HF, MT = self.QT, self.KT, self.KTOK, self.VTOK, self.OTOK, self.GT, self.HF, self.MT
        tiles = self.seq_tiles(512)
        NCH = T // 128
        with contextlib.ExitStack() as pst:
            S.stack = pst
            w_in = S.sb("ml_w_in", [128, 8, 3088], BF16)
            for k in range(8):
                S.load(w_in[:, k, 0:2048], I["ml_in"][:, k, 0:2048], [], ["w_in"], q="pool")
                S.load(w_in[:, k, 2048:3088], I["ml_in"][:, k, 2048:3088], [], ["w_in"], q="pool")
            self.cast_up(l)
            bg = S.sb("ml_bg", [128, 16])
            S.load(bg[:], I["ml_bg"], [], ["bg"])
            xts = [S.sb("xt%d" % i, [128, 8, 512]) for i in range(2)]
            sq = S.sb("sq", [128, 8, 512], BF16)
            xn = [S.sb("xn%d" % i, [128, 512]) for i in range(2)]
            rstd = S.sb("rstd", [128, 512])
            hT = S.sb("hT", [128, 8, 512], BF16)
            qk = [S.sb("qk%d" % i, [128, 8, 512]) for i in range(2)]
            tok = [S.sb("tok%d" % i, [128, 2560]) for i in range(2)]
            gt = [S.sb("gt%d" % i, [128, 16]) for i in range(2)]
            gtmp = S.sb("gtmp", [128, 2, 4])
            it = 0
            pit = 0
            qit = 0
            for b in range(NB):
                for (t0, t1) in tiles:
                    mi = 2 if t0 < CT else b
                    n = t1 - t0
                    xt = xts[it % 2]; kx = "xt%d" % (it % 2)
                    qkt = qk[it % 2]; kqk = "qk%d" % (it % 2)
                    it += 1
                    S.load(xt[:, :, :n], src[1][b, :, :, t0:t1], xkeys(src[0], b, t0, t1), [kx])
                    self.norm_mod(xt, n, M[:, l, 0, :, mi], M[:, l, 1, :, mi], hT, sq, xn, rstd, P[7], kx, "hT", "")
                    for c in range(8):
                        pp = P[pit % 4]; kp = "pp%d" % (pit % 4); pit += 1
                        for k in range(8):
                            S.mm(pp[:, :n], w_in[:, k, c * 128:(c + 1) * 128], hT[:, k, :n], k == 0, k == 7, ["hT", "w_in"], [kp])
                        S.act(qkt[:, c, :n], pp[:, :n], AF.Copy, [kp], [kqk], scale=(128.0 ** -0.5 if c < 4 else 1.0))
                    S.dma("pool", lambda e, t=qkt, b=b, t0=t0, t1=t1, n=n: e.dma_start(out=QT[b, :, :, t0:t1], in_=t[:, 0:4, :n]),
                          [kqk], xkeys("QT", b, t0, t1))
                    S.dma("pool", lambda e, t=qkt, b=b, t0=t0, t1=t1, n=n: e.dma_start(out=KT[b, :, :, t0:t1], in_=t[:, 4:8, :n]),
                          [kqk], xkeys("KT", b, t0, t1))
                    for q in range(n // 128):
                        ci = (t0 + q * 128) // 128
                        tk = tok[qit % 2]; ktk = "tok%d" % (qit % 2)
                        g_ = gt[qit % 2]; kg = "gt%d" % (qit % 2)
                        qit += 1
                        for grp in range(5):
                            pp = P[pit % 4]; kp = "pp%d" % (pit % 4); pit += 1
                            for k in range(8):
                                S.mm(pp[:, :], hT[:, k, q * 128:(q + 1) * 128], w_in[:, k, 512 + grp * 512:512 + (grp + 1) * 512],
                                     k == 0, k == 7, ["hT", "w_in"], [kp])
                            if grp < 3:
                                if grp % 2 == 0:
                                    S.act(tk[:, grp * 512:(grp + 1) * 512], pp[:, :], AF.Copy, [kp], [(ktk, grp)])
                                else:
                                    S.op("dve", lambda e, o=tk[:, grp * 512:(grp + 1) * 512], i_=pp[:, :]: e.tensor_copy(out=o, in_=i_), [kp], [(ktk, grp)])
                            else:
                                S.act(tk[:, grp * 512:(grp + 1) * 512], pp[:, :], AF.Sigmoid, [kp], [(ktk, grp)])
                        pp = P[4 + qit % 2]; kp = "pg%d" % (qit % 2)
                        for k in range(8):
                            S.mm(pp[:, 0:16], hT[:, k, q * 128:(q + 1) * 128], w_in[:, k, 3072:3088], k == 0, k == 7, ["hT", "w_in"], [kp])
                        S.tt(g_[:], pp[:, 0:16], bg[:], ALU.add, [kp, "bg"], [kg])
                        gv = g_[:].rearrange("p (d g h) -> p d g h", d=2, g=2)
                        S.act(gtmp[:], gv[:, :, 1, :], AF.Exp, [kg], ["gtmp"], scale=-1.0)
                        S.act(gtmp[:], gtmp[:], AF.Ln, ["gtmp"], ["gtmp"], bias=1.0)
                        S.ts(gv[:, :, 1, :], gtmp[:], -1.0, None, ALU.mult, None, ["gtmp", kg], [kg])
                        S.dma("pool", lambda e, tk=tk, b=b, ci=ci: e.dma_start(out=KTOK[b, ci], in_=tk[:, 0:512]), [(ktk, 0)], [("KTOK", b, ci)])
                        S.dma("pool", lambda e, tk=tk, b=b, ci=ci: e.dma_start(out=VTOK[b, ci], in_=tk[:, 512:1536]), [(ktk, 1), (ktk, 2)], [("VTOK", b, ci)])
                        S.dma("pool", lambda e, tk=tk, b=b, ci=ci: e.dma_start(out=OTOK[b, ci], in_=tk[:, 1536:2560]), [(ktk, 3), (ktk, 4)], [("OTOK", b, ci)])
                        S.dma("pool", lambda e, g_=g_, b=b, ci=ci: e.dma_start(out=GT[b, ci], in_=g_[:]), [kg], [("GT", b, ci)])
            S.end_phase()
        with contextlib.ExitStack() as pst:
            S.stack = pst
            cst = S.sb("ml_cst", [128, 4, 128])
            S.load(cst[:], I["ml_cst"], [], ["cst"])
            ident_bf = S.sb("ident_bf", [128, 128], BF16)
            S.op("dve", lambda e: e.tensor_copy(out=ident_bf[:], in_=cst[:, 0, :]), ["cst"], ["ident_bf"])
            ngt = S.sb("ml_ng", [128, 1024])
            S.load(ngt[:], I["ml_ng"], [], ["ngt"])
            qTs = [S.sb("qT%d" % i, [128, 4, 128]) for i in range(4)]
            kTs = [S.sb("kT%d" % i, [128, 4, 128]) for i in range(4)]
            kts = [S.sb("ktok%d" % i, [128, 512]) for i in range(4)]
            vts = [S.sb("vtok%d" % i, [128, 1024]) for i in range(4)]
            gs = [S.sb("g%d" % i, [128, 16]) for i in range(4)]
            hfs = [S.sb("hf%d" % i, [128, 1024]) for i in range(4)]
            ots = [S.sb("ot%d" % i, [128, 1024]) for i in range(4)]
            kTbs = [S.sb("kTb%d" % i, [128, 4, 128], BF16) for i in range(NB)]
            LFbs = [S.sb("LFb%d" % i, [128, 4, 128]) for i in range(NB)]
            Erows = [S.sb("Erow%d" % i, [128, 4, 128]) for i in range(NB)]
            sms = [S.sb("sm%d" % i, [128, 8, 4]) for i in range(NB)]
            qss = [S.sb("qs%d" % i, [128, 4, 128], BF16) for i in range(NB)]
            kss = [S.sb("ks%d" % i, [128, 4, 128], BF16) for i in range(NB)]
            vexts = [S.sb("vext%d" % i, [128, 4, 260], BF16) for i in range(NB)]
            STss = [S.sb("STs%d" % i, [128, 4, 128], BF16) for i in range(NB)]
            STfs = [S.sb("STf%d" % i, [128, 4, 128]) for i in range(NB)]
            Cs = [S.sb("C%d" % i, [128, 4, 260]) for i in range(NB)]
            Cbfs = [S.sb("Cbf%d" % i, [128, 4, 260], BF16) for i in range(NB)]
            Hds = [S.sb("Hd%d" % i, [128, 1024]) for i in range(NB)]
            dns = [S.sb("dn%d" % i, [128, 4, 2]) for i in range(NB)]
            sss = [S.sb("ss%d" % i, [128, 8]) for i in range(NB)]
            junks = [S.sb("junk%d" % i, [128, 256], BF16) for i in range(NB)]
            mtoks = [S.sb("mtok%d" % i, [128, 1024], BF16) for i in range(NB)]
            mTts = [S.sb("mTt%d" % i, [128, 8, 128], BF16) for i in range(NB)]
            for b in range(NB):
                S.op("dve", lambda e, b=b: e.memset(vexts[b][:], 1.0), [], ["vext%d" % b])
            its = [0, 0]
            for d in range(2):
                order = list(range(NCH)) if d == 0 else [1, 0] + list(range(NCH - 1, 1, -1))
                tri = cst[:, 1 + d, :]
                for b in range(NB):
                    S.op("dve", lambda e, b=b: e.memset(Cs[b][:], 0.0), [], [("C%d" % b, hd) for hd in range(4)])
                    S.op("pool", lambda e, b=b: e.memset(Cbfs[b][:], 0.0), [], [("Cbf%d" % b, hd) for hd in range(4)])
                for ci in order:
                    for b in range(NB):
                        B_ = "%d" % b
                        kTb, LFb, Erow, sm, qs, ks, vext, STs = kTbs[b], LFbs[b], Erows[b], sms[b], qss[b], kss[b], vexts[b], STss[b]
                        STf = STfs[b]
                        C, Cbf, Hdt, dn, ss, junk, mtok, mT_ = Cs[b], Cbfs[b], Hds[b], dns[b], sss[b], junks[b], mtoks[b], mTts[b]
                        pbc = P[0][:, b * 8:b * 8 + 8]
                        pbrow = P[1 + b]
                        pstp = P[1 + b]
                        tk0 = ci * 128
                        sl = 2 * b + its[b] % 2; its[b] += 1
                        qT, kT, kt, vt, g_ = qTs[sl], kTs[sl], kts[sl], vts[sl], gs[sl]
                        kq, kk, kkt, kvt, kg = "qT%d" % sl, "kT%d" % sl, "ktok%d" % sl, "vtok%d" % sl, "g%d" % sl
                        S.load(qT[:], QT[b, :, :, tk0:tk0 + 128], xkeys("QT", b, tk0, tk0 + 128), [kq])
                        S.load(kT[:], KT[b, :, :, tk0:tk0 + 128], xkeys("KT", b, tk0, tk0 + 128), [kk])
                        S.load(kt[:], KTOK[b, ci], [("KTOK", b, ci)], [kkt])
                        S.load(vt[:], VTOK[b, ci], [("VTOK", b, ci)], [kvt])
                        S.load(g_[:], GT[b, ci], [("GT", b, ci)], [kg])
                        if d == 1:
                            hf, ot = hfs[sl], ots[sl]
                            khf, kot = "hf%d" % sl, "ot%d" % sl
                            S.load(hf[:], HF[b, ci], [("HF", b, ci)], [khf])
                            S.load(ot[:], OTOK[b, ci], [("OTOK", b, ci)], [kot])
                        ig = g_[:, d * 8:d * 8 + 4]
                        lf = g_[:, d * 8 + 4:d * 8 + 8]
                        S.mm(pbc[:, 0:4], tri, lf, True, True, ["cst", kg], ["bc"])
                        S.mm(pbc[:, 4:8], cst[:, 3, :], lf, True, True, ["cst", kg], ["bc"])
                        S.op("dve", lambda e, lf=lf, LFb=LFb: e.tensor_copy(out=LFb[:], in_=lf.rearrange("p (h n) -> p h n", n=1).to_broadcast([128, 4, 128])),
                             [kg], ["LFb" + B_])
                        for hd in range(4):
                            S.mm(pbrow[:, hd * 128:(hd + 1) * 128], LFb[:, hd, :], tri, True, True, ["LFb" + B_, "cst"], ["bst" + B_])
                        S.act(Erow[:], pbrow[:, :].rearrange("p (h n) -> p h n", h=4), AF.Exp, ["bst" + B_], ["Erow" + B_, "bst" + B_])
                        S.tt(sm[:, 0, :], ig, pbc[:, 0:4], ALU.subtract, [kg, "bc"], ["sm0" + B_, "bc"])
                        S.act(sm[:, 1, :], sm[:, 0, :], AF.Exp, ["sm0" + B_], ["ek" + B_])
                        S.act(sm[:, 2, :], pbc[:, 4:8], AF.Exp, ["bc"], ["ebl" + B_, "bc"])
                        S.tt(qs[:], qT[:], Erow[:], ALU.mult, [kq, "Erow" + B_], ["qs" + B_])
                        S.op("pool", lambda e, kT=kT, kTb=kTb: e.tensor_copy(out=kTb[:], in_=kT[:]), [kk], ["kTb" + B_])
                        S.tt(ks[:], kt[:].rearrange("p (h n) -> p h n", h=4),
                             sm[:, 1, :].rearrange("p (h n) -> p h n", n=1).to_broadcast([128, 4, 128]), ALU.mult, [kkt, "ek" + B_], ["ks" + B_])
                        S.op("pool", lambda e, vt=vt, vext=vext: e.tensor_copy(out=vext[:, :, 0:256], in_=vt[:].rearrange("p (h n) -> p h n", h=4)),
                             [kvt], ["vext" + B_])
                        for hd in range(4):
                            S.mm(pstp[:, hd * 128:(hd + 1) * 128], kTb[:, hd, :], qs[:, hd, :], True, True, ["kTb" + B_, "qs" + B_], ["bst" + B_])
                        S.tt(STf[:], pstp[:, :].rearrange("p (h n) -> p h n", h=4),
                             sm[:, 1, :].rearrange("p (h n) -> p h n", n=1).to_broadcast([128, 4, 128]), ALU.mult, ["bst" + B_, "ek" + B_], ["STf" + B_, "bst" + B_])
                        S.tt(STs[:], STf[:], tri.rearrange("p (h n) -> p h n", h=1).to_broadcast([128, 4, 128]), ALU.mult,
                             ["STf" + B_, "cst"], [("STs" + B_, hd) for hd in range(4)])
                        pden = P[0][:, 16 + b * 4:16 + b * 4 + 4]
                        for hd in range(4):
                            S.mm(pden[:, hd:hd + 1], qs[:, hd, :], Cbf[:, hd, 256:257], True, False, ["qs" + B_, ("Cbf" + B_, hd)], ["bc"])
                            S.mm(pden[:, hd:hd + 1], STs[:, hd, :], vext[:, hd, 256:257], False, True, [("STs" + B_, hd), "vext" + B_], ["bc"])
                        S.act(dn[:, :, 0], pden, AF.Abs, ["bc"], ["dn" + B_, "bc"])
                        S.ts(dn[:, :, 0], dn[:, :, 0], 1.0, None, ALU.max, None, ["dn" + B_], ["dn" + B_])
                        S.op("dve", lambda e, dn=dn: e.reciprocal(out=dn[:, :, 1], in_=dn[:, :, 0]), ["dn" + B_], ["dn" + B_])
                        khd = "Hd" + B_
                        for hd in range(4):
                            nh = P[3 + hd % 2]; knh = "nh%d" % (hd % 2)
                            up = P[5 + hd % 2]; kup = "up%d" % (hd % 2)
                            S.mm(nh[:, 0:257], qs[:, hd, :], Cbf[:, hd, 0:257], True, False, ["qs" + B_, ("Cbf" + B_, hd)], [knh])
                            S.mm(nh[:, 0:257], STs[:, hd, :], vext[:, hd, 0:257], False, True, [("STs" + B_, hd), "vext" + B_], [knh])
                            S.mm(up[:, 0:257], ks[:, hd, :], vext[:, hd, 0:257], True, True, ["ks" + B_, "vext" + B_], [kup])
                            S.ts(Hdt[:, hd * 256:(hd + 1) * 256], nh[:, 0:256], dn[:, hd, 1:2], None, ALU.mult, None, [knh, "dn" + B_], [(khd, hd)])
                            S.tt(C[:, hd, 0:257], C[:, hd, 0:257], up[:, 0:257], ALU.add, [("C" + B_, hd), kup], [("C" + B_, hd)])
                            S.act(C[:, hd, 0:257], C[:, hd, 0:257], AF.Copy, [("C" + B_, hd), "ebl" + B_], [("C" + B_, hd)], scale=sm[:, 2, hd:hd + 1])
                            S.op("pool", lambda e, hd=hd, C=C, Cbf=Cbf: e.tensor_copy(out=Cbf[:, hd, 0:257], in_=C[:, hd, 0:257]), [("C" + B_, hd)], [("Cbf" + B_, hd)])
                        hkeys = [(khd, hd) for hd in range(4)]
                        if d == 0:
                            S.dma("pool", lambda e, Hdt=Hdt, b=b, ci=ci: e.dma_start(out=HF[b, ci], in_=Hdt[:]), hkeys, [("HF", b, ci)])
                        else:
                            S.tt(Hdt[:], Hdt[:], hf[:], ALU.add, hkeys + [khf], hkeys)
                            for hd in range(4):
                                S.act(junk[:], Hdt[:, hd * 256:(hd + 1) * 256], AF.Square, [(khd, hd)], ["junk" + B_], accum_out=ss[:, hd:hd + 1])
                            S.act(ss[:, 4:8], ss[:, 0:4], AF.Sqrt, ["junk" + B_], ["ss4" + B_], bias=EPS, scale=1.0 / 256)
                            S.op("dve", lambda e, ss=ss: e.reciprocal(out=ss[:, 4:8], in_=ss[:, 4:8]), ["ss4" + B_], ["ss4" + B_])
                            S.tt(Hdt[:].rearrange("p (h n) -> p h n", h=4), Hdt[:].rearrange("p (h n) -> p h n", h=4),
                                 ss[:, 4:8].rearrange("p (h n) -> p h n", n=1).to_broadcast([128, 4, 256]), ALU.mult, hkeys + ["ss4" + B_], hkeys)
                            S.tt(Hdt[:], Hdt[:], ngt[:], ALU.mult, hkeys + ["ngt"], hkeys)
                            S.tt(mtok[:], Hdt[:], ot[:], ALU.mult, hkeys + [kot], ["mtok" + B_])
                            kmt = "mTt" + B_
                            for half in range(2):
                                for c4 in range(4):
                                    c = half * 4 + c4
                                    S.mm(P[7][:, c4 * 128:(c4 + 1) * 128], mtok[:, c * 128:(c + 1) * 128], ident_bf[:], True, True,
                                         ["mtok" + B_, "ident_bf"], ["trp"])
                                S.act(mT_[:, half * 4:half * 4 + 4, :], P[7][:, :].rearrange("p (c n) -> p c n", c=4), AF.Copy, ["trp"], [kmt])
                            S.dma("pool", lambda e, mT_=mT_, b=b, tk0=tk0: e.dma_start(out=MT[b, :, :, tk0:tk0 + 128], in_=mT_[:]),
                                  [kmt], xkeys("MT", b, tk0, tk0 + 128))
            S.end_phase()
        with contextlib.ExitStack() as pst:
            S.stack = pst
            w_out = S.sb("ml_w_out", [128, 8, 1024], BF16)
            for k in range(8):
                S.load(w_out[:, k, :], I["ml_out"][:, k, :], [], ["w_out"], q="pool")
            self.out_proj(l, src, dst, MT, "MT", w_out, 8, tiles)
            S.end_phase()

    def phase_final(self, cur):
        S = self.S
        I = self.I
        with contextlib.ExitStack() as pst:
            S.stack = pst
            fg = S.sb("fg", [128, 8])
            S.load(fg[:], I["final_g"], [], ["fg"])
            xts = [S.sb("xt%d" % i, [128, 8, 512]) for i in range(2)]
            sq = S.sb("sq", [128, 8, 512], BF16)
            xn = S.sb("xn", [128, 8, 512])
            rstd = S.sb("rstd", [128, 512])
            xo = [S.sb("xo%d" % i, [128, 8, 512]) for i in range(2)]
            it = 0
            for b in range(NB):
                for i in range(LT // 512):
                    t0 = CT + i * 512
                    n = 512
                    xt = xts[it % 2]; kx = "xt%d" % (it % 2)
                    xot = xo[it % 2]; kxo = "xo%d" % (it % 2)
                    it += 1
                    S.load(xt[:], cur[1][b, :, :, t0:t0 + n], xkeys(cur[0], b, t0, t0 + n), [kx])
                    ssps = self.psum[it % 2]; kss = "ssps%d" % (it % 2)
                    S.act(sq[:], xt[:], AF.Square, [kx], ["sq"])
                    for c in range(8):
                        S.mm(ssps[:, :n], self.ones_bf[:], sq[:, c, :n], c == 0, c == 7, ["sq", "ones_bf"], [kss])
                    S.act(rstd[:], ssps[:, :n], AF.Sqrt, [kss], ["rstd"], bias=EPS, scale=1.0 / D)
                    S.op("dve", lambda e: e.reciprocal(out=rstd[:], in_=rstd[:]), ["rstd"], ["rstd"])
                    S.tt(xn[:], xt[:], rstd[:].rearrange("p (c n) -> p c n", c=1).to_broadcast([128, 8, n]), ALU.mult, [kx, "rstd"], ["xn"])
                    S.tt(xot[:], xn[:], fg[:].rearrange("p (c n) -> p c n", n=1).to_broadcast([128, 8, n]), ALU.mult, ["xn", "fg"], [kxo])
                    S.dma("pool", lambda e, xot=xot, b=b, i=i: e.dma_start(out=self.outT[b, :, :, i * 512:(i + 1) * 512], in_=xot[:]),
                          [kxo], [("out", b, i)])
            S.end_phase()


INPUT_SHAPES = {
    "xin": (NB, 128, 8, T),
    "cond": (128, 8, 4),
    "mod_w": (4, 128, 8, 6144),
    "mod_b": (128, 4, 48),
    "norm_g": (128, 4, 2, 8),
    "final_g": (128, 8),
    "ffn_up": (4, NF, 128, 2, 8, 128),
    "ffn_down": (4, 128, NF, 1024),
    "ffn_cw": (128, 4, 42, 9),
    "ffn_cb": (128, 4, 42),
    "cm_in": (2, 128, 8, 4096),
    "cm_out": (2, 128, 16, 1024),
    "cm_wsT": (2, 128, 8, 128),
    "cm_bin_u": (128, 2, 16),
    "cm_rep": (2, 3, 128, 2048),
    "cm_vgb": (128, 2, 2, 16),
    "cm_bs": (2, 128, 8, 128),
    "lru_in": (128, 8, 2048),
    "lru_out": (128, 8, 1024),
    "lru_gw": (128, 2, 2, 4, 2, 256),
    "lru_vec": (128, 8, 11),
    "ml_in": (128, 8, 3088),
    "ml_out": (128, 8, 1024),
    "ml_bg": (128, 16),
    "ml_ng": (128, 1024),
    "ml_cst": (128, 4, 128),
}


def fm(v, nch):
    v = np.asarray(v, np.float32)
    lead = v.shape[:-1]
    a = v.reshape(lead + (nch, 128))
    a = np.moveaxis(a, -1, 0)
    return np.ascontiguousarray(a)


def layout_shared(inp):
    f32 = np.float32
    W = {}
    W["mod_w"] = np.ascontiguousarray(inp["mod_w"].reshape(4, 8, 128, 6144).transpose(0, 2, 1, 3))
    W["mod_b"] = fm(inp["mod_b"], 48)
    W["norm_g"] = np.ascontiguousarray(np.stack([fm(inp["norm1_g"], 8), fm(inp["norm2_g"], 8)], axis=2))
    W["final_g"] = fm(inp["final_norm_g"], 8)
    up = inp["ffn_w_up"].reshape(4, 8, 128, 2, NF, 128)
    W["ffn_up"] = np.ascontiguousarray(up.transpose(0, 4, 2, 3, 1, 5))
    W["ffn_down"] = np.ascontiguousarray(inp["ffn_w_down"].reshape(4, NF, 128, 1024).transpose(0, 2, 1, 3))
    cw = inp["ffn_conv_w"].reshape(4, 9, 42, 128)
    W["ffn_cw"] = np.ascontiguousarray(cw.transpose(3, 0, 2, 1))
    W["ffn_cb"] = fm(inp["ffn_conv_b"], 42)
    W["cm_in"] = np.ascontiguousarray(inp["cm_w_in"].reshape(2, 8, 128, 4096).transpose(0, 2, 1, 3))
    W["cm_out"] = np.ascontiguousarray(inp["cm_w_out"].reshape(2, 16, 128, 1024).transpose(0, 2, 1, 3))
    W["cm_wsT"] = np.ascontiguousarray(inp["cm_w_s"].transpose(0, 3, 1, 2))
    W["cm_bin_u"] = fm(inp["cm_b_in"][:, :2048], 16)
    W["cm_vgb"] = np.ascontiguousarray(np.stack([fm(inp["cm_v_g"], 16), fm(inp["cm_v_b"], 16)], axis=2))
    rep = np.stack([inp["cm_b_in"][:, 2048:], inp["cm_v_g"], inp["cm_v_b"]], axis=1)
    W["cm_rep"] = np.ascontiguousarray(np.broadcast_to(rep[:, :, None, :], (2, 3, 128, 2048)))
    bs = np.tile(inp["cm_b_s"][:, None, :, :], (1, 128, 1, 1))
    W["cm_bs"] = np.ascontiguousarray(bs)
    W["lru_in"] = np.ascontiguousarray(inp["lru_w_in"][0].reshape(8, 128, 2048).transpose(1, 0, 2))
    W["lru_out"] = np.ascontiguousarray(inp["lru_w_out"][0].reshape(8, 128, 1024).transpose(1, 0, 2))
    gw = np.stack([inp["lru_w_rg"][0], inp["lru_w_ig"][0]], axis=1)
    gw = gw.reshape(2, 2, 4, 2, 128, 256)
    W["lru_gw"] = np.ascontiguousarray(gw.transpose(4, 0, 1, 2, 3, 5))
    vec = np.concatenate([fm(inp["lru_conv_w"][0], 8).transpose(0, 2, 1),
                          fm(inp["lru_conv_b"][0], 8)[:, :, None],
                          fm(inp["lru_b_rg"][0], 8).transpose(0, 2, 1),
                          fm(inp["lru_b_ig"][0], 8).transpose(0, 2, 1),
                          fm(inp["lru_lambda"][0], 8).transpose(0, 2, 1)], axis=2)
    W["lru_vec"] = np.ascontiguousarray(vec)
    W["ml_in"] = np.ascontiguousarray(inp["ml_w_in"][0].reshape(8, 128, 3088).transpose(1, 0, 2))
    W["ml_out"] = np.ascontiguousarray(inp["ml_w_out"][0].reshape(8, 128, 1024).transpose(1, 0, 2))
    W["ml_bg"] = np.ascontiguousarray(np.broadcast_to(inp["ml_b_gate"][0].reshape(1, 16), (128, 16)))
    W["ml_ng"] = np.ascontiguousarray(np.broadcast_to(inp["ml_norm_g"][0].reshape(1, 1024), (128, 1024)))
    r = np.arange(128)
    cst = np.stack([np.eye(128), (r[:, None] <= r[None, :]), (r[:, None] >= r[None, :]), np.ones((128, 128))], axis=1)
    W["ml_cst"] = np.ascontiguousarray(cst.astype(np.float32))
    return {k: np.ascontiguousarray(v, dtype=f32) for k, v in W.items()}


def layout_core(inp, i):
    b0 = NB * i
    seq = np.concatenate([inp["ctx"][b0:b0 + NB], inp["x"][b0:b0 + NB]], axis=1)
    xin = np.ascontiguousarray(seq.reshape(NB, T, 8, 128).transpose(0, 3, 2, 1))
    cond = np.zeros((4, D), np.float32)
    cond[0:NB] = inp["c"][b0:b0 + NB]
    cond[2] = inp["c_ctx"]
    condT = np.ascontiguousarray(cond.reshape(4, 8, 128).transpose(2, 1, 0))
    return {"xin": xin.astype(np.float32), "cond": condT}


_CACHE = {}


def kernel(**inputs):
    inp = {k: np.asarray(v) for k, v in inputs.items()}
    n_cores = 8
    shared = layout_shared(inp)
    if "nc" not in _CACHE:
        _CACHE["nc"] = Prog().build()
    nc = _CACHE["nc"]
    in_maps = []
    for i in range(n_cores):
        m = dict(shared)
        m.update(layout_core(inp, i))
        in_maps.append(m)
    res = run_bass_kernel_spmd(nc, in_maps, core_ids=list(range(n_cores)))
    outs = []
    for i in range(n_cores):
        oT = res.results[i]["outT"]
        outs.append(np.ascontiguousarray(oT.transpose(0, 3, 2, 1)).reshape(NB, LT, D))
    return np.concatenate(outs, axis=0).astype(np.float32)
```
